# Optimizing a Trainium2 kernel written in Bass

```python
import math
import jax, jax.numpy as jnp
from jax import lax
import numpy as np

D_MODEL = 1024
BATCH = 8
SEQ = 2048
DEPTH = 1

PLE_DIM = 256
D_MIX = 2 * D_MODEL
LRU_WIDTH = D_MIX // 2
LRU_HEADS = 16
LRU_HEAD_DIM = LRU_WIDTH // LRU_HEADS
LRU_C = 8.0
SSD_WIDTH = D_MIX - LRU_WIDTH
SSD_HEAD_DIM = 64
SSD_HEADS = SSD_WIDTH // SSD_HEAD_DIM
SSD_GROUPS = 4
SSD_STATE = 128
SSD_CHUNK = 128
SSD_XBC = SSD_WIDTH + 2 * SSD_GROUPS * SSD_STATE
CONV_WIDTH = 4
D_FF = 4 * D_MODEL
ALPHA = (2.0 * DEPTH) ** 0.25
BETA = (8.0 * DEPTH) ** -0.25
LN_EPS = 1e-5
RMS_EPS = 1e-5
IN_SPLIT_POINTS = (LRU_WIDTH, 2 * LRU_WIDTH, 2 * LRU_WIDTH + SSD_WIDTH, 2 * LRU_WIDTH + SSD_WIDTH + SSD_XBC)
D_IN_PROJ = 2 * LRU_WIDTH + SSD_WIDTH + SSD_XBC + SSD_HEADS

kernel_name = "hymba_style_rglru_ssd_deepnorm_block"


def layer_norm(x, g, b):
    xf = x.astype(jnp.float32)
    mu = jnp.mean(xf, axis=-1, keepdims=True)
    xc = xf - mu
    var = jnp.mean(xc * xc, axis=-1, keepdims=True)
    y = xc * lax.rsqrt(var + LN_EPS) * g.astype(jnp.float32) + b.astype(jnp.float32)
    return y.astype(x.dtype)


def causal_depthwise_conv(x, w, b):
    c = x.shape[-1]
    y = lax.conv_general_dilated(x, w[:, None, :], window_strides=(1,), padding=[(CONV_WIDTH - 1, 0)],
                                 dimension_numbers=("NWC", "WIO", "NWC"), feature_group_count=c)
    return y + b


def rg_lru(x, w_a, b_a, w_x, b_x, a_param):
    bsz, s, _ = x.shape
    xf = x.astype(jnp.float32)
    xh = xf.reshape(bsz, s, LRU_HEADS, LRU_HEAD_DIM)
    r = jax.nn.sigmoid(jnp.einsum("bshi,hij->bshj", xh, w_a.astype(jnp.float32)) + b_a.astype(jnp.float32))
    i = jax.nn.sigmoid(jnp.einsum("bshi,hij->bshj", xh, w_x.astype(jnp.float32)) + b_x.astype(jnp.float32))
    r = r.reshape(bsz, s, LRU_WIDTH)
    i = i.reshape(bsz, s, LRU_WIDTH)
    log_a = -LRU_C * r * jax.nn.softplus(-a_param.astype(jnp.float32))
    a = jnp.exp(log_a)
    mult = jnp.sqrt(-jnp.expm1(2.0 * log_a))
    first = (jnp.arange(s) == 0)[None, :, None]
    mult = jnp.where(first, 1.0, mult)
    u = xf * i * mult

    def combine(left, right):
        a_l, u_l = left
        a_r, u_r = right
        return a_l * a_r, a_r * u_l + u_r

    _, h = lax.associative_scan(combine, (a, u), axis=1)
    return h


def ssd_chunked(x, dt, A, Bm, Cm):
    bsz, s, nh, hp = x.shape
    g, n = Bm.shape[2], Bm.shape[3]
    j = nh // g
    c = s // SSD_CHUNK
    l = SSD_CHUNK
    xdt = (x * dt[..., None]).reshape(bsz, c, l, g, j, hp)
    a = (dt * A).reshape(bsz, c, l, g, j).transpose(0, 3, 4, 1, 2)
    Bc = Bm.reshape(bsz, c, l, g, n)
    Cc = Cm.reshape(bsz, c, l, g, n)
    a_cs = jnp.cumsum(a, axis=-1)
    causal = jnp.tril(jnp.ones((l, l), dtype=bool))
    seg = a_cs[..., :, None] - a_cs[..., None, :]
    decay_mat = jnp.exp(jnp.where(causal, seg, -jnp.inf))
    cb = jnp.einsum("bclgn,bcsgn->bgcls", Cc, Bc)
    scores = cb[:, :, None] * decay_mat
    y_diag = jnp.einsum("bgjcls,bcsgjp->bclgjp", scores, xdt)
    decay_states = jnp.exp(a_cs[..., -1:] - a_cs)
    states = jnp.einsum("bcsgn,bgjcs,bcsgjp->bcgjpn", Bc, decay_states, xdt)
    chunk_decay = jnp.exp(a_cs[..., -1])

    def step(carry, inp):
        st, dec = inp
        new = dec[..., None, None] * carry + st
        return new, carry

    init = jnp.zeros((bsz, g, j, hp, n), jnp.float32)
    _, prev = lax.scan(step, init, (states.transpose(1, 0, 2, 3, 4, 5), chunk_decay.transpose(3, 0, 1, 2)))
    prev = prev.transpose(1, 0, 2, 3, 4, 5)
    y_off = jnp.einsum("bclgn,bcgjpn,bgjcl->bclgjp", Cc, prev, jnp.exp(a_cs))
    return (y_diag + y_off).reshape(bsz, s, nh, hp)


def gated_rmsnorm(y, z, w):
    bsz, s, _ = y.shape
    yf = y * jax.nn.silu(z.astype(jnp.float32))
    yg = yf.reshape(bsz, s, SSD_GROUPS, -1)
    yg = yg * lax.rsqrt(jnp.mean(yg * yg, axis=-1, keepdims=True) + RMS_EPS)
    return yg.reshape(bsz, s, -1) * w.astype(jnp.float32)


def setup_inputs(seed: int = 0) -> dict:
    key = jax.random.key(seed)
    ks = jax.random.split(key, 32)
    f32 = jnp.float32
    L = DEPTH
    nrm = lambda k, shape, scale: jax.random.normal(k, shape, f32) * scale
    x = jax.random.normal(ks[0], (BATCH, SEQ, D_MODEL), f32)
    p = jax.random.normal(ks[1], (DEPTH, BATCH, SEQ, PLE_DIM), f32)
    w_in = nrm(ks[2], (L, D_MODEL, D_IN_PROJ), D_MODEL ** -0.5)
    lru_conv_w = nrm(ks[3], (L, CONV_WIDTH, LRU_WIDTH), CONV_WIDTH ** -0.5)
    lru_conv_b = nrm(ks[4], (L, LRU_WIDTH), 0.02)
    lru_gate_a_w = nrm(ks[5], (L, LRU_HEADS, LRU_HEAD_DIM, LRU_HEAD_DIM), LRU_HEAD_DIM ** -0.5)
    lru_gate_a_b = nrm(ks[6], (L, LRU_HEADS, LRU_HEAD_DIM), 0.02)
    lru_gate_x_w = nrm(ks[7], (L, LRU_HEADS, LRU_HEAD_DIM, LRU_HEAD_DIM), LRU_HEAD_DIM ** -0.5)
    lru_gate_x_b = nrm(ks[8], (L, LRU_HEADS, LRU_HEAD_DIM), 0.02)
    a_pow = jax.random.uniform(ks[9], (L, LRU_WIDTH), f32, 0.9, 0.999) ** (1.0 / LRU_C)
    lru_a_param = jnp.log(a_pow) - jnp.log1p(-a_pow)
    ssd_conv_w = nrm(ks[10], (L, CONV_WIDTH, SSD_XBC), CONV_WIDTH ** -0.5)
    ssd_conv_b = nrm(ks[11], (L, SSD_XBC), 0.02)
    dt0 = jnp.exp(jax.random.uniform(ks[12], (L, SSD_HEADS), f32, math.log(1e-3), math.log(1e-1)))
    ssd_dt_bias = dt0 + jnp.log(-jnp.expm1(-dt0))
    ssd_a_log = jnp.log(jax.random.uniform(ks[13], (L, SSD_HEADS), f32, 1.0, 16.0))
    ssd_d = 1.0 + nrm(ks[14], (L, SSD_HEADS), 0.02)
    ssd_norm_w = 1.0 + nrm(ks[15], (L, SSD_WIDTH), 0.02)
    w_out = nrm(ks[16], (L, D_MIX, D_MODEL), BETA * D_MIX ** -0.5)
    ln1_g = 1.0 + nrm(ks[17], (L, D_MODEL), 0.02)
    ln1_b = nrm(ks[18], (L, D_MODEL), 0.02)
    w_ff1 = nrm(ks[19], (L, D_MODEL, D_FF), D_MODEL ** -0.5)
    w_ff2 = nrm(ks[20], (L, D_FF, D_MODEL), BETA * D_FF ** -0.5)
    ln2_g = 1.0 + nrm(ks[21], (L, D_MODEL), 0.02)
    ln2_b = nrm(ks[22], (L, D_MODEL), 0.02)
    w_ple_gate = nrm(ks[23], (L, D_MODEL, D_MODEL), D_MODEL ** -0.5)
    w_ple = nrm(ks[24], (L, PLE_DIM, D_MODEL), BETA * PLE_DIM ** -0.5)
    ln3_g = 1.0 + nrm(ks[25], (L, D_MODEL), 0.02)
    ln3_b = nrm(ks[26], (L, D_MODEL), 0.02)
    return {"x": x, "p": p, "w_in": w_in, "lru_conv_w": lru_conv_w, "lru_conv_b": lru_conv_b,
            "lru_gate_a_w": lru_gate_a_w, "lru_gate_a_b": lru_gate_a_b, "lru_gate_x_w": lru_gate_x_w,
            "lru_gate_x_b": lru_gate_x_b, "lru_a_param": lru_a_param, "ssd_conv_w": ssd_conv_w,
            "ssd_conv_b": ssd_conv_b, "ssd_dt_bias": ssd_dt_bias, "ssd_a_log": ssd_a_log, "ssd_d": ssd_d,
            "ssd_norm_w": ssd_norm_w, "w_out": w_out, "ln1_g": ln1_g, "ln1_b": ln1_b, "w_ff1": w_ff1,
            "w_ff2": w_ff2, "ln2_g": ln2_g, "ln2_b": ln2_b, "w_ple_gate": w_ple_gate, "w_ple": w_ple,
            "ln3_g": ln3_g, "ln3_b": ln3_b}


def reference(x, p, w_in, lru_conv_w, lru_conv_b, lru_gate_a_w, lru_gate_a_b, lru_gate_x_w, lru_gate_x_b,
              lru_a_param, ssd_conv_w, ssd_conv_b, ssd_dt_bias, ssd_a_log, ssd_d, ssd_norm_w, w_out,
              ln1_g, ln1_b, w_ff1, w_ff2, ln2_g, ln2_b, w_ple_gate, w_ple, ln3_g, ln3_b):
    bsz, s, _ = x.shape
    for i in range(DEPTH):
        proj = jnp.einsum("bsd,de->bse", x, w_in[i])
        x_lru, g_lru, z, xbc, dt_raw = jnp.split(proj, IN_SPLIT_POINTS, axis=-1)
        xl = causal_depthwise_conv(x_lru, lru_conv_w[i], lru_conv_b[i])
        h = rg_lru(xl, lru_gate_a_w[i], lru_gate_a_b[i], lru_gate_x_w[i], lru_gate_x_b[i], lru_a_param[i])
        y_lru = (jax.nn.gelu(g_lru.astype(jnp.float32)) * h).astype(x.dtype)
        xbc = jax.nn.silu(causal_depthwise_conv(xbc, ssd_conv_w[i], ssd_conv_b[i]).astype(jnp.float32))
        xs, Bm, Cm = jnp.split(xbc, (SSD_WIDTH, SSD_WIDTH + SSD_GROUPS * SSD_STATE), axis=-1)
        xs = xs.reshape(bsz, s, SSD_HEADS, SSD_HEAD_DIM)
        Bm = Bm.reshape(bsz, s, SSD_GROUPS, SSD_STATE)
        Cm = Cm.reshape(bsz, s, SSD_GROUPS, SSD_STATE)
        dt = jax.nn.softplus(dt_raw.astype(jnp.float32) + ssd_dt_bias[i].astype(jnp.float32))
        A = -jnp.exp(ssd_a_log[i].astype(jnp.float32))
        y = ssd_chunked(xs, dt, A, Bm, Cm) + xs * ssd_d[i].astype(jnp.float32)[:, None]
        y_ssd = gated_rmsnorm(y.reshape(bsz, s, SSD_WIDTH), z, ssd_norm_w[i]).astype(x.dtype)
        mix = jnp.einsum("bse,ed->bsd", jnp.concatenate([y_lru, y_ssd], axis=-1), w_out[i])
        x = layer_norm(ALPHA * x + mix, ln1_g[i], ln1_b[i])
        hid = jnp.square(jax.nn.relu(jnp.einsum("bsd,df->bsf", x, w_ff1[i])))
        ff = jnp.einsum("bsf,fd->bsd", hid, w_ff2[i])
        x = layer_norm(ALPHA * x + ff, ln2_g[i], ln2_b[i])
        gate = jax.nn.sigmoid(jnp.einsum("bsd,de->bse", x, w_ple_gate[i]).astype(jnp.float32))
        ple = jnp.einsum("bsk,kd->bsd", p[i], w_ple[i]).astype(jnp.float32)
        x = layer_norm(ALPHA * x + (gate * ple).astype(x.dtype), ln3_g[i], ln3_b[i])
    return x
```

```python
import numpy as np
from contextlib import ExitStack
import ml_dtypes
import concourse.bass as bass
import concourse.mybir as mybir
from concourse.bass_utils import run_bass_kernel_spmd

F32 = mybir.dt.float32
BF16 = mybir.dt.bfloat16
AF = mybir.ActivationFunctionType
ALU = mybir.AluOpType

S_LEN = 2048
D = 1024
TB = 512
NB = S_LEN // TB
CH = 128
ALPHA = 2.0 ** 0.25
LN_EPS = 1e-5
RMS_EPS = 1e-5
SB_BASE = 16512
SB_TOP = 229344
NEG = -30000.0
REORDER = True
PRIO_BLEV = True
PRIO_NOISE = 0.0
PRIO_SEED = 0
PYT_BANK = 6
LRU_PX = [0, 1]
LRU_PRI = [2]
LRU_PCV = [4]
LRU_PG = [5, 6, 7]
PHASE_MARKS = []
ACT_GRP_WINDOW = 3000
ACT_SWITCH_US = 0.0
ACT_CHAIN_LRU = True
_ACT_GRP = {AF.Exp: "explog", AF.Ln: "explog", AF.Sigmoid: "sig", AF.Tanh: "gelu", AF.Silu: "silu", AF.Gelu_apprx_tanh: "gelu"}

_DS = {F32: 4, BF16: 2}


class Buf:
    __slots__ = ("name", "t", "off", "size", "w", "r", "inh", "live", "inh_owner")

    def __init__(self, name, t, off, size):
        self.name, self.t, self.off, self.size = name, t, off, size
        self.w = {}
        self.r = {}
        self.inh = []
        self.inh_owner = None
        self.live = True

    def __getitem__(self, k):
        return self.t[k]


class Sched:
    ENG = ("pe", "act", "dve", "pool", "sp")
    LAT = 0.25

    def __init__(self, nc):
        self.nc = nc
        self.nodes = []
        self.sem = {}
        self.dma_sems = {}
        self.dma_last = {}
        self.bufs = []
        self.names = {}
        for e in ("pe", "act", "dve", "pool"):
            self.sem[e] = nc.alloc_semaphore("sem_" + e)

    def _uname(self, name):
        n = self.names.get(name, 0)
        self.names[name] = n + 1
        return name if n == 0 else "%s_%d" % (name, n)

    def sb(self, name, shape, dtype, off):
        size = int(np.prod(shape[1:])) * _DS[dtype]
        assert off % 32 == 0, (name, off)
        assert SB_BASE + off + size <= SB_TOP, (name, off, size)
        t = self.nc.alloc_sbuf_tensor_at(self._uname(name), list(shape), dtype, offset=SB_BASE + off)
        b = Buf(name, t, off, size)
        inh = set()
        for o in self.bufs:
            if o.off < off + size and off < o.off + o.size:
                o.live = False
                inh.update(o.inh)
                if o.inh_owner is not None:
                    inh.add(o.inh_owner)
                inh.update(o.w.values())
                for l in o.r.values():
                    inh.update(l)
        b.inh = sorted(inh)
        self.bufs.append(b)
        return b

    def ps(self, name, shape, dtype=F32):
        t = self.nc.alloc_psum_tensor(self._uname(name), list(shape), dtype)
        return Buf(name, t, -1, 0)

    def dma_sem(self, name):
        if name not in self.dma_sems:
            self.dma_sems[name] = self.nc.alloc_semaphore("d_" + name)
        return self.dma_sems[name]

    @staticmethod
    def _key(k):
        return k if isinstance(k, tuple) else (k, None)

    def op(self, eng, fn, reads=(), writes=(), dma=None, cost=1.0, grp=None):
        idx = len(self.nodes)
        preds = {}

        def add(p, raw):
            if p is None or p == idx:
                return
            if raw or p not in preds:
                preds[p] = preds.get(p, False) or raw

        def take_inh(b):
            if b.inh:
                for p in b.inh:
                    add(p, True)
                b.inh = []
                b.inh_owner = idx
            elif b.inh_owner is not None:
                add(b.inh_owner, True)

        for k in reads:
            b, sub = self._key(k)
            take_inh(b)
            if sub is None:
                for p in b.w.values():
                    add(p, True)
            else:
                add(b.w.get(sub), True)
                add(b.w.get(None), True)
        for k in writes:
            b, sub = self._key(k)
            take_inh(b)
            subs = set(b.w.keys()) | set(b.r.keys()) if sub is None else (sub, None)
            for kk in subs:
                add(b.w.get(kk), False)
                for p in b.r.get(kk, ()):
                    add(p, False)
        if dma is not None:
            self.dma_sem(dma)
            add(self.dma_last.get(dma), False)
            self.dma_last[dma] = idx
        self.nodes.append(dict(eng=eng, fn=fn, dma=dma, preds=preds, cost=cost, grp=grp, after=set()))
        for k in reads:
            b, sub = self._key(k)
            b.r.setdefault(sub, []).append(idx)
        for k in writes:
            b, sub = self._key(k)
            if sub is None:
                b.w = {None: idx}
                b.r = {}
            else:
                b.w[sub] = idx
                b.r[sub] = []
        return idx

    def order_after(self, first, second):
        if first is not None and second is not None and first != second:
            self.nodes[second]["after"].add(first)

    def schedule(self, reorder=True):
        import heapq
        n = len(self.nodes)
        succ = [[] for _ in range(n)]
        npred = [0] * n
        for i, nd in enumerate(self.nodes):
            allp = set(nd["preds"].keys()) | nd["after"]
            npred[i] = len(allp)
            for p in allp:
                succ[p].append(i)
        order = {e: [] for e in self.ENG}
        if not reorder:
            for i, nd in enumerate(self.nodes):
                order[nd["eng"]].append(i)
            return order
        finish = [0.0] * n
        blev = [0.0] * n
        for i in range(n - 1, -1, -1):
            m = 0.0
            for j in succ[i]:
                if blev[j] > m:
                    m = blev[j]
            blev[i] = m + self.nodes[i]["cost"]
        if PRIO_NOISE > 0.0:
            rs = np.random.RandomState(PRIO_SEED)
            nz = rs.standard_normal(n)
            blev = [b_ * (1.0 + PRIO_NOISE * z_) for b_, z_ in zip(blev, nz)]
        self.last_grp = None
        ready = {e: [] for e in self.ENG}
        avail = {e: [] for e in self.ENG}
        free = {e: 0.0 for e in self.ENG}
        for i in range(n):
            if npred[i] == 0:
                heapq.heappush(ready[self.nodes[i]["eng"]], (0.0, i))
        done = 0
        while done < n:
            best = None
            for e in self.ENG:
                while ready[e] and ready[e][0][0] <= free[e]:
                    j_ = heapq.heappop(ready[e])[1]
                    heapq.heappush(avail[e], (-blev[j_] if PRIO_BLEV else 0.0, j_))
                if avail[e]:
                    st, i = free[e], avail[e][0][1]
                    if e == "act" and self.nodes[i].get("grp") not in (None, self.last_grp):
                        cand = [j for _, j in sorted(avail[e]) if self.nodes[j].get("grp") in (None, self.last_grp)]
                        if cand:
                            i = cand[0]
                        else:
                            soon = [(dr_, j) for dr_, j in ready[e] if dr_ <= free[e] + ACT_SWITCH_US and self.nodes[j].get("grp") in (None, self.last_grp)]
                            if soon:
                                st, i = min(soon)
                elif ready[e]:
                    st, i = ready[e][0]
                else:
                    continue
                if best is None or (st, -blev[i], i) < (best[0], -blev[best[1]], best[1]):
                    best = (st, i, e)
            st, i, e = best
            if avail[e] and any(j == i for _, j in avail[e]):
                if avail[e][0][1] == i:
                    heapq.heappop(avail[e])
                else:
                    avail[e] = [(k_, j) for k_, j in avail[e] if j != i]
                    heapq.heapify(avail[e])
            elif ready[e][0][1] == i:
                heapq.heappop(ready[e])
            else:
                ready[e] = [(k_, j) for k_, j in ready[e] if j != i]
                heapq.heapify(ready[e])
            if e == "act" and self.nodes[i].get("grp") is not None:
                self.last_grp = self.nodes[i]["grp"]
            nd = self.nodes[i]
            issue = nd["cost"] if nd["dma"] is None else 0.15
            free[e] = st + issue
            finish[i] = st + nd["cost"]
            order[e].append(i)
            done += 1
            for j in succ[i]:
                npred[j] -= 1
                if npred[j] == 0:
                    ej = self.nodes[j]["eng"]
                    dr = 0.0
                    for p in self.nodes[j]["preds"]:
                        f = finish[p] + (self.LAT if (self.nodes[p]["eng"] != ej or self.nodes[p]["dma"]) else 0.0)
                        if f > dr:
                            dr = f
                    if self.nodes[j]["dma"] is not None and ej == "pool" and dr > 0.0:
                        dr += 4.0
                    for p in self.nodes[j]["after"]:
                        f = finish[p] - self.nodes[p]["cost"] + 0.15
                        if f > dr:
                            dr = f
                    heapq.heappush(ready[ej], (dr, j))
        self.model_time = max(finish) if finish else 0.0
        return order

    def finalize(self, reorder=True):
        order = self.schedule(reorder)
        pos = {}
        dma_cum = {}
        tok = [None] * len(self.nodes)
        for e in self.ENG:
            c = 0
            for i in order[e]:
                nd = self.nodes[i]
                if nd["dma"] is None:
                    c += 1
                    tok[i] = (self.sem[e], c)
                pos[i] = None
        for e in self.ENG:
            for i in order[e]:
                nd = self.nodes[i]
                if nd["dma"] is not None:
                    dma_cum[nd["dma"]] = dma_cum.get(nd["dma"], 0) + 16
                    tok[i] = (self.dma_sems[nd["dma"]], dma_cum[nd["dma"]])
        self.dma_total = dma_cum
        streams = {}
        for e in self.ENG:
            waited = {}
            lst = []
            for i in order[e]:
                nd = self.nodes[i]
                waits = []
                for p, raw in nd["preds"].items():
                    pn = self.nodes[p]
                    if pn["dma"] is None and pn["eng"] == e and e == "pe":
                        continue
                    sem, val = tok[p]
                    if waited.get(sem, 0) >= val:
                        continue
                    waited[sem] = val
                    waits.append((sem, val))
                inc = (tok[i][0], 16 if nd["dma"] is not None else 1)
                lst.append((waits, nd["fn"], inc))
            streams[e] = lst
        self.streams = streams
        return streams

    def emit(self, eng, e, final=()):
        for waits, fn, inc in self.streams[eng]:
            for sem, val in waits:
                e.wait_ge(sem, val)
            fn(e).then_inc(inc[0], inc[1])
        for name in final:
            e.wait_ge(self.dma_sems[name], self.dma_total[name])


def build_program(dbg=()):
    dbg = set(dbg)
    nc = bass.Bass("TRN2", target_bir_lowering=False)
    S = Sched(nc)

    def din(name, shape, dt=F32):
        return nc.dram_tensor(name, list(shape), dt, kind="ExternalInput").ap()

    xT_d = din("xT", [D, S_LEN])
    xtok_d = din("xtok", [S_LEN, D])
    pT_d = din("pT", [256, S_LEN])
    w_z_d = din("w_z", [D, 1024])
    w_xbc_d = din("w_xbc", [D, 2048])
    w_dt_d = din("w_dt", [D, 80])
    w_lru_d = din("w_lru", [D, 2048])
    gate_bd_d = din("gate_bd", [16, 128, 128])
    lru_cdiag_d = din("lru_cdiag", [128, 32 * 128])
    w_out_d = din("w_out", [2048, D])
    w_ff1_d = din("w_ff1", [D, 4096])
    w_ff2_d = din("w_ff2", [4096, D])
    w_gate_d = din("w_gate", [D, D])
    w_ple_d = din("w_ple", [256, D])
    colpar_d = din("colpar", [128, 176])
    rowpar_d = din("rowpar", [8, 128, D])
    ident_bf_d = din("ident_bf", [128, 128], BF16)
    ident_f_d = din("ident_f", [128, 128])
    negmask_d = din("negmask", [128, 512], BF16)
    lconst_d = din("lconst", [48, 16 * 128])
    rconst_d = din("rconst", [48, 512])
    nege_d = din("nege", [16, 16])
    out_d = nc.dram_tensor("out", [D, S_LEN], F32, kind="ExternalOutput").ap()
    dbg_out = {}

    def dbg_tensor(name, shape):
        dbg_out[name] = nc.dram_tensor("dbg_" + name, list(shape), F32, kind="ExternalOutput").ap()
        return dbg_out[name]

    KB = 1024
    ident_bf = S.sb("ident_bf", [128, 128], BF16, 0)
    ident_f = S.sb("ident_f", [128, 128], F32, 256)
    negmask = S.sb("negmask", [128, 512], BF16, 768)
    colpar = S.sb("colpar", [128, 176], F32, 1792)
    derived = S.sb("derived", [128, 32], F32, 2496)
    nege = S.sb("nege", [16, 16], F32, 2624)
    pose = S.sb("pose", [16, 16], F32, 2688)
    ones_c = S.sb("ones_c", [128, 128], F32, 2752)
    mhalf = S.sb("mhalf", [128, 8], F32, 3264)
    derived2 = S.sb("derived2", [128, 32], F32, 3328)
    P0 = 4 * KB

    CW_L, CB_L, GAB, GXB, APAR, CW_S, CB_S, DTB, ALOG = 0, 32, 40, 48, 56, 64, 128, 144, 145

    def col(i):
        return colpar[:, i:i + 1]

    PS = [S.ps("psum%d" % i, [128, 1024], F32) for i in range(4)]

    def bank(i):
        return PS[i // 2], (i % 2) * 512

    def fsz(ap):
        n = 1
        for d in ap.shape[1:]:
            n *= int(d)
        return n

    def dma(eng, out_ap, in_ap, reads, writes, sem):
        nbytes = fsz(in_ap) * int(in_ap.shape[0]) * _DS.get(in_ap.dtype, 4)
        return S.op(eng, lambda e: e.dma_start(out=out_ap, in_=in_ap), reads=reads, writes=writes, dma=sem,
                    cost=2.0 + nbytes / 150e3)

    def mm(out_ap, lhsT, rhs, start, stop, reads, writes):
        n = fsz(rhs)
        c = (0.06 + n / 2600.0) if rhs.dtype != F32 else (0.06 + n * 4 / 2600.0)
        return S.op("pe", lambda e: e.matmul(out_ap, lhsT=lhsT, rhs=rhs, start=start, stop=stop),
                    reads=reads, writes=writes, cost=max(c, 0.1))

    def tr(out_ap, in_ap, ident_ap, reads, writes):
        return S.op("pe", lambda e: e.transpose(out_ap, in_ap, ident_ap), reads=reads, writes=writes, cost=0.12)

    act_chain = [None, False]

    def act(out_ap, in_ap, func, reads, writes, bias=None, scale=1.0, accum=None):
        def f(e):
            kw = {}
            if bias is not None:
                kw["bias"] = bias
            if accum is not None:
                kw["accum_out"] = accum
            return e.activation(out=out_ap, in_=in_ap, func=func, scale=scale, **kw)
        idx = S.op("act", f, reads=reads, writes=writes, cost=0.2 + fsz(in_ap) / 1150.0, grp=_ACT_GRP.get(func))
        if act_chain[1] and _ACT_GRP.get(func) is not None:
            S.order_after(act_chain[0], idx)
            act_chain[0] = idx
        return idx

    def vcost(eng, ap):
        n = fsz(ap)
        return (0.08 + n / 900.0) if eng == "dve" else (0.12 + n / 460.0)

    def tt(eng, out_ap, in0, in1, op, reads, writes):
        return S.op(eng, lambda e: e.tensor_tensor(out=out_ap, in0=in0, in1=in1, op=op), reads=reads, writes=writes,
                    cost=vcost(eng, out_ap))

    def ts(eng, out_ap, in0, s1, s2, op0, op1, reads, writes):
        if s2 is None:
            return S.op(eng, lambda e: e.tensor_scalar(out=out_ap, in0=in0, scalar1=s1, scalar2=None, op0=op0),
                        reads=reads, writes=writes, cost=vcost(eng, out_ap))
        return S.op(eng, lambda e: e.tensor_scalar(out=out_ap, in0=in0, scalar1=s1, scalar2=s2, op0=op0, op1=op1),
                    reads=reads, writes=writes, cost=vcost(eng, out_ap))

    def stt(out_ap, in0, scalar, in1, op0, op1, reads, writes):
        return S.op("dve", lambda e: e.scalar_tensor_tensor(out=out_ap, in0=in0, scalar=scalar, in1=in1, op0=op0, op1=op1),
                    reads=reads, writes=writes, cost=0.2 + fsz(out_ap) / 900.0)

    def cp(eng, out_ap, in_ap, reads, writes):
        if eng == "act":
            return act(out_ap, in_ap, AF.Copy, reads, writes)
        return S.op(eng, lambda e: e.tensor_copy(out=out_ap, in_=in_ap), reads=reads, writes=writes, cost=vcost(eng, out_ap))

    def memset(eng, ap, val, writes):
        return S.op(eng, lambda e: e.memset(ap, val), writes=writes, cost=vcost(eng, ap))

    dma_chain = [None]

    def chain(idx):
        S.order_after(dma_chain[0], idx)
        dma_chain[0] = idx
        return idx

    def wload(dst, src_d, ktiles, ncols, key_fn, sem_fn, c0=0, dst_c0=0, piece=1024):
        v = src_d.rearrange("(k p) n -> p k n", p=128)
        for a in range(0, ncols, piece):
            n = min(piece, ncols - a)
            chain(dma("pool", dst[:, 0:ktiles, dst_c0 + a:dst_c0 + a + n], v[:, 0:ktiles, c0 + a:c0 + a + n],
                      [], [key_fn(a)], sem_fn(a)))

    dma("sp", ident_bf[:], ident_bf_d, [], [ident_bf], "c_idb")
    dma("sp", ident_f[:], ident_f_d, [], [ident_f], "c_idf")
    dma("sp", negmask[:], negmask_d, [], [negmask], "c_neg")
    dma("sp", colpar[:], colpar_d, [], [colpar], "c_col")
    dma("sp", nege[:], nege_d, [], [nege], "c_nege")
    memset("pool", ones_c[:], 1.0, [ones_c])
    memset("pool", mhalf[:], -0.5, [mhalf])
    ts("pool", pose[:], nege[:], -1.0, None, ALU.mult, None, [nege], [pose])
    act(derived[:, 0:8], colpar[:, APAR:APAR + 8], AF.Exp, [colpar], [(derived, "sc")], scale=-1.0)
    act(derived[:, 0:8], derived[:, 0:8], AF.Ln, [(derived, "sc")], [(derived, "sc")], bias=1.0)
    ts("dve", derived[:, 8:16], derived[:, 0:8], 8.0, None, ALU.mult, None, [(derived, "sc")], [(derived, "nsc")])
    ts("dve", derived[:, 0:8], derived[:, 0:8], -8.0, None, ALU.mult, None, [(derived, "sc"), (derived, "nsc")], [(derived, "sc")])
    ts("dve", derived2[:, 0:8], derived[:, 0:8], 0.5, None, ALU.mult, None, [(derived, "sc")], [(derived2, 0)])
    ts("dve", derived2[:, 8:16], derived[:, 8:16], 0.5, None, ALU.mult, None, [(derived, "nsc")], [(derived2, 1)])
    ts("dve", derived2[:, 16:24], colpar[:, GAB:GAB + 8], 0.5, None, ALU.mult, None, [colpar], [(derived2, 2)])
    ts("dve", derived2[:, 24:32], colpar[:, GXB:GXB + 8], 0.5, None, ALU.mult, None, [colpar], [(derived2, 3)])
    act(derived[0:80, 16:17], colpar[0:80, ALOG:ALOG + 1], AF.Exp, [colpar], [(derived, "A")])
    ts("dve", derived[0:80, 16:17], derived[0:80, 16:17], -1.0, None, ALU.mult, None, [(derived, "A")], [(derived, "A")])

    R1 = P0
    Wz = S.sb("Wz", [128, 8, 1024], BF16, R1)
    Wxbc = S.sb("Wxbc", [128, 8, 2048], BF16, R1 + 16 * KB)
    Wdt = S.sb("Wdt", [128, 8, 80], BF16, R1 + 48 * KB)
    YS0 = 86 * KB
    y_ssdT = S.sb("y_ssdT", [128, 8, S_LEN], BF16, YS0)
    o = 54 * KB
    xTs = [S.sb("xTs%d" % i, [128, 8, TB], BF16, o + i * 8 * KB) for i in range(2)]
    o += 16 * KB
    szb = S.sb("sz", [128, 4, 1024], BF16, o)
    o += 8 * KB
    xs_sb = [S.sb("xs_sb%d" % i, [128, 1024], BF16, o + i * 2 * KB) for i in range(2)]
    o += 4 * KB
    xdt = [S.sb("xdt%d" % i, [128, 1024], BF16, o + i * 2 * KB) for i in range(2)]
    o += 4 * KB
    assert o <= 86 * KB
    o = 118 * KB
    xpad = [S.sb("xpad%d" % i, [128, TB + 32], F32, o + i * (TB + 32) * 4) for i in range(2)]
    o += 2 * (TB + 32) * 4
    cacc = [S.sb("cacc%d" % i, [128, TB], F32, o + i * 2 * KB) for i in range(2)]
    o += 4 * KB
    stail = S.sb("stail", [128, 16, 4], F32, o)
    o += 256
    xbcT = S.sb("xbcT", [128, 16, TB], BF16, o)
    o += 16 * KB
    dt_e = S.sb("dt_e", [80, TB], F32, o); o += 2 * KB
    dtT = S.sb("dtT", [80, TB], F32, o); o += 2 * KB
    aT = S.sb("aT", [80, TB], F32, o); o += 2 * KB
    acs = S.sb("acs", [80, TB], F32, o); o += 2 * KB
    smallT = S.sb("smallT", [80, TB], F32, o); o += 2 * KB
    ddtmp = dt_e
    ea0 = S.sb("ea0", [16, TB], F32, o); o += 2 * KB
    Rb = S.sb("Rb", [48, TB], F32, o); o += 2 * KB
    Lt = [S.sb("Lt0", [48, 16, 128], F32, o)]
    o += 8 * KB
    tokm = [S.sb("tokm%d" % i, [128, 80], F32, o + i * 320) for i in range(2)]
    o += 640
    diagcd = S.sb("diagcd", [16, 16], F32, o); o += 64
    cdrow = [S.sb("cdrow%d" % i, [128, 16], F32, o + i * 64) for i in range(2)]
    o += 128
    ssq = S.sb("ssq", [128, 8], F32, o); o += 32
    o = (o + 31) // 32 * 32
    xdtd = [S.sb("xdtd%d" % i, [128, 1024], BF16, o + i * 2 * KB) for i in range(2)]
    o += 4 * KB
    Btok = [S.sb("Btok%d" % i, [128, 512], BF16, o + i * KB) for i in range(2)]
    o += 2 * KB
    E4 = [S.sb("E4_%d" % i, [128, 512], F32, o + i * 2 * KB) for i in range(2)]
    o += 4 * KB
    scoresT = S.sb("scoresT", [128, 4, 512], BF16, o); o += 4 * KB
    Sst = S.sb("Sst", [128, 1024], F32, o); o += 4 * KB
    S_bf = [S.sb("S_bf%d" % i, [128, 1024], BF16, o + i * 2 * KB) for i in range(2)]
    o += 4 * KB
    t1 = S.sb("t1", [128, 1024], F32, o); o += 4 * KB
    t2 = S.sb("t2", [128, 1024], F32, o); o += 4 * KB
    ytok = S.sb("ytok", [128, 1024], BF16, o); o += 2 * KB
    Drow = S.sb("Drow", [128, 1024], F32, o); o += 4 * KB
    assert o <= 207 * KB, o

    xT_v = xT_d.rearrange("(k p) t -> p k t", p=128)

    def load_xTs(b):
        return dma("pool", xTs[b % 2][:], xT_v[:, :, b * TB:(b + 1) * TB], [], [xTs[b % 2]], "xTs%d" % (b % 2))

    chain(load_xTs(0))
    wload(Wdt, w_dt_d, 8, 80, lambda a: Wdt, lambda a: "w_dt")
    for a in range(0, 1024, 512):
        wload(Wz, w_z_d, 8, 512, lambda q, a=a: (Wz, a // 512), lambda q, a=a: "w_z%d" % (a // 512), c0=a, dst_c0=a)
    for a in range(0, 2048, 512):
        wload(Wxbc, w_xbc_d, 8, 512, lambda q, a=a: (Wxbc, a // 512), lambda q, a=a: "w_xbc%d" % (a // 512),
              c0=a, dst_c0=a)
    dma("sp", Drow[:], rowpar_d[1], [], [Drow], "c_drow")
    dma("sp", Lt[0][:].rearrange("p h s -> p (h s)"), lconst_d, [], [Lt[0]], "c_lt0")
    dma("sp", Rb[:], rconst_d, [], [Rb], "c_rb")
    memset("pool", stail[:], 0.0, [stail])
    memset("pool", Sst[:], 0.0, [Sst])
    memset("pool", S_bf[0][:], 0.0, [S_bf[0]])
    memset("pool", smallT[:], 0.0, [smallT])

    def bc3(ap2, n_inner):
        return ap2.unsqueeze(2).broadcast_to([ap2.shape[0], ap2.shape[1], n_inner])

    hd3 = lambda ap: ap.rearrange("p (h d) -> p h d", d=64)

    for b in range(NB):
        xs_b = xTs[b % 2]
        if b + 1 < NB:
            load_xTs(b + 1)
        pb, pc = bank(7)
        for k in range(8):
            mm(pb[0:80, pc:pc + TB], Wdt[:, k, :], xs_b[:, k, :], k == 0, k == 7, [Wdt, xs_b], [(pb, 1)])
        act(dt_e[:], pb[0:80, pc:pc + TB], AF.Exp, [(pb, 1), colpar], [dt_e], bias=colpar[0:80, DTB:DTB + 1])
        act(dtT[:], dt_e[:], AF.Ln, [dt_e], [dtT], bias=1.0)
        ts("dve", aT[:], dtT[:], derived[0:80, 16:17], None, ALU.mult, None, [dtT, (derived, "A")], [aT])
        for c in range(4):
            S.op("dve", lambda e, c=c: e.tensor_tensor_scan(out=acs[:, c * CH:(c + 1) * CH], data0=ones_c[0:80, 0:CH],
                                                            data1=aT[:, c * CH:(c + 1) * CH], initial=0.0,
                                                            op0=ALU.mult, op1=ALU.add),
                 reads=[ones_c, aT], writes=[(acs, c)])
        cp("pool", Rb[32:48, :], acs[32:48, :], [acs], [Rb])
        cp("pool", smallT[0:16, :], dtT[0:16, :], [dtT], [(smallT, 0)])
        for c in range(4):
            act(ddtmp[32:48, c * CH:(c + 1) * CH], acs[32:48, c * CH:(c + 1) * CH], AF.Exp, [acs, dtT], [(ddtmp, c)],
                bias=acs[32:48, c * CH + CH - 1:c * CH + CH], scale=-1.0)
        tt("pool", smallT[32:48, :], ddtmp[32:48, :], dtT[32:48, :], ALU.mult, [ddtmp, dtT], [(smallT, 1)])
        act(smallT[64:80, :], acs[64:80, :], AF.Exp, [acs], [(smallT, 2)])
        act(ea0[:], acs[0:16, :], AF.Exp, [acs], [ea0])
        for c in range(4):
            pz = PS[c % 2]
            for hf in range(2):
                for k in range(8):
                    mm(pz[:, hf * 512:(hf + 1) * 512], xs_b[:, k, c * CH:(c + 1) * CH], Wz[:, k, hf * 512:(hf + 1) * 512],
                       k == 0, k == 7, [xs_b, (Wz, hf)], [(pz, hf)])
            act(szb[:, c, :], pz[:], AF.Silu, [pz], [(szb, c)])
        for e_ in range(16):
            pb, pc = bank(4 + (e_ % 2))
            pkey = (pb, (4 + e_ % 2) % 2)
            for k in range(8):
                mm(pb[:, pc:pc + TB], Wxbc[:, k, e_ * 128:(e_ + 1) * 128], xs_b[:, k, :], k == 0, k == 7,
                   [xs_b, (Wxbc, e_ // 4)], [pkey])
            xp = xpad[e_ % 2]
            ca = cacc[e_ % 2]
            cw = CW_S + e_ * 4
            act(xp[:, 4:4 + TB], pb[:, pc:pc + TB], AF.Copy, [pkey], [(xp, "m")])
            act(ca[:], pb[:, pc:pc + TB], AF.Identity, [pkey, colpar], [ca], bias=col(CB_S + e_), scale=col(cw + 3))
            cp("pool", xp[:, 0:4], stail[:, e_, :], [(stail, e_)], [(xp, "t")])
            for kk in range(3):
                stt(ca[:], xp[:, 1 + kk:1 + kk + TB], col(cw + kk), ca[:], ALU.mult, ALU.add, [xp, ca, colpar], [ca])
            cp("pool", stail[:, e_, :], xp[:, TB:TB + 4], [xp], [(stail, e_)])
            act(xbcT[:, e_, :], ca[:], AF.Silu, [ca], [(xbcT, e_)])

        def stA(c):
            ci = b * 4 + c
            c0 = c * CH
            r = ci % 2
            tk = tokm[r]
            pb7, pc7 = bank(7)
            tr(pb7[:, pc7:pc7 + 80], smallT[0:80, c0:c0 + CH], ident_f[0:80, 0:80], [smallT, ident_f], [(pb7, 1)])
            ts("pool", diagcd[:], pose[:], ea0[:, c0 + CH - 1:c0 + CH], 1.0, ALU.mult, ALU.mult, [pose, ea0], [diagcd])
            mm(pb7[:, pc7 + 128:pc7 + 144], Rb[0:16, 0:128], diagcd[:], True, True, [Rb, diagcd], [(pb7, 1)])
            pB = pb7[:, pc7 + 256:pc7 + 512].bitcast(BF16)
            for g in range(4):
                tr(pB[:, g * 128:(g + 1) * 128], xbcT[:, 8 + g, c0:c0 + CH], ident_bf[:], [(xbcT, 8 + g), ident_bf], [(pb7, 1)])
            cp("act", tk[:], pb7[:, pc7:pc7 + 80], [(pb7, 1)], [tk])
            cp("act", cdrow[r][:], pb7[:, pc7 + 128:pc7 + 144], [(pb7, 1)], [cdrow[r]])
            cp("act", Btok[r][:], pB, [(pb7, 1)], [Btok[r]])
            pb6, pc6 = bank(6)
            pxs = pb6[:, pc6:pc6 + 512].bitcast(BF16)
            for e_ in range(8):
                tr(pxs[:, e_ * 128:(e_ + 1) * 128], xbcT[:, e_, c0:c0 + CH], ident_bf[:], [(xbcT, e_), ident_bf], [(pb6, 0)])
            cp("act", xs_sb[r][:], pxs, [(pb6, 0)], [xs_sb[r]])
            tt("dve", hd3(xdt[r][:]), hd3(xs_sb[r][:]), bc3(tk[:, 0:16], 64), ALU.mult, [xs_sb[r], tk], [xdt[r]])
            tt("dve", hd3(xdtd[r][:]), hd3(xs_sb[r][:]), bc3(tk[:, 32:48], 64), ALU.mult, [xs_sb[r], tk], [xdtd[r]])
            pb4, pc4 = bank(4)
            for g in range(4):
                mm(pb4[:, pc4 + g * 128:pc4 + (g + 1) * 128], xbcT[:, 8 + g, c0:c0 + CH], xbcT[:, 12 + g, c0:c0 + CH], True, True,
                   [(xbcT, 8 + g), (xbcT, 12 + g)], [(pb4, 0)])
            tt("pool", Lt[0][0:16, :, :], acs[0:16, c0:c0 + CH].unsqueeze(1).broadcast_to([16, 16, CH]),
               bc3(nege[:], CH), ALU.mult, [(acs, c), nege], [Lt[0]])

        def stB(c):
            c0 = c * CH
            lt = Lt[0]
            pb4, pc4 = bank(4)
            for g in range(4):
                pbs, pcs = bank(5) if g % 2 == 0 else bank(6)
                skey = (pbs, 1) if g % 2 == 0 else (pbs, 0)
                mm(pbs[:, pcs:pcs + 512], ident_bf[:], negmask[:], True, False, [ident_bf, negmask], [skey])
                for j in range(4):
                    h = 4 * g + j
                    mm(pbs[:, pcs + j * 128:pcs + (j + 1) * 128], lt[0:48, h, :], Rb[0:48, c0:c0 + CH], False, j == 3,
                       [lt, Rb], [skey])
                e4 = E4[g % 2]
                act(e4[:], pbs[:, pcs:pcs + 512], AF.Exp, [skey], [e4])
                tt("dve", scoresT[:, g, :].rearrange("p (j l) -> p j l", j=4), e4[:].rearrange("p (j l) -> p j l", j=4),
                   pb4[:, pc4 + g * 128:pc4 + (g + 1) * 128].unsqueeze(1).broadcast_to([128, 4, 128]), ALU.mult,
                   [e4, (pb4, 0)], [(scoresT, g)])

        def stC1(c):
            ci = b * 4 + c
            c0 = c * CH
            r = ci % 2
            tk = tokm[r]
            py = PS[0]
            pyo = PS[1]
            sprev = S_bf[ci % 2]
            snext = S_bf[(ci + 1) % 2]
            for g in range(4):
                mm(pyo[:, g * 256:(g + 1) * 256], xbcT[:, 12 + g, c0:c0 + CH], sprev[:, g * 256:(g + 1) * 256], True, True,
                   [(xbcT, 12 + g), sprev], [(pyo, g // 2)])
            for g in range(4):
                for j in range(4):
                    h = 4 * g + j
                    mm(py[:, h * 64:(h + 1) * 64], scoresT[:, g, j * 128:(j + 1) * 128], xdt[r][:, h * 64:(h + 1) * 64], True, True,
                       [(scoresT, g), xdt[r]], [(py, h // 8)])
            tt("dve", hd3(t1[:]), hd3(pyo[:]), bc3(tk[:, 64:80], 64), ALU.mult, [pyo, tk], [t1])
            pst = PS[1]
            for g in range(4):
                mm(pst[:, g * 256:(g + 1) * 256], Btok[r][:, g * 128:(g + 1) * 128], xdtd[r][:, g * 256:(g + 1) * 256], True, True,
                   [Btok[r], xdtd[r]], [(pst, g // 2)])
            tt("dve", hd3(Sst[:]), hd3(Sst[:]), bc3(cdrow[r][:], 64), ALU.mult, [Sst, cdrow[r]], [Sst])
            tt("dve", Sst[:], Sst[:], pst[:], ALU.add, [Sst, pst], [Sst])
            cp("act", snext[:], Sst[:], [Sst], [snext])
            tt("dve", t1[:], t1[:], py[:], ALU.add, [t1, py], [t1])
            tt("dve", t2[:], xs_sb[r][:], Drow[:], ALU.mult, [xs_sb[r], Drow], [t2])

        def stC2(c):
            ci = b * 4 + c
            tt("dve", t1[:], t1[:], t2[:], ALU.add, [t1, t2], [t1])
            tt("dve", t1[:], t1[:], szb[:, c, :], ALU.mult, [t1, (szb, c)], [t1])
            sq = ssq[:, (ci % 2) * 4:(ci % 2) * 4 + 4]
            for g in range(4):
                act(t2[:, g * 256:(g + 1) * 256], t1[:, g * 256:(g + 1) * 256], AF.Square, [t1], [t2], accum=sq[:, g:g + 1])
            ts("pool", sq, sq, 1.0 / 256.0, RMS_EPS, ALU.mult, ALU.add, [t2], [t2])
            tt("pool", sq, sq, mhalf[:, 0:4], ALU.pow, [t2, mhalf], [t2])
            for g in range(4):
                act(ytok[:, g * 256:(g + 1) * 256], t1[:, g * 256:(g + 1) * 256], AF.Identity, [t1, t2], [ytok], scale=sq[:, g:g + 1])
            pb6, pc6 = bank(PYT_BANK)
            pyT = pb6[:, pc6:pc6 + 512].bitcast(BF16)
            for e_ in range(8):
                tr(pyT[:, e_ * 128:(e_ + 1) * 128], ytok[:, e_ * 128:(e_ + 1) * 128], ident_bf[:], [ytok, ident_bf], [(pb6, PYT_BANK % 2)])
            cp("act", y_ssdT[:, :, ci * CH:(ci + 1) * CH], pyT.rearrange("p (e l) -> p e l", e=8), [(pb6, PYT_BANK % 2)], [(y_ssdT, ci)])

        stA(0); stB(0); stA(1)
        for c in range(4):
            stC1(c)
            if c + 1 < 4:
                stB(c + 1)
            if c + 2 < 4:
                stA(c + 2)
            stC2(c)

    final_tokens = []
    if "y_ssd" in dbg:
        dt_ = dbg_tensor("y_ssd", [D, S_LEN])
        stage = S.sb("dbgstage", [128, 8, S_LEN], F32, 118 * KB)
        cp("dve", stage[:], y_ssdT[:], [y_ssdT], [stage])
        dma("sp", dt_.rearrange("(e p) t -> p e t", p=128), stage[:], [stage], [], "dbg"); final_tokens.append("dbg")

    PHASE_MARKS.append(("lru", len(S.nodes)))
    if "stop_ssd" not in dbg:
        Wlx = S.sb("w_lru_x", [128, 8, 1024], BF16, 4 * KB)
        Wlg = S.sb("w_lru_g", [128, 8, 1024], BF16, 20 * KB)
        y_lruT = S.sb("y_lruT", [128, 8, S_LEN], BF16, 118 * KB)
        xl_f = S.sb("xl_f", [128, 8, TB], F32, 54 * KB)
        ra = S.sb("ra", [128, 8, TB], F32, 70 * KB)
        iu = S.sb("iu", [128, 8, TB], F32, 36 * KB)
        o = 150 * KB
        Tm = S.sb("Tm", [128, 8, TB], F32, o); o += 16 * KB
        xl_b = S.sb("xl_b", [128, 8, TB], BF16, o); o += 8 * KB
        xTl = [S.sb("xTl%d" % i, [128, 8, TB], BF16, o + i * 8 * KB) for i in range(2)]
        o += 16 * KB
        xpl = [S.sb("xpl%d" % i, [128, TB + 32], BF16, o + i * (TB + 32) * 2) for i in range(2)]
        o += 2 * (TB + 32) * 2
        Wga = S.sb("Wga", [128, 16, 128], BF16, o); o += 4 * KB
        cdiag = S.sb("cdiag", [128, 32, 128], BF16, o); o += 8 * KB
        ltail = S.sb("ltail", [128, 8, 4], BF16, o); o += 64
        hcarry = S.sb("hcarry", [128, 8], F32, o); o += 32
        assert o <= 207 * KB, o

        def load_xTl(b):
            dma("pool", xTl[b % 2][:], xT_v[:, :, b * TB:(b + 1) * TB], [], [xTl[b % 2]], "xTl%d" % (b % 2))

        load_xTl(0)
        for a in range(2):
            wload(Wlx if a == 0 else Wlg, w_lru_d, 8, 1024, lambda q, a=a: (Wlx if a == 0 else Wlg), lambda q, a=a: "w_lru%d" % a, c0=a * 1024, dst_c0=0)
        dma("pool", Wga[:], gate_bd_d.rearrange("g i j -> i g j"), [], [Wga], "w_ga")
        cdv = cdiag[:].rearrange("p g j -> p (g j)")
        for hh in range(2):
            dma("pool", cdv[:, hh * 2048:(hh + 1) * 2048], lru_cdiag_d[:, hh * 2048:(hh + 1) * 2048], [], [(cdiag, hh)], "w_cd%d" % hh)
        memset("pool", ltail[:], 0.0, [ltail])
        SC, NSC = 0, 8

        act_chain[1] = ACT_CHAIN_LRU
        for b in range(NB):
            xs_b = xTl[b % 2]
            if b + 1 < NB:
                load_xTl(b + 1)

            def st1a(c):
                bx_ = LRU_PX[c % len(LRU_PX)]
                pb, pc = bank(bx_)
                pk = (pb, bx_ % 2)
                for k in range(8):
                    mm(pb[:, pc:pc + TB], Wlx[:, k, c * 128:(c + 1) * 128], xs_b[:, k, :], k == 0, k == 7, [xs_b, Wlx], [pk])
                xp = xpl[c % 2]
                act(xp[:, 4:4 + TB], pb[:, pc:pc + TB], AF.Copy, [pk], [(xp, "m")])
                cp("pool", xp[:, 0:4], ltail[:, c, :], [(ltail, c)], [(xp, "t")])
                bcv_ = LRU_PCV[c % len(LRU_PCV)]
                pcb, pcc = bank(bcv_)
                pck = (pcb, bcv_ % 2)
                for kk in range(4):
                    mm(pcb[:, pcc:pcc + TB], cdiag[:, c * 4 + kk, :], xp[:, 1 + kk:1 + kk + TB], kk == 0, kk == 3, [(cdiag, (c * 4 + kk) // 16), xp], [pck])
                cp("pool", ltail[:, c, :], xp[:, TB:TB + 4], [xp], [(ltail, c)])
                act(xl_f[:, c, :], pcb[:, pcc:pcc + TB], AF.Identity, [pck, colpar], [(xl_f, c)], bias=col(CB_L + c))
                cp("dve", xl_b[:, c, :], xl_f[:, c, :], [(xl_f, c)], [(xl_b, c)])

            def st1b(c):
                for gi in range(2):
                    pb, pc = bank(LRU_PRI[c % len(LRU_PRI)] + gi)
                    pk = (pb, gi)
                    mm(pb[:, pc:pc + TB], Wga[:, gi * 8 + c, :], xl_b[:, c, :], True, True, [Wga, (xl_b, c)], [pk])

            def st2(c):
                pb, _ = bank(LRU_PRI[c % len(LRU_PRI)])
                act(ra[:, c, :], pb[:, 0:TB], AF.Tanh, [(pb, 0), (derived2, 2)], [(ra, c)], bias=derived2[:, 16 + c:17 + c], scale=0.5)
                act(iu[:, c, :], pb[:, 512:512 + TB], AF.Tanh, [(pb, 1), (derived2, 3)], [(iu, c)], bias=derived2[:, 24 + c:25 + c], scale=0.5)
                act(Tm[:, c, :], ra[:, c, :], AF.Tanh, [(ra, c), (derived2, 1)], [(Tm, c)], scale=derived2[:, 8 + c:9 + c],
                    bias=derived2[:, 8 + c:9 + c])

            for c in range(10):
                if c < 8:
                    st1a(c)
                if c >= 2:
                    st2(c - 2)
                if 1 <= c < 9:
                    st1b(c - 1)
            for c in range(8):
                act(ra[:, c, :], ra[:, c, :], AF.Exp, [(ra, c), (derived2, 0)], [(ra, c)], scale=derived2[:, c:c + 1], bias=derived2[:, c:c + 1])
                stt(iu[:, c, :], iu[:, c, :], 1.0, xl_f[:, c, :], ALU.add, ALU.mult, [(iu, c), (xl_f, c)], [(iu, c)])
            for c in range(8):
                tt("dve", xl_f[:, c, :], ra[:, c, :], ra[:, c, :], ALU.mult, [(ra, c)], [(xl_f, c)])
                stt(Tm[:, c, :], xl_f[:, c, :], 1.0, Tm[:, c, :], ALU.add, ALU.mult, [(xl_f, c), (Tm, c)], [(Tm, c)])
            for c in range(8):
                act(Tm[:, c, :], Tm[:, c, :], AF.Ln, [(Tm, c)], [(Tm, c)])
                act(Tm[:, c, :], Tm[:, c, :], AF.Exp, [(Tm, c)], [(Tm, c)], scale=0.5)
                if b == 0:
                    memset("pool", Tm[:, c, 0:1], 1.0, [(Tm, c)])
            for c in range(8):
                tt("dve", iu[:, c, :], iu[:, c, :], Tm[:, c, :], ALU.mult, [(iu, c), (Tm, c)], [(iu, c)])
                init = 0.0 if b == 0 else hcarry[:, c:c + 1]
                S.op("dve", lambda e, c=c, init=init: e.tensor_tensor_scan(out=xl_f[:, c, :], data0=ra[:, c, :], data1=iu[:, c, :],
                                                                           initial=init, op0=ALU.mult, op1=ALU.add),
                     reads=[(ra, c), (iu, c), (hcarry, c)], writes=[(xl_f, c)])
                cp("pool", hcarry[:, c:c + 1], xl_f[:, c, TB - 1:TB], [(xl_f, c)], [(hcarry, c)])
            for c in range(8):
                bg_ = LRU_PG[c % len(LRU_PG)]
                pb, pc = bank(bg_)
                pk = (pb, bg_ % 2)
                for k in range(8):
                    mm(pb[:, pc:pc + TB], Wlg[:, k, c * 128:(c + 1) * 128], xs_b[:, k, :], k == 0, k == 7, [xs_b, Wlg], [pk])
                act(ra[:, c, :], pb[:, pc:pc + TB], AF.Gelu_apprx_tanh, [pk], [(ra, c)])
                stt(y_lruT[:, c, b * TB:(b + 1) * TB], ra[:, c, :], 0.5, xl_f[:, c, :], ALU.mult, ALU.mult, [(ra, c), (xl_f, c)], [(y_lruT, b)])

        act_chain[1] = False
        if "cdiag" in dbg:
            dt_ = dbg_tensor("cdiag", [128, 4096])
            stg = S.sb("dbgstage3", [128, 4096], F32, 54 * KB)
            cp("act", stg[:], cdiag[:].rearrange("p g j -> p (g j)"), [cdiag], [stg])
            dma("sp", dt_, stg[:], [stg], [], "dbg"); final_tokens.append("dbg")
        if "y_lru" in dbg:
            dt_ = dbg_tensor("y_lru", [D, S_LEN])
            stage = S.sb("dbgstage2", [128, 8, S_LEN], F32, 4 * KB)
            cp("dve", stage[:], y_lruT[:], [y_lruT], [stage])
            dma("sp", dt_.rearrange("(e p) t -> p e t", p=128), stage[:], [stage], [], "dbg"); final_tokens.append("dbg")

    def ln_tile(v_ap, vkey, out_ap, okey, grow, brow, wk):
        stats, mv, xh = wk["stats"], wk["mv"], wk["xh"]
        for j in range(2):
            S.op("dve", lambda e, j=j: e.bn_stats(out=stats[:, j * 6:(j + 1) * 6], in_=v_ap[:, j * 512:(j + 1) * 512]),
                 reads=[vkey], writes=[(stats, j)])
        S.op("dve", lambda e: e.bn_aggr(out=mv[:, 0:2], in_=stats[:, 0:12]), reads=[stats], writes=[(mv, 0)])
        ts("pool", mv[:, 2:3], mv[:, 1:2], LN_EPS, 1.0, ALU.add, ALU.mult, [(mv, 0)], [(mv, 1)])
        tt("pool", mv[:, 2:3], mv[:, 2:3], mhalf[:, 0:1], ALU.pow, [(mv, 1), mhalf], [(mv, 1)])
        ts("dve", mv[:, 3:4], mv[:, 0:1], mv[:, 2:3], -1.0, ALU.mult, ALU.mult, [(mv, 0), (mv, 1)], [(mv, 2)])
        act(xh[:], v_ap, AF.Identity, [vkey, (mv, 1), (mv, 2)], [xh], bias=mv[:, 3:4], scale=mv[:, 2:3])
        tt("dve", xh[:], xh[:], grow[:], ALU.mult, [xh, grow], [xh])
        tt("dve", out_ap, xh[:], brow[:], ALU.add, [xh, brow], [okey])

    def ln_work(o):
        wk = {}
        wk["stats"] = S.sb("ln_stats", [128, 12], F32, o); o += 64
        wk["mv"] = S.sb("ln_mv", [128, 8], F32, o); o += 32
        wk["xh"] = S.sb("ln_xh", [128, D], F32, o); o += 4 * KB
        return wk, o

    PHASE_MARKS.append(("o", len(S.nodes)))
    if "stop_ssd" not in dbg and "stop_lru" not in dbg:
        Wo_a = S.sb("w_out_a", [128, 8, D], BF16, 4 * KB)
        Wo_b = S.sb("w_out_b", [128, 8, D], BF16, 166 * KB)

        def Wo_k(k):
            return (Wo_a, k) if k < 8 else (Wo_b, k - 8)
        x1h = [S.sb("x1a", [128, 8, D], F32, 54 * KB), S.sb("x1b", [128, 8, D], F32, 20 * KB)]
        o = 182 * KB
        xtk = [S.sb("xtk%d" % i, [128, D], F32, o + i * 4 * KB) for i in range(2)]
        o += 8 * KB
        g1 = S.sb("g1", [128, D], F32, o); o += 4 * KB
        b1 = S.sb("b1", [128, D], F32, o); o += 4 * KB
        vtmp = [S.sb("vtmp%d" % i, [128, D], F32, o + i * 4 * KB) for i in range(1)]
        o += 4 * KB
        wkO, o = ln_work(o)
        assert o <= 207 * KB, o
        castO = S.sb("castO", [128, D], BF16, 52 * KB)
        v = w_out_d.rearrange("(k p) n -> p k n", p=128)
        for q in range(4):
            wt_, k0_ = Wo_k(q * 4)
            dma("pool", wt_[:, k0_:k0_ + 4, :], v[:, q * 4:(q + 1) * 4, :], [], [(wt_, q % 2)], "w_out%d" % q)
        dma("sp", g1[:], rowpar_d[2], [], [g1], "c_g1")
        dma("sp", b1[:], rowpar_d[3], [], [b1], "c_b1")
        for j in range(8):
            ts("dve", Wo_b[:, j, :], Wo_b[:, j, :], col(146 + j), None, ALU.mult, None, [(Wo_b, j // 4), colpar], [(Wo_b, j // 4)])

        def x1_tile(tt_):
            return x1h[tt_ // 8][:, tt_ % 8, :], (x1h[tt_ // 8], tt_ % 8)

        def load_xtk(tt_):
            dma("sp", xtk[tt_ % 2][:], xtok_d[tt_ * 128:(tt_ + 1) * 128, :], [], [xtk[tt_ % 2]], "xtk%d" % (tt_ % 2))

        load_xtk(0)
        for tt_ in range(16):
            if tt_ + 1 < 16:
                load_xtk(tt_ + 1)
            pm = PS[tt_ % 3]
            for hf in range(2):
                for k in range(16):
                    src = y_lruT if k < 8 else y_ssdT
                    skey = (y_lruT, tt_ // 4) if k < 8 else (y_ssdT, tt_)
                    wt_, kk_ = Wo_k(k)
                    mm(pm[:, hf * 512:(hf + 1) * 512], src[:, k % 8, tt_ * 128:(tt_ + 1) * 128], wt_[:, kk_, hf * 512:(hf + 1) * 512],
                       k == 0, k == 15, [skey, (wt_, kk_ // 4)], [(pm, hf)])
            vt = vtmp[0]
            stt(vt[:], xtk[tt_ % 2][:], ALPHA, pm[:], ALU.mult, ALU.add, [xtk[tt_ % 2], pm], [vt])
            oap, okey = x1_tile(tt_)
            ln_tile(vt[:], vt, oap, okey, g1, b1, wkO)
            cp("act", castO[:], oap, [okey], [castO])
            pb, pc = bank(6 + tt_ % 2)
            pk = (pb, tt_ % 2)
            pT_ = pb[:, pc:pc + 512].bitcast(BF16)
            for e_ in range(8):
                tr(pT_[:, e_ * 128:(e_ + 1) * 128], castO[:, e_ * 128:(e_ + 1) * 128], ident_bf[:], [castO, ident_bf], [pk])
            cp("act", y_ssdT[:, :, tt_ * 128:(tt_ + 1) * 128], pT_.rearrange("p (e l) -> p e l", e=8), [pk], [(y_ssdT, tt_)])

        if "x1" in dbg:
            dt_ = dbg_tensor("x1", [S_LEN, D])
            for hh in range(2):
                dma("sp", dt_[hh * 1024:(hh + 1) * 1024, :].rearrange("(t p) d -> p t d", p=128), x1h[hh][:], [x1h[hh]], [], "dbg")
                final_tokens.append("dbg")

    def to_feature_major(xh2, dstT, castbuf, tiles, nbanks=4):
        for tt_ in tiles:
            cb = castbuf[tt_ % 2]
            cp("act" if tt_ % 2 == 0 else "dve", cb[:], xh2[tt_ // 8][:, tt_ % 8, :], [(xh2[tt_ // 8], tt_ % 8)], [cb])
            pb, pc = bank(tt_ % nbanks)
            pk = (pb, (tt_ % nbanks) % 2)
            pT_ = pb[:, pc:pc + 512].bitcast(BF16)
            for e_ in range(8):
                tr(pT_[:, e_ * 128:(e_ + 1) * 128], cb[:, e_ * 128:(e_ + 1) * 128], ident_bf[:], [cb, ident_bf], [pk])
            cp("dve" if tt_ % 2 == 0 else "act", dstT[:, :, tt_ * 128:(tt_ + 1) * 128], pT_.rearrange("p (e l) -> p e l", e=8), [pk], [(dstT, tt_)])

    PHASE_MARKS.append(("f", len(S.nodes)))
    if not ({"stop_ssd", "stop_lru", "stop_o"} & dbg):
        x1T = y_ssdT
        hidT = [S.sb("hidT%d" % i, [128, 4, S_LEN], BF16, 118 * KB + i * 16 * KB) for i in range(2)]
        Wc = []
        for i, base_ in ((2, 150 * KB), (0, 4 * KB), (1, 166 * KB)):
            Wc.append((S.sb("W1c%d" % i, [128, 8, 512], BF16, base_),
                       S.sb("W2c%d" % i, [128, 4, D], BF16, base_ + 8 * KB)))
        o = 182 * KB
        rtmp = [S.sb("rtmp%d" % i, [128, 512], BF16, o + i * KB) for i in range(2)]
        o += 2 * KB
        assert o <= 187 * KB, o
        NFC = 8
        w1v = w_ff1_d.rearrange("(k p) n -> p k n", p=128)
        w2v = w_ff2_d.rearrange("(k p) n -> p k n", p=128)

        def load_wc(fc):
            w1, w2 = Wc[fc % 3]
            dma("pool", w1[:], w1v[:, :, fc * 512:(fc + 1) * 512], [], [w1], "w1c%d" % (fc % 3))
            dma("pool", w2[:], w2v[:, fc * 4:(fc + 1) * 4, :], [], [w2], "w2c%d" % (fc % 3))

        load_wc(0)
        load_wc(1)
        load_wc(2)

        def ff1(fc, tq):
            w1, _ = Wc[fc % 3]
            hd = hidT[fc % 2]
            for f in range(4):
                n = tq * 4 + f
                pb, pc = bank(n % 4)
                pk = (pb, (n % 4) % 2)
                for k in range(8):
                    mm(pb[:, pc:pc + 512], w1[:, k, f * 128:(f + 1) * 128], x1T[:, k, tq * 512:(tq + 1) * 512], k == 0, k == 7,
                       [w1] + [(x1T, tq * 4 + q) for q in range(4)], [pk])
                rt = rtmp[n % 2]
                act(rt[:], pb[:, pc:pc + 512], AF.Relu, [pk], [rt])
                tt("dve", hd[:, f, tq * 512:(tq + 1) * 512], rt[:], rt[:], ALU.mult, [rt], [(hd, tq)])

        def ff2(fc, tq):
            _, w2 = Wc[fc % 3]
            hd = hidT[fc % 2]
            for t4 in range(4):
                tt_ = tq * 4 + t4
                pa = PS[2 + (tt_ % 2)]
                for hf in range(2):
                    for f in range(4):
                        mm(pa[:, hf * 512:(hf + 1) * 512], hd[:, f, tt_ * 128:(tt_ + 1) * 128], w2[:, f, hf * 512:(hf + 1) * 512],
                           f == 0, f == 3, [(hd, tq), w2], [(pa, hf)])
                xa, xk = x1h[tt_ // 8][:, tt_ % 8, :], (x1h[tt_ // 8], tt_ % 8)
                if fc == 0:
                    stt(xa, xa, ALPHA, pa[:], ALU.mult, ALU.add, [xk, pa], [xk])
                else:
                    tt("dve", xa, xa, pa[:], ALU.add, [xk, pa], [xk])

        seq = []
        for fc in range(NFC):
            for tq in range(4):
                seq.append(("1", fc, tq))
        n_steps = len(seq)
        for i in range(n_steps + 2):
            if i < n_steps:
                _, fc, tq = seq[i]
                ff1(fc, tq)
            if i >= 2:
                _, fc, tq = seq[i - 2]
                ff2(fc, tq)
                if tq == 3 and fc + 3 < NFC:
                    load_wc(fc + 3)

        if "x2" in dbg:
            dt_ = dbg_tensor("x2", [S_LEN, D])
            for hh in range(2):
                dma("sp", dt_[hh * 1024:(hh + 1) * 1024, :].rearrange("(t p) d -> p t d", p=128), x1h[hh][:], [x1h[hh]], [], "dbg")
                final_tokens.append("dbg")

    def ln_group(tiles, grow, brow, gw, affine=True):
        stats, mv, xhs = gw["stats"], gw["mv"], gw["xh"]
        for j, tt_ in enumerate(tiles):
            xa, xk = x1h[tt_ // 8][:, tt_ % 8, :], (x1h[tt_ // 8], tt_ % 8)
            for q in range(2):
                S.op("dve", lambda e, j=j, q=q, xa=xa: e.bn_stats(out=stats[:, j, q * 6:(q + 1) * 6], in_=xa[:, q * 512:(q + 1) * 512]),
                     reads=[xk], writes=[(stats, (j, q))], cost=0.65)
            S.op("dve", lambda e, j=j: e.bn_aggr(out=mv[:, j, 0:2], in_=stats[:, j, :]), reads=[(stats, (j, 0)), (stats, (j, 1))],
                 writes=[(mv, (j, "a"))], cost=0.25)
            ts("pool", mv[:, j, 2:3], mv[:, j, 1:2], LN_EPS, 1.0, ALU.add, ALU.mult, [(mv, (j, "a"))], [(mv, (j, "b"))])
            tt("pool", mv[:, j, 2:3], mv[:, j, 2:3], mhalf[:, 0:1], ALU.pow, [(mv, (j, "b")), mhalf], [(mv, (j, "b"))])
            ts("dve", mv[:, j, 3:4], mv[:, j, 0:1], mv[:, j, 2:3], -1.0, ALU.mult, ALU.mult, [(mv, (j, "a")), (mv, (j, "b"))], [(mv, (j, "c"))])
            act(xhs[j][:], xa, AF.Identity, [xk, (mv, (j, "b")), (mv, (j, "c"))], [xhs[j]], bias=mv[:, j, 3:4], scale=mv[:, j, 2:3])
            if affine:
                tt("dve", xhs[j][:], xhs[j][:], grow[:], ALU.mult, [xhs[j], grow], [xhs[j]])
                tt("dve", xa, xhs[j][:], brow[:], ALU.add, [xhs[j], brow], [xk])

    def ln_group_work(o):
        gw = {}
        gw["stats"] = S.sb("lg_stats", [128, 4, 12], F32, o); o += 192
        gw["mv"] = S.sb("lg_mv", [128, 4, 8], F32, o); o += 128
        gw["xh"] = [S.sb("lg_xh%d" % j, [128, D], F32, o + j * 4 * KB) for j in range(4)]
        o += 16 * KB
        return gw, o

    PHASE_MARKS.append(("g", len(S.nodes)))
    if not ({"stop_ssd", "stop_lru", "stop_o", "stop_f"} & dbg):
        x2T = y_ssdT
        Wg = S.sb("Wg", [128, 8, D], BF16, 187 * KB)
        Wp = S.sb("Wp", [128, 2, D], BF16, 203 * KB)
        pTs = S.sb("pTs", [128, 2, S_LEN], BF16, 118 * KB)
        o = 126 * KB
        castg = [S.sb("castg%d" % i, [128, D], BF16, o + i * 2 * KB) for i in range(4)]
        o += 8 * KB
        gsb = [S.sb("gsb%d" % i, [128, D], F32, o + i * 4 * KB) for i in range(4)]
        o += 16 * KB
        assert o <= 150 * KB
        otb = [S.sb("otb%d" % i, [128, 512], F32, 12 * KB + i * 2 * KB) for i in range(4)]
        o = 4 * KB
        g2 = S.sb("g2", [128, D], F32, o); o += 4 * KB
        b2 = S.sb("b2", [128, D], F32, o); o += 4 * KB
        g3 = b3 = None
        gw2, o = ln_group_work(150 * KB)
        o = (o + 31) // 32 * 32
        gw3, o = ln_group_work(o)
        assert o <= 187 * KB, o
        dma("pool", Wg[:], w_gate_d.rearrange("(k p) n -> p k n", p=128), [], [Wg], "w_g")
        dma("pool", Wp[:], w_ple_d.rearrange("(k p) n -> p k n", p=128), [], [Wp], "w_p")
        for hh in range(2):
            dma("pool", pTs[:, :, hh * 1024:(hh + 1) * 1024], pT_d.rearrange("(k p) t -> p k t", p=128)[:, :, hh * 1024:(hh + 1) * 1024],
                [], [(pTs, hh)], "pTs%d" % hh)
        for r_, t_, n_ in ((4, g2, "c_g2"), (5, b2, "c_b2")):
            dma("sp", t_[:], rowpar_d[r_], [], [t_], n_)

        def stP(grp):
            tiles = list(range(4 * grp, 4 * grp + 4))
            ln_group(tiles, g2, b2, gw2)
            for j, tt_ in enumerate(tiles):
                cp("act", castg[j][:], x1h[tt_ // 8][:, tt_ % 8, :], [(x1h[tt_ // 8], tt_ % 8)], [castg[j]])
            for j, tt_ in enumerate(tiles):
                pb, pc = bank(tt_ % 2)
                pk = (pb, tt_ % 2)
                pT_ = pb[:, pc:pc + 512].bitcast(BF16)
                for e_ in range(8):
                    tr(pT_[:, e_ * 128:(e_ + 1) * 128], castg[j][:, e_ * 128:(e_ + 1) * 128], ident_bf[:], [castg[j], ident_bf], [pk])
                cp("act", x2T[:, :, tt_ * 128:(tt_ + 1) * 128], pT_.rearrange("p (e l) -> p e l", e=8), [pk], [(x2T, tt_)])

        def stQ(grp):
            tiles = list(range(4 * grp, 4 * grp + 4))
            for j, tt_ in enumerate(tiles):
                pg = PS[1 + tt_ % 2]
                pp = PS[3]
                for hf in range(2):
                    for k in range(8):
                        mm(pg[:, hf * 512:(hf + 1) * 512], x2T[:, k, tt_ * 128:(tt_ + 1) * 128], Wg[:, k, hf * 512:(hf + 1) * 512],
                           k == 0, k == 7, [(x2T, tt_), Wg], [(pg, hf)])
                for hf in range(2):
                    for k in range(2):
                        mm(pp[:, hf * 512:(hf + 1) * 512], pTs[:, k, tt_ * 128:(tt_ + 1) * 128], Wp[:, k, hf * 512:(hf + 1) * 512],
                           k == 0, k == 1, [(pTs, tt_ // 8), Wp], [(pp, hf)])
                gs = gsb[j]
                act(gs[:], pg[:], AF.Sigmoid, [pg], [gs])
                tt("dve", gs[:], gs[:], pp[:], ALU.mult, [gs, pp], [gs])
            for j, tt_ in enumerate(tiles):
                xa, xk = x1h[tt_ // 8][:, tt_ % 8, :], (x1h[tt_ // 8], tt_ % 8)
                stt(xa, xa, ALPHA, gsb[j][:], ALU.mult, ALU.add, [xk, gsb[j]], [xk])
            ln_group(tiles, g3, b3, gw3, affine=False)
            for e_ in range(8):
                pb, pc = bank(e_ % 2)
                pk = (pb, e_ % 2)
                for j in range(4):
                    tr(pb[:, pc + j * 128:pc + (j + 1) * 128], gw3["xh"][j][:, e_ * 128:(e_ + 1) * 128], ident_f[:], [gw3["xh"][j], ident_f], [pk])
                ot = otb[(grp * 8 + e_) % 4]
                act(ot[:], pb[:, pc:pc + 512], AF.Identity, [pk, colpar], [ot], bias=col(162 + e_), scale=col(154 + e_))
                dma("sp", out_d[e_ * 128:(e_ + 1) * 128, grp * 512:(grp + 1) * 512], ot[:], [ot], [], "out")
                final_tokens.append("out")

        for i in range(5):
            if i < 4:
                stP(i)
            if i >= 1:
                stQ(i - 1)

    S.finalize(reorder=REORDER)
    finals = sorted(set(final_tokens))
    with nc.Block() as block:
        @block.sync
        def _(e):
            S.emit("sp", e, finals)

        @block.tensor
        def _(e):
            S.emit("pe", e)

        @block.scalar
        def _(e):
            S.emit("act", e)

        @block.vector
        def _(e):
            S.emit("dve", e)

        @block.gpsimd
        def _(e):
            S.emit("pool", e)
    return nc, dbg_out


def _host_inputs(inp):
    f = lambda k: np.asarray(inp[k], dtype=np.float32)
    w_in = f("w_in")[0]
    sh = {}
    sh["w_lru"] = np.ascontiguousarray(w_in[:, 0:2048])
    sh["w_z"] = np.ascontiguousarray(w_in[:, 2048:3072])
    sh["w_xbc"] = np.ascontiguousarray(w_in[:, 3072:5120])
    wdt = np.zeros((D, 80), np.float32)
    for r0 in (0, 32, 64):
        wdt[:, r0:r0 + 16] = w_in[:, 5120:5136]
    sh["w_dt"] = wdt
    gbd = np.zeros((16, 128, 128), np.float32)
    for gi, key in enumerate(("lru_gate_a_w", "lru_gate_x_w")):
        w = f(key)[0]
        for c in range(8):
            for j in range(2):
                gbd[gi * 8 + c, j * 64:(j + 1) * 64, j * 64:(j + 1) * 64] = w[2 * c + j]
    sh["gate_bd"] = gbd
    lcw_ = f("lru_conv_w")[0]
    cdg = np.zeros((128, 32, 128), np.float32)
    ii_ = np.arange(128)
    for c in range(8):
        for k in range(4):
            cdg[ii_, c * 4 + k, ii_] = lcw_[k, c * 128:(c + 1) * 128]
    sh["lru_cdiag"] = cdg.reshape(128, 32 * 128)
    sh["w_out"] = f("w_out")[0]
    sh["w_ff1"] = f("w_ff1")[0]
    sh["w_ff2"] = f("w_ff2")[0]
    sh["w_gate"] = f("w_ple_gate")[0]
    sh["w_ple"] = f("w_ple")[0]
    cpar = np.zeros((128, 176), np.float32)
    tile_cols = lambda v, n: v.reshape(n, 128).T
    lcw = f("lru_conv_w")[0]
    for k in range(4):
        cpar[:, 0 + np.arange(8) * 4 + k] = tile_cols(lcw[k], 8)
    cpar[:, 32:40] = tile_cols(f("lru_conv_b")[0], 8)
    cpar[:, 40:48] = tile_cols(f("lru_gate_a_b")[0].reshape(-1), 8)
    cpar[:, 48:56] = tile_cols(f("lru_gate_x_b")[0].reshape(-1), 8)
    cpar[:, 56:64] = tile_cols(f("lru_a_param")[0], 8)
    scw = f("ssd_conv_w")[0]
    for k in range(4):
        cpar[:, 64 + np.arange(16) * 4 + k] = tile_cols(scw[k], 16)
    cpar[:, 128:144] = tile_cols(f("ssd_conv_b")[0], 16)
    for r0 in (0, 32, 64):
        cpar[r0:r0 + 16, 144] = f("ssd_dt_bias")[0]
        cpar[r0:r0 + 16, 145] = f("ssd_a_log")[0]
    cpar[:, 146:154] = tile_cols(f("ssd_norm_w")[0], 8)
    cpar[:, 154:162] = tile_cols(f("ln3_g")[0], 8)
    cpar[:, 162:170] = tile_cols(f("ln3_b")[0], 8)
    sh["colpar"] = cpar
    rows = np.stack([f("ssd_norm_w")[0], np.repeat(f("ssd_d")[0], 64), f("ln1_g")[0], f("ln1_b")[0],
                     f("ln2_g")[0], f("ln2_b")[0], f("ln3_g")[0], f("ln3_b")[0]], 0)
    sh["rowpar"] = np.ascontiguousarray(np.broadcast_to(rows[:, None, :], (8, 128, D)))
    sh["ident_bf"] = np.eye(128, dtype=np.float32).astype(ml_dtypes.bfloat16)
    sh["ident_f"] = np.eye(128, dtype=np.float32)
    s_i = np.arange(128)[:, None]
    l_i = np.arange(128)[None, :]
    neg = np.where(l_i >= s_i, 0.0, NEG).astype(np.float32)
    sh["negmask"] = np.tile(neg, (1, 4)).astype(ml_dtypes.bfloat16)
    lconst = np.zeros((48, 16, 128), np.float32)
    for h in range(16):
        lconst[32 + h, h, :] = 1.0
    sh["lconst"] = lconst.reshape(48, 16 * 128)
    rconst = np.zeros((48, 512), np.float32)
    rconst[0:16, :] = 1.0
    sh["rconst"] = rconst
    sh["nege"] = -np.eye(16, dtype=np.float32)
    x = f("x")
    p = f("p")[0]
    per_core = []
    for b in range(8):
        m = dict(sh)
        m["xT"] = np.ascontiguousarray(x[b].T)
        m["xtok"] = np.ascontiguousarray(x[b])
        m["pT"] = np.ascontiguousarray(p[b].T)
        per_core.append(m)
    return per_core


def kernel(**inputs):
    nc, _ = build_program()
    in_maps = _host_inputs(inputs)
    res = run_bass_kernel_spmd(nc, in_maps, core_ids=list(range(8)))
    out = np.stack([np.ascontiguousarray(np.asarray(r["out"], dtype=np.float32).T) for r in res.results], 0)
    return out
```

```python
import numpy as np
from contextlib import ExitStack
import ml_dtypes
import concourse.bass as bass
import concourse.mybir as mybir
from concourse.bass_utils import run_bass_kernel_spmd

F32 = mybir.dt.float32
BF16 = mybir.dt.bfloat16
AF = mybir.ActivationFunctionType
ALU = mybir.AluOpType

S_LEN = 2048
D = 1024
TB = 512
NB = S_LEN // TB
CH = 128
ALPHA = 2.0 ** 0.25
LN_EPS = 1e-5
RMS_EPS = 1e-5
SB_BASE = 16512
SB_TOP = 229344
NEG = -30000.0
REORDER = True
PRIO_BLEV = True
PRIO_NOISE = 0.0
PRIO_SEED = 0
PYT_BANK = 6
LRU_PX = [0, 1]
LRU_PRI = [2]
LRU_PCV = [4]
LRU_PG = [5, 6, 7]
PHASE_MARKS = []
ACT_GRP_WINDOW = 3000
ACT_SWITCH_US = 0.0
ACT_CHAIN_LRU = True
_ACT_GRP = {AF.Exp: "explog", AF.Ln: "explog", AF.Sigmoid: "sig", AF.Tanh: "gelu", AF.Silu: "silu", AF.Gelu_apprx_tanh: "gelu"}

_DS = {F32: 4, BF16: 2}


class Buf:
    __slots__ = ("name", "t", "off", "size", "w", "r", "inh", "live", "inh_owner")

    def __init__(self, name, t, off, size):
        self.name, self.t, self.off, self.size = name, t, off, size
        self.w = {}
        self.r = {}
        self.inh = []
        self.inh_owner = None
        self.live = True

    def __getitem__(self, k):
        return self.t[k]


class Sched:
    ENG = ("pe", "act", "dve", "pool", "sp")
    LAT = 0.25

    def __init__(self, nc):
        self.nc = nc
        self.nodes = []
        self.sem = {}
        self.dma_sems = {}
        self.dma_last = {}
        self.bufs = []
        self.names = {}
        for e in ("pe", "act", "dve", "pool"):
            self.sem[e] = nc.alloc_semaphore("sem_" + e)

    def _uname(self, name):
        n = self.names.get(name, 0)
        self.names[name] = n + 1
        return name if n == 0 else "%s_%d" % (name, n)

    def sb(self, name, shape, dtype, off):
        size = int(np.prod(shape[1:])) * _DS[dtype]
        assert off % 32 == 0, (name, off)
        assert SB_BASE + off + size <= SB_TOP, (name, off, size)
        t = self.nc.alloc_sbuf_tensor_at(self._uname(name), list(shape), dtype, offset=SB_BASE + off)
        b = Buf(name, t, off, size)
        inh = set()
        for o in self.bufs:
            if o.off < off + size and off < o.off + o.size:
                o.live = False
                inh.update(o.inh)
                if o.inh_owner is not None:
                    inh.add(o.inh_owner)
                inh.update(o.w.values())
                for l in o.r.values():
                    inh.update(l)
        b.inh = sorted(inh)
        self.bufs.append(b)
        return b

    def ps(self, name, shape, dtype=F32):
        t = self.nc.alloc_psum_tensor(self._uname(name), list(shape), dtype)
        return Buf(name, t, -1, 0)

    def dma_sem(self, name):
        if name not in self.dma_sems:
            self.dma_sems[name] = self.nc.alloc_semaphore("d_" + name)
        return self.dma_sems[name]

    @staticmethod
    def _key(k):
        return k if isinstance(k, tuple) else (k, None)

    def op(self, eng, fn, reads=(), writes=(), dma=None, cost=1.0, grp=None):
        idx = len(self.nodes)
        preds = {}

        def add(p, raw):
            if p is None or p == idx:
                return
            if raw or p not in preds:
                preds[p] = preds.get(p, False) or raw

        def take_inh(b):
            if b.inh:
                for p in b.inh:
                    add(p, True)
                b.inh = []
                b.inh_owner = idx
            elif b.inh_owner is not None:
                add(b.inh_owner, True)

        for k in reads:
            b, sub = self._key(k)
            take_inh(b)
            if sub is None:
                for p in b.w.values():
                    add(p, True)
            else:
                add(b.w.get(sub), True)
                add(b.w.get(None), True)
        for k in writes:
            b, sub = self._key(k)
            take_inh(b)
            subs = set(b.w.keys()) | set(b.r.keys()) if sub is None else (sub, None)
            for kk in subs:
                add(b.w.get(kk), False)
                for p in b.r.get(kk, ()):
                    add(p, False)
        if dma is not None:
            self.dma_sem(dma)
            add(self.dma_last.get(dma), False)
            self.dma_last[dma] = idx
        self.nodes.append(dict(eng=eng, fn=fn, dma=dma, preds=preds, cost=cost, grp=grp, after=set()))
        for k in reads:
            b, sub = self._key(k)
            b.r.setdefault(sub, []).append(idx)
        for k in writes:
            b, sub = self._key(k)
            if sub is None:
                b.w = {None: idx}
                b.r = {}
            else:
                b.w[sub] = idx
                b.r[sub] = []
        return idx

    def order_after(self, first, second):
        if first is not None and second is not None and first != second:
            self.nodes[second]["after"].add(first)

    def schedule(self, reorder=True):
        import heapq
        n = len(self.nodes)
        succ = [[] for _ in range(n)]
        npred = [0] * n
        for i, nd in enumerate(self.nodes):
            allp = set(nd["preds"].keys()) | nd["after"]
            npred[i] = len(allp)
            for p in allp:
                succ[p].append(i)
        order = {e: [] for e in self.ENG}
        if not reorder:
            for i, nd in enumerate(self.nodes):
                order[nd["eng"]].append(i)
            return order
        finish = [0.0] * n
        blev = [0.0] * n
        for i in range(n - 1, -1, -1):
            m = 0.0
            for j in succ[i]:
                if blev[j] > m:
                    m = blev[j]
            blev[i] = m + self.nodes[i]["cost"]
        if PRIO_NOISE > 0.0:
            rs = np.random.RandomState(PRIO_SEED)
            nz = rs.standard_normal(n)
            blev = [b_ * (1.0 + PRIO_NOISE * z_) for b_, z_ in zip(blev, nz)]
        self.last_grp = None
        ready = {e: [] for e in self.ENG}
        avail = {e: [] for e in self.ENG}
        free = {e: 0.0 for e in self.ENG}
        for i in range(n):
            if npred[i] == 0:
                heapq.heappush(ready[self.nodes[i]["eng"]], (0.0, i))
        done = 0
        while done < n:
            best = None
            for e in self.ENG:
                while ready[e] and ready[e][0][0] <= free[e]:
                    j_ = heapq.heappop(ready[e])[1]
                    heapq.heappush(avail[e], (-blev[j_] if PRIO_BLEV else 0.0, j_))
                if avail[e]:
                    st, i = free[e], avail[e][0][1]
                    if e == "act" and self.nodes[i].get("grp") not in (None, self.last_grp):
                        cand = [j for _, j in sorted(avail[e]) if self.nodes[j].get("grp") in (None, self.last_grp)]
                        if cand:
                            i = cand[0]
                        else:
                            soon = [(dr_, j) for dr_, j in ready[e] if dr_ <= free[e] + ACT_SWITCH_US and self.nodes[j].get("grp") in (None, self.last_grp)]
                            if soon:
                                st, i = min(soon)
                elif ready[e]:
                    st, i = ready[e][0]
                else:
                    continue
                if best is None or (st, -blev[i], i) < (best[0], -blev[best[1]], best[1]):
                    best = (st, i, e)
            st, i, e = best
            if avail[e] and any(j == i for _, j in avail[e]):
                if avail[e][0][1] == i:
                    heapq.heappop(avail[e])
                else:
                    avail[e] = [(k_, j) for k_, j in avail[e] if j != i]
                    heapq.heapify(avail[e])
            elif ready[e][0][1] == i:
                heapq.heappop(ready[e])
            else:
                ready[e] = [(k_, j) for k_, j in ready[e] if j != i]
                heapq.heapify(ready[e])
            if e == "act" and self.nodes[i].get("grp") is not None:
                self.last_grp = self.nodes[i]["grp"]
            nd = self.nodes[i]
            issue = nd["cost"] if nd["dma"] is None else 0.15
            free[e] = st + issue
            finish[i] = st + nd["cost"]
            order[e].append(i)
            done += 1
            for j in succ[i]:
                npred[j] -= 1
                if npred[j] == 0:
                    ej = self.nodes[j]["eng"]
                    dr = 0.0
                    for p in self.nodes[j]["preds"]:
                        f = finish[p] + (self.LAT if (self.nodes[p]["eng"] != ej or self.nodes[p]["dma"]) else 0.0)
                        if f > dr:
                            dr = f
                    if self.nodes[j]["dma"] is not None and ej == "pool" and dr > 0.0:
                        dr += 4.0
                    for p in self.nodes[j]["after"]:
                        f = finish[p] - self.nodes[p]["cost"] + 0.15
                        if f > dr:
                            dr = f
                    heapq.heappush(ready[ej], (dr, j))
        self.model_time = max(finish) if finish else 0.0
        return order

    def finalize(self, reorder=True):
        order = self.schedule(reorder)
        pos = {}
        dma_cum = {}
        tok = [None] * len(self.nodes)
        for e in self.ENG:
            c = 0
            for i in order[e]:
                nd = self.nodes[i]
                if nd["dma"] is None:
                    c += 1
                    tok[i] = (self.sem[e], c)
                pos[i] = None
        for e in self.ENG:
            for i in order[e]:
                nd = self.nodes[i]
                if nd["dma"] is not None:
                    dma_cum[nd["dma"]] = dma_cum.get(nd["dma"], 0) + 16
                    tok[i] = (self.dma_sems[nd["dma"]], dma_cum[nd["dma"]])
        self.dma_total = dma_cum
        streams = {}
        for e in self.ENG:
            waited = {}
            lst = []
            for i in order[e]:
                nd = self.nodes[i]
                waits = []
                for p, raw in nd["preds"].items():
                    pn = self.nodes[p]
                    if pn["dma"] is None and pn["eng"] == e and e == "pe":
                        continue
                    sem, val = tok[p]
                    if waited.get(sem, 0) >= val:
                        continue
                    waited[sem] = val
                    waits.append((sem, val))
                inc = (tok[i][0], 16 if nd["dma"] is not None else 1)
                lst.append((waits, nd["fn"], inc))
            streams[e] = lst
        self.streams = streams
        return streams

    def emit(self, eng, e, final=()):
        for waits, fn, inc in self.streams[eng]:
            for sem, val in waits:
                e.wait_ge(sem, val)
            fn(e).then_inc(inc[0], inc[1])
        for name in final:
            e.wait_ge(self.dma_sems[name], self.dma_total[name])


def build_program(dbg=()):
    dbg = set(dbg)
    nc = bass.Bass("TRN2", target_bir_lowering=False)
    S = Sched(nc)

    def din(name, shape, dt=F32):
        return nc.dram_tensor(name, list(shape), dt, kind="ExternalInput").ap()

    xT_d = din("xT", [D, S_LEN])
    xtok_d = din("xtok", [S_LEN, D])
    pT_d = din("pT", [256, S_LEN])
    w_z_d = din("w_z", [D, 1024])
    w_xbc_d = din("w_xbc", [D, 2048])
    w_dt_d = din("w_dt", [D, 80])
    w_lru_d = din("w_lru", [D, 2048])
    gate_bd_d = din("gate_bd", [16, 128, 128])
    lru_cdiag_d = din("lru_cdiag", [128, 32 * 128])
    w_out_d = din("w_out", [2048, D])
    w_ff1_d = din("w_ff1", [D, 4096])
    w_ff2_d = din("w_ff2", [4096, D])
    w_gate_d = din("w_gate", [D, D])
    w_ple_d = din("w_ple", [256, D])
    colpar_d = din("colpar", [128, 176])
    rowpar_d = din("rowpar", [8, 128, D])
    ident_bf_d = din("ident_bf", [128, 128], BF16)
    ident_f_d = din("ident_f", [128, 128])
    negmask_d = din("negmask", [128, 512], BF16)
    lconst_d = din("lconst", [48, 16 * 128])
    rconst_d = din("rconst", [48, 512])
    nege_d = din("nege", [16, 16])
    out_d = nc.dram_tensor("out", [D, S_LEN], F32, kind="ExternalOutput").ap()
    dbg_out = {}

    def dbg_tensor(name, shape):
        dbg_out[name] = nc.dram_tensor("dbg_" + name, list(shape), F32, kind="ExternalOutput").ap()
        return dbg_out[name]

    KB = 1024
    ident_bf = S.sb("ident_bf", [128, 128], BF16, 0)
    ident_f = S.sb("ident_f", [128, 128], F32, 256)
    negmask = S.sb("negmask", [128, 512], BF16, 768)
    colpar = S.sb("colpar", [128, 176], F32, 1792)
    derived = S.sb("derived", [128, 32], F32, 2496)
    nege = S.sb("nege", [16, 16], F32, 2624)
    pose = S.sb("pose", [16, 16], F32, 2688)
    ones_c = S.sb("ones_c", [128, 128], F32, 2752)
    mhalf = S.sb("mhalf", [128, 8], F32, 3264)
    derived2 = S.sb("derived2", [128, 32], F32, 3328)
    P0 = 4 * KB

    CW_L, CB_L, GAB, GXB, APAR, CW_S, CB_S, DTB, ALOG = 0, 32, 40, 48, 56, 64, 128, 144, 145

    def col(i):
        return colpar[:, i:i + 1]

    PS = [S.ps("psum%d" % i, [128, 1024], F32) for i in range(4)]

    def bank(i):
        return PS[i // 2], (i % 2) * 512

    def fsz(ap):
        n = 1
        for d in ap.shape[1:]:
            n *= int(d)
        return n

    def dma(eng, out_ap, in_ap, reads, writes, sem):
        nbytes = fsz(in_ap) * int(in_ap.shape[0]) * _DS.get(in_ap.dtype, 4)
        return S.op(eng, lambda e: e.dma_start(out=out_ap, in_=in_ap), reads=reads, writes=writes, dma=sem,
                    cost=2.0 + nbytes / 150e3)

    def mm(out_ap, lhsT, rhs, start, stop, reads, writes):
        n = fsz(rhs)
        c = (0.06 + n / 2600.0) if rhs.dtype != F32 else (0.06 + n * 4 / 2600.0)
        return S.op("pe", lambda e: e.matmul(out_ap, lhsT=lhsT, rhs=rhs, start=start, stop=stop),
                    reads=reads, writes=writes, cost=max(c, 0.1))

    def tr(out_ap, in_ap, ident_ap, reads, writes):
        return S.op("pe", lambda e: e.transpose(out_ap, in_ap, ident_ap), reads=reads, writes=writes, cost=0.12)

    act_chain = [None, False]

    def act(out_ap, in_ap, func, reads, writes, bias=None, scale=1.0, accum=None):
        def f(e):
            kw = {}
            if bias is not None:
                kw["bias"] = bias
            if accum is not None:
                kw["accum_out"] = accum
            return e.activation(out=out_ap, in_=in_ap, func=func, scale=scale, **kw)
        idx = S.op("act", f, reads=reads, writes=writes, cost=0.2 + fsz(in_ap) / 1150.0, grp=_ACT_GRP.get(func))
        if act_chain[1] and _ACT_GRP.get(func) is not None:
            S.order_after(act_chain[0], idx)
            act_chain[0] = idx
        return idx

    def vcost(eng, ap):
        n = fsz(ap)
        return (0.08 + n / 900.0) if eng == "dve" else (0.12 + n / 460.0)

    def tt(eng, out_ap, in0, in1, op, reads, writes):
        return S.op(eng, lambda e: e.tensor_tensor(out=out_ap, in0=in0, in1=in1, op=op), reads=reads, writes=writes,
                    cost=vcost(eng, out_ap))

    def ts(eng, out_ap, in0, s1, s2, op0, op1, reads, writes):
        if s2 is None:
            return S.op(eng, lambda e: e.tensor_scalar(out=out_ap, in0=in0, scalar1=s1, scalar2=None, op0=op0),
                        reads=reads, writes=writes, cost=vcost(eng, out_ap))
        return S.op(eng, lambda e: e.tensor_scalar(out=out_ap, in0=in0, scalar1=s1, scalar2=s2, op0=op0, op1=op1),
                    reads=reads, writes=writes, cost=vcost(eng, out_ap))

    def stt(out_ap, in0, scalar, in1, op0, op1, reads, writes):
        return S.op("dve", lambda e: e.scalar_tensor_tensor(out=out_ap, in0=in0, scalar=scalar, in1=in1, op0=op0, op1=op1),
                    reads=reads, writes=writes, cost=0.2 + fsz(out_ap) / 900.0)

    def cp(eng, out_ap, in_ap, reads, writes):
        if eng == "act":
            return act(out_ap, in_ap, AF.Copy, reads, writes)
        return S.op(eng, lambda e: e.tensor_copy(out=out_ap, in_=in_ap), reads=reads, writes=writes, cost=vcost(eng, out_ap))

    def memset(eng, ap, val, writes):
        return S.op(eng, lambda e: e.memset(ap, val), writes=writes, cost=vcost(eng, ap))

    dma_chain = [None]

    def chain(idx):
        S.order_after(dma_chain[0], idx)
        dma_chain[0] = idx
        return idx

    def wload(dst, src_d, ktiles, ncols, key_fn, sem_fn, c0=0, dst_c0=0, piece=1024):
        v = src_d.rearrange("(k p) n -> p k n", p=128)
        for a in range(0, ncols, piece):
            n = min(piece, ncols - a)
            chain(dma("pool", dst[:, 0:ktiles, dst_c0 + a:dst_c0 + a + n], v[:, 0:ktiles, c0 + a:c0 + a + n],
                      [], [key_fn(a)], sem_fn(a)))

    dma("sp", ident_bf[:], ident_bf_d, [], [ident_bf], "c_idb")
    dma("sp", ident_f[:], ident_f_d, [], [ident_f], "c_idf")
    dma("sp", negmask[:], negmask_d, [], [negmask], "c_neg")
    dma("sp", colpar[:], colpar_d, [], [colpar], "c_col")
    dma("sp", nege[:], nege_d, [], [nege], "c_nege")
    memset("pool", ones_c[:], 1.0, [ones_c])
    memset("pool", mhalf[:], -0.5, [mhalf])
    ts("pool", pose[:], nege[:], -1.0, None, ALU.mult, None, [nege], [pose])
    act(derived[:, 0:8], colpar[:, APAR:APAR + 8], AF.Exp, [colpar], [(derived, "sc")], scale=-1.0)
    act(derived[:, 0:8], derived[:, 0:8], AF.Ln, [(derived, "sc")], [(derived, "sc")], bias=1.0)
    ts("dve", derived[:, 8:16], derived[:, 0:8], 8.0, None, ALU.mult, None, [(derived, "sc")], [(derived, "nsc")])
    ts("dve", derived[:, 0:8], derived[:, 0:8], -8.0, None, ALU.mult, None, [(derived, "sc"), (derived, "nsc")], [(derived, "sc")])
    ts("dve", derived2[:, 0:8], derived[:, 0:8], 0.5, None, ALU.mult, None, [(derived, "sc")], [(derived2, 0)])
    ts("dve", derived2[:, 8:16], derived[:, 8:16], 0.5, None, ALU.mult, None, [(derived, "nsc")], [(derived2, 1)])
    ts("dve", derived2[:, 16:24], colpar[:, GAB:GAB + 8], 0.5, None, ALU.mult, None, [colpar], [(derived2, 2)])
    ts("dve", derived2[:, 24:32], colpar[:, GXB:GXB + 8], 0.5, None, ALU.mult, None, [colpar], [(derived2, 3)])
    act(derived[0:80, 16:17], colpar[0:80, ALOG:ALOG + 1], AF.Exp, [colpar], [(derived, "A")])
    ts("dve", derived[0:80, 16:17], derived[0:80, 16:17], -1.0, None, ALU.mult, None, [(derived, "A")], [(derived, "A")])

    R1 = P0
    Wz = S.sb("Wz", [128, 8, 1024], BF16, R1)
    Wxbc = S.sb("Wxbc", [128, 8, 2048], BF16, R1 + 16 * KB)
    Wdt = S.sb("Wdt", [128, 8, 80], BF16, R1 + 48 * KB)
    YS0 = 86 * KB
    y_ssdT = S.sb("y_ssdT", [128, 8, S_LEN], BF16, YS0)
    o = 54 * KB
    xTs = [S.sb("xTs%d" % i, [128, 8, TB], BF16, o + i * 8 * KB) for i in range(2)]
    o += 16 * KB
    szb = S.sb("sz", [128, 4, 1024], BF16, o)
    o += 8 * KB
    xs_sb = [S.sb("xs_sb%d" % i, [128, 1024], BF16, o + i * 2 * KB) for i in range(2)]
    o += 4 * KB
    xdt = [S.sb("xdt%d" % i, [128, 1024], BF16, o + i * 2 * KB) for i in range(2)]
    o += 4 * KB
    assert o <= 86 * KB
    o = 118 * KB
    xpad = [S.sb("xpad%d" % i, [128, TB + 32], F32, o + i * (TB + 32) * 4) for i in range(2)]
    o += 2 * (TB + 32) * 4
    cacc = [S.sb("cacc%d" % i, [128, TB], F32, o + i * 2 * KB) for i in range(2)]
    o += 4 * KB
    stail = S.sb("stail", [128, 16, 4], F32, o)
    o += 256
    xbcT = S.sb("xbcT", [128, 16, TB], BF16, o)
    o += 16 * KB
    dt_e = S.sb("dt_e", [80, TB], F32, o); o += 2 * KB
    dtT = S.sb("dtT", [80, TB], F32, o); o += 2 * KB
    aT = S.sb("aT", [80, TB], F32, o); o += 2 * KB
    acs = S.sb("acs", [80, TB], F32, o); o += 2 * KB
    smallT = S.sb("smallT", [80, TB], F32, o); o += 2 * KB
    ddtmp = dt_e
    ea0 = S.sb("ea0", [16, TB], F32, o); o += 2 * KB
    Rb = S.sb("Rb", [48, TB], F32, o); o += 2 * KB
    Lt = [S.sb("Lt0", [48, 16, 128], F32, o)]
    o += 8 * KB
    tokm = [S.sb("tokm%d" % i, [128, 80], F32, o + i * 320) for i in range(2)]
    o += 640
    diagcd = S.sb("diagcd", [16, 16], F32, o); o += 64
    cdrow = [S.sb("cdrow%d" % i, [128, 16], F32, o + i * 64) for i in range(2)]
    o += 128
    ssq = S.sb("ssq", [128, 8], F32, o); o += 32
    o = (o + 31) // 32 * 32
    xdtd = [S.sb("xdtd%d" % i, [128, 1024], BF16, o + i * 2 * KB) for i in range(2)]
    o += 4 * KB
    Btok = [S.sb("Btok%d" % i, [128, 512], BF16, o + i * KB) for i in range(2)]
    o += 2 * KB
    E4 = [S.sb("E4_%d" % i, [128, 512], F32, o + i * 2 * KB) for i in range(2)]
    o += 4 * KB
    scoresT = S.sb("scoresT", [128, 4, 512], BF16, o); o += 4 * KB
    Sst = S.sb("Sst", [128, 1024], F32, o); o += 4 * KB
    S_bf = [S.sb("S_bf%d" % i, [128, 1024], BF16, o + i * 2 * KB) for i in range(2)]
    o += 4 * KB
    t1 = S.sb("t1", [128, 1024], F32, o); o += 4 * KB
    t2 = S.sb("t2", [128, 1024], F32, o); o += 4 * KB
    ytok = S.sb("ytok", [128, 1024], BF16, o); o += 2 * KB
    Drow = S.sb("Drow", [128, 1024], F32, o); o += 4 * KB
    assert o <= 207 * KB, o

    xT_v = xT_d.rearrange("(k p) t -> p k t", p=128)

    def load_xTs(b):
        return dma("pool", xTs[b % 2][:], xT_v[:, :, b * TB:(b + 1) * TB], [], [xTs[b % 2]], "xTs%d" % (b % 2))

    chain(load_xTs(0))
    wload(Wdt, w_dt_d, 8, 80, lambda a: Wdt, lambda a: "w_dt")
    for a in range(0, 1024, 512):
        wload(Wz, w_z_d, 8, 512, lambda q, a=a: (Wz, a // 512), lambda q, a=a: "w_z%d" % (a // 512), c0=a, dst_c0=a)
    for a in range(0, 2048, 512):
        wload(Wxbc, w_xbc_d, 8, 512, lambda q, a=a: (Wxbc, a // 512), lambda q, a=a: "w_xbc%d" % (a // 512),
              c0=a, dst_c0=a)
    dma("sp", Drow[:], rowpar_d[1], [], [Drow], "c_drow")
    dma("sp", Lt[0][:].rearrange("p h s -> p (h s)"), lconst_d, [], [Lt[0]], "c_lt0")
    dma("sp", Rb[:], rconst_d, [], [Rb], "c_rb")
    memset("pool", stail[:], 0.0, [stail])
    memset("pool", Sst[:], 0.0, [Sst])
    memset("pool", S_bf[0][:], 0.0, [S_bf[0]])
    memset("pool", smallT[:], 0.0, [smallT])

    def bc3(ap2, n_inner):
        return ap2.unsqueeze(2).broadcast_to([ap2.shape[0], ap2.shape[1], n_inner])

    hd3 = lambda ap: ap.rearrange("p (h d) -> p h d", d=64)

    for b in range(NB):
        xs_b = xTs[b % 2]
        if b + 1 < NB:
            load_xTs(b + 1)
        pb, pc = bank(7)
        for k in range(8):
            mm(pb[0:80, pc:pc + TB], Wdt[:, k, :], xs_b[:, k, :], k == 0, k == 7, [Wdt, xs_b], [(pb, 1)])
        act(dt_e[:], pb[0:80, pc:pc + TB], AF.Exp, [(pb, 1), colpar], [dt_e], bias=colpar[0:80, DTB:DTB + 1])
        act(dtT[:], dt_e[:], AF.Ln, [dt_e], [dtT], bias=1.0)
        ts("dve", aT[:], dtT[:], derived[0:80, 16:17], None, ALU.mult, None, [dtT, (derived, "A")], [aT])
        for c in range(4):
            S.op("dve", lambda e, c=c: e.tensor_tensor_scan(out=acs[:, c * CH:(c + 1) * CH], data0=ones_c[0:80, 0:CH],
                                                            data1=aT[:, c * CH:(c + 1) * CH], initial=0.0,
                                                            op0=ALU.mult, op1=ALU.add),
                 reads=[ones_c, aT], writes=[(acs, c)])
        cp("pool", Rb[32:48, :], acs[32:48, :], [acs], [Rb])
        cp("pool", smallT[0:16, :], dtT[0:16, :], [dtT], [(smallT, 0)])
        for c in range(4):
            act(ddtmp[32:48, c * CH:(c + 1) * CH], acs[32:48, c * CH:(c + 1) * CH], AF.Exp, [acs, dtT], [(ddtmp, c)],
                bias=acs[32:48, c * CH + CH - 1:c * CH + CH], scale=-1.0)
        tt("pool", smallT[32:48, :], ddtmp[32:48, :], dtT[32:48, :], ALU.mult, [ddtmp, dtT], [(smallT, 1)])
        act(smallT[64:80, :], acs[64:80, :], AF.Exp, [acs], [(smallT, 2)])
        act(ea0[:], acs[0:16, :], AF.Exp, [acs], [ea0])
        for c in range(4):
            pz = PS[c % 2]
            for hf in range(2):
                for k in range(8):
                    mm(pz[:, hf * 512:(hf + 1) * 512], xs_b[:, k, c * CH:(c + 1) * CH], Wz[:, k, hf * 512:(hf + 1) * 512],
                       k == 0, k == 7, [xs_b, (Wz, hf)], [(pz, hf)])
            act(szb[:, c, :], pz[:], AF.Silu, [pz], [(szb, c)])
        for e_ in range(16):
            pb, pc = bank(4 + (e_ % 2))
            pkey = (pb, (4 + e_ % 2) % 2)
            for k in range(8):
                mm(pb[:, pc:pc + TB], Wxbc[:, k, e_ * 128:(e_ + 1) * 128], xs_b[:, k, :], k == 0, k == 7,
                   [xs_b, (Wxbc, e_ // 4)], [pkey])
            xp = xpad[e_ % 2]
            ca = cacc[e_ % 2]
            cw = CW_S + e_ * 4
            act(xp[:, 4:4 + TB], pb[:, pc:pc + TB], AF.Copy, [pkey], [(xp, "m")])
            act(ca[:], pb[:, pc:pc + TB], AF.Identity, [pkey, colpar], [ca], bias=col(CB_S + e_), scale=col(cw + 3))
            cp("pool", xp[:, 0:4], stail[:, e_, :], [(stail, e_)], [(xp, "t")])
            for kk in range(3):
                stt(ca[:], xp[:, 1 + kk:1 + kk + TB], col(cw + kk), ca[:], ALU.mult, ALU.add, [xp, ca, colpar], [ca])
            cp("pool", stail[:, e_, :], xp[:, TB:TB + 4], [xp], [(stail, e_)])
            act(xbcT[:, e_, :], ca[:], AF.Silu, [ca], [(xbcT, e_)])

        def stA(c):
            ci = b * 4 + c
            c0 = c * CH
            r = ci % 2
            tk = tokm[r]
            pb7, pc7 = bank(7)
            tr(pb7[:, pc7:pc7 + 80], smallT[0:80, c0:c0 + CH], ident_f[0:80, 0:80], [smallT, ident_f], [(pb7, 1)])
            ts("pool", diagcd[:], pose[:], ea0[:, c0 + CH - 1:c0 + CH], 1.0, ALU.mult, ALU.mult, [pose, ea0], [diagcd])
            mm(pb7[:, pc7 + 128:pc7 + 144], Rb[0:16, 0:128], diagcd[:], True, True, [Rb, diagcd], [(pb7, 1)])
            pB = pb7[:, pc7 + 256:pc7 + 512].bitcast(BF16)
            for g in range(4):
                tr(pB[:, g * 128:(g + 1) * 128], xbcT[:, 8 + g, c0:c0 + CH], ident_bf[:], [(xbcT, 8 + g), ident_bf], [(pb7, 1)])
            cp("act", tk[:], pb7[:, pc7:pc7 + 80], [(pb7, 1)], [tk])
            cp("act", cdrow[r][:], pb7[:, pc7 + 128:pc7 + 144], [(pb7, 1)], [cdrow[r]])
            cp("act", Btok[r][:], pB, [(pb7, 1)], [Btok[r]])
            pb6, pc6 = bank(6)
            pxs = pb6[:, pc6:pc6 + 512].bitcast(BF16)
            for e_ in range(8):
                tr(pxs[:, e_ * 128:(e_ + 1) * 128], xbcT[:, e_, c0:c0 + CH], ident_bf[:], [(xbcT, e_), ident_bf], [(pb6, 0)])
            cp("act", xs_sb[r][:], pxs, [(pb6, 0)], [xs_sb[r]])
            tt("dve", hd3(xdt[r][:]), hd3(xs_sb[r][:]), bc3(tk[:, 0:16], 64), ALU.mult, [xs_sb[r], tk], [xdt[r]])
            tt("dve", hd3(xdtd[r][:]), hd3(xs_sb[r][:]), bc3(tk[:, 32:48], 64), ALU.mult, [xs_sb[r], tk], [xdtd[r]])
            pb4, pc4 = bank(4)
            for g in range(4):
                mm(pb4[:, pc4 + g * 128:pc4 + (g + 1) * 128], xbcT[:, 8 + g, c0:c0 + CH], xbcT[:, 12 + g, c0:c0 + CH], True, True,
                   [(xbcT, 8 + g), (xbcT, 12 + g)], [(pb4, 0)])
            tt("pool", Lt[0][0:16, :, :], acs[0:16, c0:c0 + CH].unsqueeze(1).broadcast_to([16, 16, CH]),
               bc3(nege[:], CH), ALU.mult, [(acs, c), nege], [Lt[0]])

        def stB(c):
            c0 = c * CH
            lt = Lt[0]
            pb4, pc4 = bank(4)
            for g in range(4):
                pbs, pcs = bank(5) if g % 2 == 0 else bank(6)
                skey = (pbs, 1) if g % 2 == 0 else (pbs, 0)
                mm(pbs[:, pcs:pcs + 512], ident_bf[:], negmask[:], True, False, [ident_bf, negmask], [skey])
                for j in range(4):
                    h = 4 * g + j
                    mm(pbs[:, pcs + j * 128:pcs + (j + 1) * 128], lt[0:48, h, :], Rb[0:48, c0:c0 + CH], False, j == 3,
                       [lt, Rb], [skey])
                e4 = E4[g % 2]
                act(e4[:], pbs[:, pcs:pcs + 512], AF.Exp, [skey], [e4])
                tt("dve", scoresT[:, g, :].rearrange("p (j l) -> p j l", j=4), e4[:].rearrange("p (j l) -> p j l", j=4),
                   pb4[:, pc4 + g * 128:pc4 + (g + 1) * 128].unsqueeze(1).broadcast_to([128, 4, 128]), ALU.mult,
                   [e4, (pb4, 0)], [(scoresT, g)])

        def stC1(c):
            ci = b * 4 + c
            c0 = c * CH
            r = ci % 2
            tk = tokm[r]
            py = PS[0]
            pyo = PS[1]
            sprev = S_bf[ci % 2]
            snext = S_bf[(ci + 1) % 2]
            for g in range(4):
                mm(pyo[:, g * 256:(g + 1) * 256], xbcT[:, 12 + g, c0:c0 + CH], sprev[:, g * 256:(g + 1) * 256], True, True,
                   [(xbcT, 12 + g), sprev], [(pyo, g // 2)])
            for g in range(4):
                for j in range(4):
                    h = 4 * g + j
                    mm(py[:, h * 64:(h + 1) * 64], scoresT[:, g, j * 128:(j + 1) * 128], xdt[r][:, h * 64:(h + 1) * 64], True, True,
                       [(scoresT, g), xdt[r]], [(py, h // 8)])
            tt("dve", hd3(t1[:]), hd3(pyo[:]), bc3(tk[:, 64:80], 64), ALU.mult, [pyo, tk], [t1])
            pst = PS[1]
            for g in range(4):
                mm(pst[:, g * 256:(g + 1) * 256], Btok[r][:, g * 128:(g + 1) * 128], xdtd[r][:, g * 256:(g + 1) * 256], True, True,
                   [Btok[r], xdtd[r]], [(pst, g // 2)])
            tt("dve", hd3(Sst[:]), hd3(Sst[:]), bc3(cdrow[r][:], 64), ALU.mult, [Sst, cdrow[r]], [Sst])
            tt("dve", Sst[:], Sst[:], pst[:], ALU.add, [Sst, pst], [Sst])
            cp("act", snext[:], Sst[:], [Sst], [snext])
            tt("dve", t1[:], t1[:], py[:], ALU.add, [t1, py], [t1])
            tt("dve", t2[:], xs_sb[r][:], Drow[:], ALU.mult, [xs_sb[r], Drow], [t2])

        def stC2(c):
            ci = b * 4 + c
            tt("dve", t1[:], t1[:], t2[:], ALU.add, [t1, t2], [t1])
            tt("dve", t1[:], t1[:], szb[:, c, :], ALU.mult, [t1, (szb, c)], [t1])
            sq = ssq[:, (ci % 2) * 4:(ci % 2) * 4 + 4]
            for g in range(4):
                act(t2[:, g * 256:(g + 1) * 256], t1[:, g * 256:(g + 1) * 256], AF.Square, [t1], [t2], accum=sq[:, g:g + 1])
            ts("pool", sq, sq, 1.0 / 256.0, RMS_EPS, ALU.mult, ALU.add, [t2], [t2])
            tt("pool", sq, sq, mhalf[:, 0:4], ALU.pow, [t2, mhalf], [t2])
            for g in range(4):
                act(ytok[:, g * 256:(g + 1) * 256], t1[:, g * 256:(g + 1) * 256], AF.Identity, [t1, t2], [ytok], scale=sq[:, g:g + 1])
            pb6, pc6 = bank(PYT_BANK)
            pyT = pb6[:, pc6:pc6 + 512].bitcast(BF16)
            for e_ in range(8):
                tr(pyT[:, e_ * 128:(e_ + 1) * 128], ytok[:, e_ * 128:(e_ + 1) * 128], ident_bf[:], [ytok, ident_bf], [(pb6, PYT_BANK % 2)])
            cp("act", y_ssdT[:, :, ci * CH:(ci + 1) * CH], pyT.rearrange("p (e l) -> p e l", e=8), [(pb6, PYT_BANK % 2)], [(y_ssdT, ci)])

        stA(0); stB(0); stA(1)
        for c in range(4):
            stC1(c)
            if c + 1 < 4:
                stB(c + 1)
            if c + 2 < 4:
                stA(c + 2)
            stC2(c)

    final_tokens = []
    if "y_ssd" in dbg:
        dt_ = dbg_tensor("y_ssd", [D, S_LEN])
        stage = S.sb("dbgstage", [128, 8, S_LEN], F32, 118 * KB)
        cp("dve", stage[:], y_ssdT[:], [y_ssdT], [stage])
        dma("sp", dt_.rearrange("(e p) t -> p e t", p=128), stage[:], [stage], [], "dbg"); final_tokens.append("dbg")

    PHASE_MARKS.append(("lru", len(S.nodes)))
    if "stop_ssd" not in dbg:
        Wlx = S.sb("w_lru_x", [128, 8, 1024], BF16, 4 * KB)
        Wlg = S.sb("w_lru_g", [128, 8, 1024], BF16, 20 * KB)
        y_lruT = S.sb("y_lruT", [128, 8, S_LEN], BF16, 118 * KB)
        xl_f = S.sb("xl_f", [128, 8, TB], F32, 54 * KB)
        ra = S.sb("ra", [128, 8, TB], F32, 70 * KB)
        iu = S.sb("iu", [128, 8, TB], F32, 36 * KB)
        o = 150 * KB
        Tm = S.sb("Tm", [128, 8, TB], F32, o); o += 16 * KB
        xl_b = S.sb("xl_b", [128, 8, TB], BF16, o); o += 8 * KB
        xTl = [S.sb("xTl%d" % i, [128, 8, TB], BF16, o + i * 8 * KB) for i in range(2)]
        o += 16 * KB
        xpl = [S.sb("xpl%d" % i, [128, TB + 32], BF16, o + i * (TB + 32) * 2) for i in range(2)]
        o += 2 * (TB + 32) * 2
        Wga = S.sb("Wga", [128, 16, 128], BF16, o); o += 4 * KB
        cdiag = S.sb("cdiag", [128, 32, 128], BF16, o); o += 8 * KB
        ltail = S.sb("ltail", [128, 8, 4], BF16, o); o += 64
        hcarry = S.sb("hcarry", [128, 8], F32, o); o += 32
        assert o <= 207 * KB, o

        def load_xTl(b):
            dma("pool", xTl[b % 2][:], xT_v[:, :, b * TB:(b + 1) * TB], [], [xTl[b % 2]], "xTl%d" % (b % 2))

        load_xTl(0)
        for a in range(2):
            wload(Wlx if a == 0 else Wlg, w_lru_d, 8, 1024, lambda q, a=a: (Wlx if a == 0 else Wlg), lambda q, a=a: "w_lru%d" % a, c0=a * 1024, dst_c0=0)
        dma("pool", Wga[:], gate_bd_d.rearrange("g i j -> i g j"), [], [Wga], "w_ga")
        cdv = cdiag[:].rearrange("p g j -> p (g j)")
        for hh in range(2):
            dma("pool", cdv[:, hh * 2048:(hh + 1) * 2048], lru_cdiag_d[:, hh * 2048:(hh + 1) * 2048], [], [(cdiag, hh)], "w_cd%d" % hh)
        memset("pool", ltail[:], 0.0, [ltail])
        SC, NSC = 0, 8

        act_chain[1] = ACT_CHAIN_LRU
        for b in range(NB):
            xs_b = xTl[b % 2]
            if b + 1 < NB:
                load_xTl(b + 1)

            def st1a(c):
                bx_ = LRU_PX[c % len(LRU_PX)]
                pb, pc = bank(bx_)
                pk = (pb, bx_ % 2)
                for k in range(8):
                    mm(pb[:, pc:pc + TB], Wlx[:, k, c * 128:(c + 1) * 128], xs_b[:, k, :], k == 0, k == 7, [xs_b, Wlx], [pk])
                xp = xpl[c % 2]
                act(xp[:, 4:4 + TB], pb[:, pc:pc + TB], AF.Copy, [pk], [(xp, "m")])
                cp("pool", xp[:, 0:4], ltail[:, c, :], [(ltail, c)], [(xp, "t")])
                bcv_ = LRU_PCV[c % len(LRU_PCV)]
                pcb, pcc = bank(bcv_)
                pck = (pcb, bcv_ % 2)
                for kk in range(4):
                    mm(pcb[:, pcc:pcc + TB], cdiag[:, c * 4 + kk, :], xp[:, 1 + kk:1 + kk + TB], kk == 0, kk == 3, [(cdiag, (c * 4 + kk) // 16), xp], [pck])
                cp("pool", ltail[:, c, :], xp[:, TB:TB + 4], [xp], [(ltail, c)])
                act(xl_f[:, c, :], pcb[:, pcc:pcc + TB], AF.Identity, [pck, colpar], [(xl_f, c)], bias=col(CB_L + c))
                cp("dve", xl_b[:, c, :], xl_f[:, c, :], [(xl_f, c)], [(xl_b, c)])

            def st1b(c):
                for gi in range(2):
                    pb, pc = bank(LRU_PRI[c % len(LRU_PRI)] + gi)
                    pk = (pb, gi)
                    mm(pb[:, pc:pc + TB], Wga[:, gi * 8 + c, :], xl_b[:, c, :], True, True, [Wga, (xl_b, c)], [pk])

            def st2(c):
                pb, _ = bank(LRU_PRI[c % len(LRU_PRI)])
                act(ra[:, c, :], pb[:, 0:TB], AF.Tanh, [(pb, 0), (derived2, 2)], [(ra, c)], bias=derived2[:, 16 + c:17 + c], scale=0.5)
                act(iu[:, c, :], pb[:, 512:512 + TB], AF.Tanh, [(pb, 1), (derived2, 3)], [(iu, c)], bias=derived2[:, 24 + c:25 + c], scale=0.5)
                act(Tm[:, c, :], ra[:, c, :], AF.Tanh, [(ra, c), (derived2, 1)], [(Tm, c)], scale=derived2[:, 8 + c:9 + c],
                    bias=derived2[:, 8 + c:9 + c])

            for c in range(10):
                if c < 8:
                    st1a(c)
                if c >= 2:
                    st2(c - 2)
                if 1 <= c < 9:
                    st1b(c - 1)
            for c in range(8):
                act(ra[:, c, :], ra[:, c, :], AF.Exp, [(ra, c), (derived2, 0)], [(ra, c)], scale=derived2[:, c:c + 1], bias=derived2[:, c:c + 1])
                stt(iu[:, c, :], iu[:, c, :], 1.0, xl_f[:, c, :], ALU.add, ALU.mult, [(iu, c), (xl_f, c)], [(iu, c)])
            for c in range(8):
                tt("dve", xl_f[:, c, :], ra[:, c, :], ra[:, c, :], ALU.mult, [(ra, c)], [(xl_f, c)])
                stt(Tm[:, c, :], xl_f[:, c, :], 1.0, Tm[:, c, :], ALU.add, ALU.mult, [(xl_f, c), (Tm, c)], [(Tm, c)])
            for c in range(8):
                act(Tm[:, c, :], Tm[:, c, :], AF.Ln, [(Tm, c)], [(Tm, c)])
                act(Tm[:, c, :], Tm[:, c, :], AF.Exp, [(Tm, c)], [(Tm, c)], scale=0.5)
                if b == 0:
                    memset("pool", Tm[:, c, 0:1], 1.0, [(Tm, c)])
            for c in range(8):
                tt("dve", iu[:, c, :], iu[:, c, :], Tm[:, c, :], ALU.mult, [(iu, c), (Tm, c)], [(iu, c)])
                init = 0.0 if b == 0 else hcarry[:, c:c + 1]
                S.op("dve", lambda e, c=c, init=init: e.tensor_tensor_scan(out=xl_f[:, c, :], data0=ra[:, c, :], data1=iu[:, c, :],
                                                                           initial=init, op0=ALU.mult, op1=ALU.add),
                     reads=[(ra, c), (iu, c), (hcarry, c)], writes=[(xl_f, c)])
                cp("pool", hcarry[:, c:c + 1], xl_f[:, c, TB - 1:TB], [(xl_f, c)], [(hcarry, c)])
            for c in range(8):
                bg_ = LRU_PG[c % len(LRU_PG)]
                pb, pc = bank(bg_)
                pk = (pb, bg_ % 2)
                for k in range(8):
                    mm(pb[:, pc:pc + TB], Wlg[:, k, c * 128:(c + 1) * 128], xs_b[:, k, :], k == 0, k == 7, [xs_b, Wlg], [pk])
                act(ra[:, c, :], pb[:, pc:pc + TB], AF.Gelu_apprx_tanh, [pk], [(ra, c)])
                stt(y_lruT[:, c, b * TB:(b + 1) * TB], ra[:, c, :], 0.5, xl_f[:, c, :], ALU.mult, ALU.mult, [(ra, c), (xl_f, c)], [(y_lruT, b)])

        act_chain[1] = False
        if "cdiag" in dbg:
            dt_ = dbg_tensor("cdiag", [128, 4096])
            stg = S.sb("dbgstage3", [128, 4096], F32, 54 * KB)
            cp("act", stg[:], cdiag[:].rearrange("p g j -> p (g j)"), [cdiag], [stg])
            dma("sp", dt_, stg[:], [stg], [], "dbg"); final_tokens.append("dbg")
        if "y_lru" in dbg:
            dt_ = dbg_tensor("y_lru", [D, S_LEN])
            stage = S.sb("dbgstage2", [128, 8, S_LEN], F32, 4 * KB)
            cp("dve", stage[:], y_lruT[:], [y_lruT], [stage])
            dma("sp", dt_.rearrange("(e p) t -> p e t", p=128), stage[:], [stage], [], "dbg"); final_tokens.append("dbg")

    def ln_tile(v_ap, vkey, out_ap, okey, grow, brow, wk):
        stats, mv, xh = wk["stats"], wk["mv"], wk["xh"]
        for j in range(2):
            S.op("dve", lambda e, j=j: e.bn_stats(out=stats[:, j * 6:(j + 1) * 6], in_=v_ap[:, j * 512:(j + 1) * 512]),
                 reads=[vkey], writes=[(stats, j)])
        S.op("dve", lambda e: e.bn_aggr(out=mv[:, 0:2], in_=stats[:, 0:12]), reads=[stats], writes=[(mv, 0)])
        ts("pool", mv[:, 2:3], mv[:, 1:2], LN_EPS, 1.0, ALU.add, ALU.mult, [(mv, 0)], [(mv, 1)])
        tt("pool", mv[:, 2:3], mv[:, 2:3], mhalf[:, 0:1], ALU.pow, [(mv, 1), mhalf], [(mv, 1)])
        ts("dve", mv[:, 3:4], mv[:, 0:1], mv[:, 2:3], -1.0, ALU.mult, ALU.mult, [(mv, 0), (mv, 1)], [(mv, 2)])
        act(xh[:], v_ap, AF.Identity, [vkey, (mv, 1), (mv, 2)], [xh], bias=mv[:, 3:4], scale=mv[:, 2:3])
        tt("dve", xh[:], xh[:], grow[:], ALU.mult, [xh, grow], [xh])
        tt("dve", out_ap, xh[:], brow[:], ALU.add, [xh, brow], [okey])

    def ln_work(o):
        wk = {}
        wk["stats"] = S.sb("ln_stats", [128, 12], F32, o); o += 64
        wk["mv"] = S.sb("ln_mv", [128, 8], F32, o); o += 32
        wk["xh"] = S.sb("ln_xh", [128, D], F32, o); o += 4 * KB
        return wk, o

    PHASE_MARKS.append(("o", len(S.nodes)))
    if "stop_ssd" not in dbg and "stop_lru" not in dbg:
        Wo_a = S.sb("w_out_a", [128, 8, D], BF16, 4 * KB)
        Wo_b = S.sb("w_out_b", [128, 8, D], BF16, 166 * KB)

        def Wo_k(k):
            return (Wo_a, k) if k < 8 else (Wo_b, k - 8)
        x1h = [S.sb("x1a", [128, 8, D], F32, 54 * KB), S.sb("x1b", [128, 8, D], F32, 20 * KB)]
        o = 182 * KB
        xtk = [S.sb("xtk%d" % i, [128, D], F32, o + i * 4 * KB) for i in range(2)]
        o += 8 * KB
        g1 = S.sb("g1", [128, D], F32, o); o += 4 * KB
        b1 = S.sb("b1", [128, D], F32, o); o += 4 * KB
        vtmp = [S.sb("vtmp%d" % i, [128, D], F32, o + i * 4 * KB) for i in range(1)]
        o += 4 * KB
        wkO, o = ln_work(o)
        assert o <= 207 * KB, o
        castO = S.sb("castO", [128, D], BF16, 52 * KB)
        v = w_out_d.rearrange("(k p) n -> p k n", p=128)
        for q in range(4):
            wt_, k0_ = Wo_k(q * 4)
            dma("pool", wt_[:, k0_:k0_ + 4, :], v[:, q * 4:(q + 1) * 4, :], [], [(wt_, q % 2)], "w_out%d" % q)
        dma("sp", g1[:], rowpar_d[2], [], [g1], "c_g1")
        dma("sp", b1[:], rowpar_d[3], [], [b1], "c_b1")
        for j in range(8):
            ts("dve", Wo_b[:, j, :], Wo_b[:, j, :], col(146 + j), None, ALU.mult, None, [(Wo_b, j // 4), colpar], [(Wo_b, j // 4)])

        def x1_tile(tt_):
            return x1h[tt_ // 8][:, tt_ % 8, :], (x1h[tt_ // 8], tt_ % 8)

        def load_xtk(tt_):
            dma("sp", xtk[tt_ % 2][:], xtok_d[tt_ * 128:(tt_ + 1) * 128, :], [], [xtk[tt_ % 2]], "xtk%d" % (tt_ % 2))

        load_xtk(0)
        for tt_ in range(16):
            if tt_ + 1 < 16:
                load_xtk(tt_ + 1)
            pm = PS[tt_ % 3]
            for hf in range(2):
                for k in range(16):
                    src = y_lruT if k < 8 else y_ssdT
                    skey = (y_lruT, tt_ // 4) if k < 8 else (y_ssdT, tt_)
                    wt_, kk_ = Wo_k(k)
                    mm(pm[:, hf * 512:(hf + 1) * 512], src[:, k % 8, tt_ * 128:(tt_ + 1) * 128], wt_[:, kk_, hf * 512:(hf + 1) * 512],
                       k == 0, k == 15, [skey, (wt_, kk_ // 4)], [(pm, hf)])
            vt = vtmp[0]
            stt(vt[:], xtk[tt_ % 2][:], ALPHA, pm[:], ALU.mult, ALU.add, [xtk[tt_ % 2], pm], [vt])
            oap, okey = x1_tile(tt_)
            ln_tile(vt[:], vt, oap, okey, g1, b1, wkO)
            cp("act", castO[:], oap, [okey], [castO])
            pb, pc = bank(6 + tt_ % 2)
            pk = (pb, tt_ % 2)
            pT_ = pb[:, pc:pc + 512].bitcast(BF16)
            for e_ in range(8):
                tr(pT_[:, e_ * 128:(e_ + 1) * 128], castO[:, e_ * 128:(e_ + 1) * 128], ident_bf[:], [castO, ident_bf], [pk])
            cp("act", y_ssdT[:, :, tt_ * 128:(tt_ + 1) * 128], pT_.rearrange("p (e l) -> p e l", e=8), [pk], [(y_ssdT, tt_)])

        if "x1" in dbg:
            dt_ = dbg_tensor("x1", [S_LEN, D])
            for hh in range(2):
                dma("sp", dt_[hh * 1024:(hh + 1) * 1024, :].rearrange("(t p) d -> p t d", p=128), x1h[hh][:], [x1h[hh]], [], "dbg")
                final_tokens.append("dbg")

    def to_feature_major(xh2, dstT, castbuf, tiles, nbanks=4):
        for tt_ in tiles:
            cb = castbuf[tt_ % 2]
            cp("act" if tt_ % 2 == 0 else "dve", cb[:], xh2[tt_ // 8][:, tt_ % 8, :], [(xh2[tt_ // 8], tt_ % 8)], [cb])
            pb, pc = bank(tt_ % nbanks)
            pk = (pb, (tt_ % nbanks) % 2)
            pT_ = pb[:, pc:pc + 512].bitcast(BF16)
            for e_ in range(8):
                tr(pT_[:, e_ * 128:(e_ + 1) * 128], cb[:, e_ * 128:(e_ + 1) * 128], ident_bf[:], [cb, ident_bf], [pk])
            cp("dve" if tt_ % 2 == 0 else "act", dstT[:, :, tt_ * 128:(tt_ + 1) * 128], pT_.rearrange("p (e l) -> p e l", e=8), [pk], [(dstT, tt_)])

    PHASE_MARKS.append(("f", len(S.nodes)))
    if not ({"stop_ssd", "stop_lru", "stop_o"} & dbg):
        x1T = y_ssdT
        hidT = [S.sb("hidT%d" % i, [128, 4, S_LEN], BF16, 118 * KB + i * 16 * KB) for i in range(2)]
        Wc = []
        for i, base_ in ((2, 150 * KB), (0, 4 * KB), (1, 166 * KB)):
            Wc.append((S.sb("W1c%d" % i, [128, 8, 512], BF16, base_),
                       S.sb("W2c%d" % i, [128, 4, D], BF16, base_ + 8 * KB)))
        o = 182 * KB
        rtmp = [S.sb("rtmp%d" % i, [128, 512], BF16, o + i * KB) for i in range(2)]
        o += 2 * KB
        assert o <= 187 * KB, o
        NFC = 8
        w1v = w_ff1_d.rearrange("(k p) n -> p k n", p=128)
        w2v = w_ff2_d.rearrange("(k p) n -> p k n", p=128)

        def load_wc(fc):
            w1, w2 = Wc[fc % 3]
            dma("pool", w1[:], w1v[:, :, fc * 512:(fc + 1) * 512], [], [w1], "w1c%d" % (fc % 3))
            dma("pool", w2[:], w2v[:, fc * 4:(fc + 1) * 4, :], [], [w2], "w2c%d" % (fc % 3))

        load_wc(0)
        load_wc(1)
        load_wc(2)

        def ff1(fc, tq):
            w1, _ = Wc[fc % 3]
            hd = hidT[fc % 2]
            for f in range(4):
                n = tq * 4 + f
                pb, pc = bank(n % 4)
                pk = (pb, (n % 4) % 2)
                for k in range(8):
                    mm(pb[:, pc:pc + 512], w1[:, k, f * 128:(f + 1) * 128], x1T[:, k, tq * 512:(tq + 1) * 512], k == 0, k == 7,
                       [w1] + [(x1T, tq * 4 + q) for q in range(4)], [pk])
                rt = rtmp[n % 2]
                act(rt[:], pb[:, pc:pc + 512], AF.Relu, [pk], [rt])
                tt("dve", hd[:, f, tq * 512:(tq + 1) * 512], rt[:], rt[:], ALU.mult, [rt], [(hd, tq)])

        def ff2(fc, tq):
            _, w2 = Wc[fc % 3]
            hd = hidT[fc % 2]
            for t4 in range(4):
                tt_ = tq * 4 + t4
                pa = PS[2 + (tt_ % 2)]
                for hf in range(2):
                    for f in range(4):
                        mm(pa[:, hf * 512:(hf + 1) * 512], hd[:, f, tt_ * 128:(tt_ + 1) * 128], w2[:, f, hf * 512:(hf + 1) * 512],
                           f == 0, f == 3, [(hd, tq), w2], [(pa, hf)])
                xa, xk = x1h[tt_ // 8][:, tt_ % 8, :], (x1h[tt_ // 8], tt_ % 8)
                if fc == 0:
                    stt(xa, xa, ALPHA, pa[:], ALU.mult, ALU.add, [xk, pa], [xk])
                else:
                    tt("dve", xa, xa, pa[:], ALU.add, [xk, pa], [xk])

        seq = []
        for fc in range(NFC):
            for tq in range(4):
                seq.append(("1", fc, tq))
        n_steps = len(seq)
        for i in range(n_steps + 2):
            if i < n_steps:
                _, fc, tq = seq[i]
                ff1(fc, tq)
            if i >= 2:
                _, fc, tq = seq[i - 2]
                ff2(fc, tq)
                if tq == 3 and fc + 3 < NFC:
                    load_wc(fc + 3)

        if "x2" in dbg:
            dt_ = dbg_tensor("x2", [S_LEN, D])
            for hh in range(2):
                dma("sp", dt_[hh * 1024:(hh + 1) * 1024, :].rearrange("(t p) d -> p t d", p=128), x1h[hh][:], [x1h[hh]], [], "dbg")
                final_tokens.append("dbg")

    def ln_group(tiles, grow, brow, gw, affine=True):
        stats, mv, xhs = gw["stats"], gw["mv"], gw["xh"]
        for j, tt_ in enumerate(tiles):
            xa, xk = x1h[tt_ // 8][:, tt_ % 8, :], (x1h[tt_ // 8], tt_ % 8)
            for q in range(2):
                S.op("dve", lambda e, j=j, q=q, xa=xa: e.bn_stats(out=stats[:, j, q * 6:(q + 1) * 6], in_=xa[:, q * 512:(q + 1) * 512]),
                     reads=[xk], writes=[(stats, (j, q))], cost=0.65)
            S.op("dve", lambda e, j=j: e.bn_aggr(out=mv[:, j, 0:2], in_=stats[:, j, :]), reads=[(stats, (j, 0)), (stats, (j, 1))],
                 writes=[(mv, (j, "a"))], cost=0.25)
            ts("pool", mv[:, j, 2:3], mv[:, j, 1:2], LN_EPS, 1.0, ALU.add, ALU.mult, [(mv, (j, "a"))], [(mv, (j, "b"))])
            tt("pool", mv[:, j, 2:3], mv[:, j, 2:3], mhalf[:, 0:1], ALU.pow, [(mv, (j, "b")), mhalf], [(mv, (j, "b"))])
            ts("dve", mv[:, j, 3:4], mv[:, j, 0:1], mv[:, j, 2:3], -1.0, ALU.mult, ALU.mult, [(mv, (j, "a")), (mv, (j, "b"))], [(mv, (j, "c"))])
            act(xhs[j][:], xa, AF.Identity, [xk, (mv, (j, "b")), (mv, (j, "c"))], [xhs[j]], bias=mv[:, j, 3:4], scale=mv[:, j, 2:3])
            if affine:
                tt("dve", xhs[j][:], xhs[j][:], grow[:], ALU.mult, [xhs[j], grow], [xhs[j]])
                tt("dve", xa, xhs[j][:], brow[:], ALU.add, [xhs[j], brow], [xk])

    def ln_group_work(o):
        gw = {}
        gw["stats"] = S.sb("lg_stats", [128, 4, 12], F32, o); o += 192
        gw["mv"] = S.sb("lg_mv", [128, 4, 8], F32, o); o += 128
        gw["xh"] = [S.sb("lg_xh%d" % j, [128, D], F32, o + j * 4 * KB) for j in range(4)]
        o += 16 * KB
        return gw, o

    PHASE_MARKS.append(("g", len(S.nodes)))
    if not ({"stop_ssd", "stop_lru", "stop_o", "stop_f"} & dbg):
        x2T = y_ssdT
        Wg = S.sb("Wg", [128, 8, D], BF16, 187 * KB)
        Wp = S.sb("Wp", [128, 2, D], BF16, 203 * KB)
        pTs = S.sb("pTs", [128, 2, S_LEN], BF16, 118 * KB)
        o = 126 * KB
        castg = [S.sb("castg%d" % i, [128, D], BF16, o + i * 2 * KB) for i in range(4)]
        o += 8 * KB
        gsb = [S.sb("gsb%d" % i, [128, D], F32, o + i * 4 * KB) for i in range(4)]
        o += 16 * KB
        assert o <= 150 * KB
        otb = [S.sb("otb%d" % i, [128, 512], F32, 12 * KB + i * 2 * KB) for i in range(4)]
        o = 4 * KB
        g2 = S.sb("g2", [128, D], F32, o); o += 4 * KB
        b2 = S.sb("b2", [128, D], F32, o); o += 4 * KB
        g3 = b3 = None
        gw2, o = ln_group_work(150 * KB)
        o = (o + 31) // 32 * 32
        gw3, o = ln_group_work(o)
        assert o <= 187 * KB, o
        dma("pool", Wg[:], w_gate_d.rearrange("(k p) n -> p k n", p=128), [], [Wg], "w_g")
        dma("pool", Wp[:], w_ple_d.rearrange("(k p) n -> p k n", p=128), [], [Wp], "w_p")
        for hh in range(2):
            dma("pool", pTs[:, :, hh * 1024:(hh + 1) * 1024], pT_d.rearrange("(k p) t -> p k t", p=128)[:, :, hh * 1024:(hh + 1) * 1024],
                [], [(pTs, hh)], "pTs%d" % hh)
        for r_, t_, n_ in ((4, g2, "c_g2"), (5, b2, "c_b2")):
            dma("sp", t_[:], rowpar_d[r_], [], [t_], n_)

        def stP(grp):
            tiles = list(range(4 * grp, 4 * grp + 4))
            ln_group(tiles, g2, b2, gw2)
            for j, tt_ in enumerate(tiles):
                cp("act", castg[j][:], x1h[tt_ // 8][:, tt_ % 8, :], [(x1h[tt_ // 8], tt_ % 8)], [castg[j]])
            for j, tt_ in enumerate(tiles):
                pb, pc = bank(tt_ % 2)
                pk = (pb, tt_ % 2)
                pT_ = pb[:, pc:pc + 512].bitcast(BF16)
                for e_ in range(8):
                    tr(pT_[:, e_ * 128:(e_ + 1) * 128], castg[j][:, e_ * 128:(e_ + 1) * 128], ident_bf[:], [castg[j], ident_bf], [pk])
                cp("act", x2T[:, :, tt_ * 128:(tt_ + 1) * 128], pT_.rearrange("p (e l) -> p e l", e=8), [pk], [(x2T, tt_)])

        def stQ(grp):
            tiles = list(range(4 * grp, 4 * grp + 4))
            for j, tt_ in enumerate(tiles):
                pg = PS[1 + tt_ % 2]
                pp = PS[3]
                for hf in range(2):
                    for k in range(8):
                        mm(pg[:, hf * 512:(hf + 1) * 512], x2T[:, k, tt_ * 128:(tt_ + 1) * 128], Wg[:, k, hf * 512:(hf + 1) * 512],
                           k == 0, k == 7, [(x2T, tt_), Wg], [(pg, hf)])
                for hf in range(2):
                    for k in range(2):
                        mm(pp[:, hf * 512:(hf + 1) * 512], pTs[:, k, tt_ * 128:(tt_ + 1) * 128], Wp[:, k, hf * 512:(hf + 1) * 512],
                           k == 0, k == 1, [(pTs, tt_ // 8), Wp], [(pp, hf)])
                gs = gsb[j]
                act(gs[:], pg[:], AF.Sigmoid, [pg], [gs])
                tt("dve", gs[:], gs[:], pp[:], ALU.mult, [gs, pp], [gs])
            for j, tt_ in enumerate(tiles):
                xa, xk = x1h[tt_ // 8][:, tt_ % 8, :], (x1h[tt_ // 8], tt_ % 8)
                stt(xa, xa, ALPHA, gsb[j][:], ALU.mult, ALU.add, [xk, gsb[j]], [xk])
            ln_group(tiles, g3, b3, gw3, affine=False)
            for e_ in range(8):
                pb, pc = bank(e_ % 2)
                pk = (pb, e_ % 2)
                for j in range(4):
                    tr(pb[:, pc + j * 128:pc + (j + 1) * 128], gw3["xh"][j][:, e_ * 128:(e_ + 1) * 128], ident_f[:], [gw3["xh"][j], ident_f], [pk])
                ot = otb[(grp * 8 + e_) % 4]
                act(ot[:], pb[:, pc:pc + 512], AF.Identity, [pk, colpar], [ot], bias=col(162 + e_), scale=col(154 + e_))
                osn = "out%d" % ((grp * 8 + e_) % 4)
                dma("sp", out_d[e_ * 128:(e_ + 1) * 128, grp * 512:(grp + 1) * 512], ot[:], [ot], [], osn)
                final_tokens.append(osn)

        for i in range(5):
            if i < 4:
                stP(i)
            if i >= 1:
                stQ(i - 1)

    S.finalize(reorder=REORDER)
    finals = sorted(set(final_tokens))
    with nc.Block() as block:
        @block.sync
        def _(e):
            S.emit("sp", e, finals)

        @block.tensor
        def _(e):
            S.emit("pe", e)

        @block.scalar
        def _(e):
            S.emit("act", e)

        @block.vector
        def _(e):
            S.emit("dve", e)

        @block.gpsimd
        def _(e):
            S.emit("pool", e)
    return nc, dbg_out


def _host_inputs(inp):
    f = lambda k: np.asarray(inp[k], dtype=np.float32)
    w_in = f("w_in")[0]
    sh = {}
    sh["w_lru"] = np.ascontiguousarray(w_in[:, 0:2048])
    sh["w_z"] = np.ascontiguousarray(w_in[:, 2048:3072])
    sh["w_xbc"] = np.ascontiguousarray(w_in[:, 3072:5120])
    wdt = np.zeros((D, 80), np.float32)
    for r0 in (0, 32, 64):
        wdt[:, r0:r0 + 16] = w_in[:, 5120:5136]
    sh["w_dt"] = wdt
    gbd = np.zeros((16, 128, 128), np.float32)
    for gi, key in enumerate(("lru_gate_a_w", "lru_gate_x_w")):
        w = f(key)[0]
        for c in range(8):
            for j in range(2):
                gbd[gi * 8 + c, j * 64:(j + 1) * 64, j * 64:(j + 1) * 64] = w[2 * c + j]
    sh["gate_bd"] = gbd
    lcw_ = f("lru_conv_w")[0]
    cdg = np.zeros((128, 32, 128), np.float32)
    ii_ = np.arange(128)
    for c in range(8):
        for k in range(4):
            cdg[ii_, c * 4 + k, ii_] = lcw_[k, c * 128:(c + 1) * 128]
    sh["lru_cdiag"] = cdg.reshape(128, 32 * 128)
    sh["w_out"] = f("w_out")[0]
    sh["w_ff1"] = f("w_ff1")[0]
    sh["w_ff2"] = f("w_ff2")[0]
    sh["w_gate"] = f("w_ple_gate")[0]
    sh["w_ple"] = f("w_ple")[0]
    cpar = np.zeros((128, 176), np.float32)
    tile_cols = lambda v, n: v.reshape(n, 128).T
    lcw = f("lru_conv_w")[0]
    for k in range(4):
        cpar[:, 0 + np.arange(8) * 4 + k] = tile_cols(lcw[k], 8)
    cpar[:, 32:40] = tile_cols(f("lru_conv_b")[0], 8)
    cpar[:, 40:48] = tile_cols(f("lru_gate_a_b")[0].reshape(-1), 8)
    cpar[:, 48:56] = tile_cols(f("lru_gate_x_b")[0].reshape(-1), 8)
    cpar[:, 56:64] = tile_cols(f("lru_a_param")[0], 8)
    scw = f("ssd_conv_w")[0]
    for k in range(4):
        cpar[:, 64 + np.arange(16) * 4 + k] = tile_cols(scw[k], 16)
    cpar[:, 128:144] = tile_cols(f("ssd_conv_b")[0], 16)
    for r0 in (0, 32, 64):
        cpar[r0:r0 + 16, 144] = f("ssd_dt_bias")[0]
        cpar[r0:r0 + 16, 145] = f("ssd_a_log")[0]
    cpar[:, 146:154] = tile_cols(f("ssd_norm_w")[0], 8)
    cpar[:, 154:162] = tile_cols(f("ln3_g")[0], 8)
    cpar[:, 162:170] = tile_cols(f("ln3_b")[0], 8)
    sh["colpar"] = cpar
    rows = np.stack([f("ssd_norm_w")[0], np.repeat(f("ssd_d")[0], 64), f("ln1_g")[0], f("ln1_b")[0],
                     f("ln2_g")[0], f("ln2_b")[0], f("ln3_g")[0], f("ln3_b")[0]], 0)
    sh["rowpar"] = np.ascontiguousarray(np.broadcast_to(rows[:, None, :], (8, 128, D)))
    sh["ident_bf"] = np.eye(128, dtype=np.float32).astype(ml_dtypes.bfloat16)
    sh["ident_f"] = np.eye(128, dtype=np.float32)
    s_i = np.arange(128)[:, None]
    l_i = np.arange(128)[None, :]
    neg = np.where(l_i >= s_i, 0.0, NEG).astype(np.float32)
    sh["negmask"] = np.tile(neg, (1, 4)).astype(ml_dtypes.bfloat16)
    lconst = np.zeros((48, 16, 128), np.float32)
    for h in range(16):
        lconst[32 + h, h, :] = 1.0
    sh["lconst"] = lconst.reshape(48, 16 * 128)
    rconst = np.zeros((48, 512), np.float32)
    rconst[0:16, :] = 1.0
    sh["rconst"] = rconst
    sh["nege"] = -np.eye(16, dtype=np.float32)
    x = f("x")
    p = f("p")[0]
    per_core = []
    for b in range(8):
        m = dict(sh)
        m["xT"] = np.ascontiguousarray(x[b].T)
        m["xtok"] = np.ascontiguousarray(x[b])
        m["pT"] = np.ascontiguousarray(p[b].T)
        per_core.append(m)
    return per_core


def kernel(**inputs):
    nc, _ = build_program()
    in_maps = _host_inputs(inputs)
    res = run_bass_kernel_spmd(nc, in_maps, core_ids=list(range(8)))
    out = np.stack([np.ascontiguousarray(np.asarray(r["out"], dtype=np.float32).T) for r in res.results], 0)
    return out
```

```python
import numpy as np
from contextlib import ExitStack
import ml_dtypes
import concourse.bass as bass
import concourse.mybir as mybir
from concourse.bass_utils import run_bass_kernel_spmd

F32 = mybir.dt.float32
BF16 = mybir.dt.bfloat16
AF = mybir.ActivationFunctionType
ALU = mybir.AluOpType

S_LEN = 2048
D = 1024
TB = 512
NB = S_LEN // TB
CH = 128
ALPHA = 2.0 ** 0.25
LN_EPS = 1e-5
RMS_EPS = 1e-5
SB_BASE = 16512
SB_TOP = 229344
NEG = -30000.0
REORDER = True
PRIO_BLEV = True
PRIO_NOISE = 0.0
PRIO_SEED = 0
PYT_BANK = 6
LRU_PX = [0, 1]
PE_CONV_SET = (1, 3, 5, 7)
LRU_PRI = [2]
LRU_PCV = [4]
LRU_PG = [5, 6, 7]
PHASE_MARKS = []
ACT_GRP_WINDOW = 3000
ACT_SWITCH_US = 0.0
ACT_CHAIN_LRU = True
_ACT_GRP = {AF.Exp: "explog", AF.Ln: "explog", AF.Sigmoid: "sig", AF.Tanh: "gelu", AF.Silu: "silu", AF.Gelu_apprx_tanh: "gelu"}

_DS = {F32: 4, BF16: 2}


class Buf:
    __slots__ = ("name", "t", "off", "size", "w", "r", "inh", "live", "inh_owner")

    def __init__(self, name, t, off, size):
        self.name, self.t, self.off, self.size = name, t, off, size
        self.w = {}
        self.r = {}
        self.inh = []
        self.inh_owner = None
        self.live = True

    def __getitem__(self, k):
        return self.t[k]


class Sched:
    ENG = ("pe", "act", "dve", "pool", "sp")
    LAT = 0.25

    def __init__(self, nc):
        self.nc = nc
        self.nodes = []
        self.sem = {}
        self.dma_sems = {}
        self.dma_last = {}
        self.bufs = []
        self.names = {}
        for e in ("pe", "act", "dve", "pool"):
            self.sem[e] = nc.alloc_semaphore("sem_" + e)

    def _uname(self, name):
        n = self.names.get(name, 0)
        self.names[name] = n + 1
        return name if n == 0 else "%s_%d" % (name, n)

    def sb(self, name, shape, dtype, off):
        size = int(np.prod(shape[1:])) * _DS[dtype]
        assert off % 32 == 0, (name, off)
        assert SB_BASE + off + size <= SB_TOP, (name, off, size)
        t = self.nc.alloc_sbuf_tensor_at(self._uname(name), list(shape), dtype, offset=SB_BASE + off)
        b = Buf(name, t, off, size)
        inh = set()
        for o in self.bufs:
            if o.off < off + size and off < o.off + o.size:
                o.live = False
                inh.update(o.inh)
                if o.inh_owner is not None:
                    inh.add(o.inh_owner)
                inh.update(o.w.values())
                for l in o.r.values():
                    inh.update(l)
        b.inh = sorted(inh)
        self.bufs.append(b)
        return b

    def ps(self, name, shape, dtype=F32):
        t = self.nc.alloc_psum_tensor(self._uname(name), list(shape), dtype)
        return Buf(name, t, -1, 0)

    def dma_sem(self, name):
        if name not in self.dma_sems:
            self.dma_sems[name] = self.nc.alloc_semaphore("d_" + name)
        return self.dma_sems[name]

    @staticmethod
    def _key(k):
        return k if isinstance(k, tuple) else (k, None)

    def op(self, eng, fn, reads=(), writes=(), dma=None, cost=1.0, grp=None):
        idx = len(self.nodes)
        preds = {}

        def add(p, raw):
            if p is None or p == idx:
                return
            if raw or p not in preds:
                preds[p] = preds.get(p, False) or raw

        def take_inh(b):
            if b.inh:
                for p in b.inh:
                    add(p, True)
                b.inh = []
                b.inh_owner = idx
            elif b.inh_owner is not None:
                add(b.inh_owner, True)

        for k in reads:
            b, sub = self._key(k)
            take_inh(b)
            if sub is None:
                for p in b.w.values():
                    add(p, True)
            else:
                add(b.w.get(sub), True)
                add(b.w.get(None), True)
        for k in writes:
            b, sub = self._key(k)
            take_inh(b)
            subs = set(b.w.keys()) | set(b.r.keys()) if sub is None else (sub, None)
            for kk in subs:
                add(b.w.get(kk), False)
                for p in b.r.get(kk, ()):
                    add(p, False)
        if dma is not None:
            self.dma_sem(dma)
            add(self.dma_last.get(dma), False)
            self.dma_last[dma] = idx
        self.nodes.append(dict(eng=eng, fn=fn, dma=dma, preds=preds, cost=cost, grp=grp, after=set()))
        for k in reads:
            b, sub = self._key(k)
            b.r.setdefault(sub, []).append(idx)
        for k in writes:
            b, sub = self._key(k)
            if sub is None:
                b.w = {None: idx}
                b.r = {}
            else:
                b.w[sub] = idx
                b.r[sub] = []
        return idx

    def order_after(self, first, second):
        if first is not None and second is not None and first != second:
            self.nodes[second]["after"].add(first)

    def schedule(self, reorder=True):
        import heapq
        n = len(self.nodes)
        succ = [[] for _ in range(n)]
        npred = [0] * n
        for i, nd in enumerate(self.nodes):
            allp = set(nd["preds"].keys()) | nd["after"]
            npred[i] = len(allp)
            for p in allp:
                succ[p].append(i)
        order = {e: [] for e in self.ENG}
        if not reorder:
            for i, nd in enumerate(self.nodes):
                order[nd["eng"]].append(i)
            return order
        finish = [0.0] * n
        blev = [0.0] * n
        for i in range(n - 1, -1, -1):
            m = 0.0
            for j in succ[i]:
                if blev[j] > m:
                    m = blev[j]
            blev[i] = m + self.nodes[i]["cost"]
        if PRIO_NOISE > 0.0:
            rs = np.random.RandomState(PRIO_SEED)
            nz = rs.standard_normal(n)
            blev = [b_ * (1.0 + PRIO_NOISE * z_) for b_, z_ in zip(blev, nz)]
        self.last_grp = None
        ready = {e: [] for e in self.ENG}
        avail = {e: [] for e in self.ENG}
        free = {e: 0.0 for e in self.ENG}
        for i in range(n):
            if npred[i] == 0:
                heapq.heappush(ready[self.nodes[i]["eng"]], (0.0, i))
        done = 0
        while done < n:
            best = None
            for e in self.ENG:
                while ready[e] and ready[e][0][0] <= free[e]:
                    j_ = heapq.heappop(ready[e])[1]
                    heapq.heappush(avail[e], (-blev[j_] if PRIO_BLEV else 0.0, j_))
                if avail[e]:
                    st, i = free[e], avail[e][0][1]
                    if e == "act" and self.nodes[i].get("grp") not in (None, self.last_grp):
                        cand = [j for _, j in sorted(avail[e]) if self.nodes[j].get("grp") in (None, self.last_grp)]
                        if cand:
                            i = cand[0]
                        else:
                            soon = [(dr_, j) for dr_, j in ready[e] if dr_ <= free[e] + ACT_SWITCH_US and self.nodes[j].get("grp") in (None, self.last_grp)]
                            if soon:
                                st, i = min(soon)
                elif ready[e]:
                    st, i = ready[e][0]
                else:
                    continue
                if best is None or (st, -blev[i], i) < (best[0], -blev[best[1]], best[1]):
                    best = (st, i, e)
            st, i, e = best
            if avail[e] and any(j == i for _, j in avail[e]):
                if avail[e][0][1] == i:
                    heapq.heappop(avail[e])
                else:
                    avail[e] = [(k_, j) for k_, j in avail[e] if j != i]
                    heapq.heapify(avail[e])
            elif ready[e][0][1] == i:
                heapq.heappop(ready[e])
            else:
                ready[e] = [(k_, j) for k_, j in ready[e] if j != i]
                heapq.heapify(ready[e])
            if e == "act" and self.nodes[i].get("grp") is not None:
                self.last_grp = self.nodes[i]["grp"]
            nd = self.nodes[i]
            issue = nd["cost"] if nd["dma"] is None else 0.15
            free[e] = st + issue
            finish[i] = st + nd["cost"]
            order[e].append(i)
            done += 1
            for j in succ[i]:
                npred[j] -= 1
                if npred[j] == 0:
                    ej = self.nodes[j]["eng"]
                    dr = 0.0
                    for p in self.nodes[j]["preds"]:
                        f = finish[p] + (self.LAT if (self.nodes[p]["eng"] != ej or self.nodes[p]["dma"]) else 0.0)
                        if f > dr:
                            dr = f
                    if self.nodes[j]["dma"] is not None and ej == "pool" and dr > 0.0:
                        dr += 4.0
                    for p in self.nodes[j]["after"]:
                        f = finish[p] - self.nodes[p]["cost"] + 0.15
                        if f > dr:
                            dr = f
                    heapq.heappush(ready[ej], (dr, j))
        self.model_time = max(finish) if finish else 0.0
        return order

    def finalize(self, reorder=True):
        order = self.schedule(reorder)
        pos = {}
        dma_cum = {}
        tok = [None] * len(self.nodes)
        for e in self.ENG:
            c = 0
            for i in order[e]:
                nd = self.nodes[i]
                if nd["dma"] is None:
                    c += 1
                    tok[i] = (self.sem[e], c)
                pos[i] = None
        for e in self.ENG:
            for i in order[e]:
                nd = self.nodes[i]
                if nd["dma"] is not None:
                    dma_cum[nd["dma"]] = dma_cum.get(nd["dma"], 0) + 16
                    tok[i] = (self.dma_sems[nd["dma"]], dma_cum[nd["dma"]])
        self.dma_total = dma_cum
        streams = {}
        for e in self.ENG:
            waited = {}
            lst = []
            for i in order[e]:
                nd = self.nodes[i]
                waits = []
                for p, raw in nd["preds"].items():
                    pn = self.nodes[p]
                    if pn["dma"] is None and pn["eng"] == e and e == "pe":
                        continue
                    sem, val = tok[p]
                    if waited.get(sem, 0) >= val:
                        continue
                    waited[sem] = val
                    waits.append((sem, val))
                inc = (tok[i][0], 16 if nd["dma"] is not None else 1)
                lst.append((waits, nd["fn"], inc))
            streams[e] = lst
        self.streams = streams
        return streams

    def emit(self, eng, e, final=()):
        for waits, fn, inc in self.streams[eng]:
            for sem, val in waits:
                e.wait_ge(sem, val)
            fn(e).then_inc(inc[0], inc[1])
        for name in final:
            e.wait_ge(self.dma_sems[name], self.dma_total[name])


def build_program(dbg=()):
    dbg = set(dbg)
    nc = bass.Bass("TRN2", target_bir_lowering=False)
    S = Sched(nc)

    def din(name, shape, dt=F32):
        return nc.dram_tensor(name, list(shape), dt, kind="ExternalInput").ap()

    xT_d = din("xT", [D, S_LEN])
    xtok_d = din("xtok", [S_LEN, D])
    pT_d = din("pT", [256, S_LEN])
    w_z_d = din("w_z", [D, 1024])
    w_xbc_d = din("w_xbc", [D, 2048])
    w_dt_d = din("w_dt", [D, 80])
    w_lru_d = din("w_lru", [D, 2048])
    gate_bd_d = din("gate_bd", [16, 128, 128])
    lru_cdiag_d = din("lru_cdiag", [128, 32 * 128])
    w_out_d = din("w_out", [2048, D])
    w_ff1_d = din("w_ff1", [D, 4096])
    w_ff2_d = din("w_ff2", [4096, D])
    w_gate_d = din("w_gate", [D, D])
    w_ple_d = din("w_ple", [256, D])
    colpar_d = din("colpar", [128, 176])
    rowpar_d = din("rowpar", [8, 128, D])
    ident_bf_d = din("ident_bf", [128, 128], BF16)
    ident_f_d = din("ident_f", [128, 128])
    negmask_d = din("negmask", [128, 512], BF16)
    lconst_d = din("lconst", [48, 16 * 128])
    rconst_d = din("rconst", [48, 512])
    nege_d = din("nege", [16, 16])
    out_d = nc.dram_tensor("out", [D, S_LEN], F32, kind="ExternalOutput").ap()
    dbg_out = {}

    def dbg_tensor(name, shape):
        dbg_out[name] = nc.dram_tensor("dbg_" + name, list(shape), F32, kind="ExternalOutput").ap()
        return dbg_out[name]

    KB = 1024
    ident_bf = S.sb("ident_bf", [128, 128], BF16, 0)
    ident_f = S.sb("ident_f", [128, 128], F32, 256)
    negmask = S.sb("negmask", [128, 512], BF16, 768)
    colpar = S.sb("colpar", [128, 176], F32, 1792)
    derived = S.sb("derived", [128, 32], F32, 2496)
    nege = S.sb("nege", [16, 16], F32, 2624)
    pose = S.sb("pose", [16, 16], F32, 2688)
    ones_c = S.sb("ones_c", [128, 128], F32, 2752)
    mhalf = S.sb("mhalf", [128, 8], F32, 3264)
    derived2 = S.sb("derived2", [128, 32], F32, 3328)
    P0 = 4 * KB

    CW_L, CB_L, GAB, GXB, APAR, CW_S, CB_S, DTB, ALOG = 0, 32, 40, 48, 56, 64, 128, 144, 145

    def col(i):
        return colpar[:, i:i + 1]

    PS = [S.ps("psum%d" % i, [128, 1024], F32) for i in range(4)]

    def bank(i):
        return PS[i // 2], (i % 2) * 512

    def fsz(ap):
        n = 1
        for d in ap.shape[1:]:
            n *= int(d)
        return n

    def dma(eng, out_ap, in_ap, reads, writes, sem):
        nbytes = fsz(in_ap) * int(in_ap.shape[0]) * _DS.get(in_ap.dtype, 4)
        return S.op(eng, lambda e: e.dma_start(out=out_ap, in_=in_ap), reads=reads, writes=writes, dma=sem,
                    cost=2.0 + nbytes / 150e3)

    def mm(out_ap, lhsT, rhs, start, stop, reads, writes):
        n = fsz(rhs)
        c = (0.06 + n / 2600.0) if rhs.dtype != F32 else (0.06 + n * 4 / 2600.0)
        return S.op("pe", lambda e: e.matmul(out_ap, lhsT=lhsT, rhs=rhs, start=start, stop=stop),
                    reads=reads, writes=writes, cost=max(c, 0.1))

    def tr(out_ap, in_ap, ident_ap, reads, writes):
        return S.op("pe", lambda e: e.transpose(out_ap, in_ap, ident_ap), reads=reads, writes=writes, cost=0.12)

    act_chain = [None, False]

    def act(out_ap, in_ap, func, reads, writes, bias=None, scale=1.0, accum=None):
        def f(e):
            kw = {}
            if bias is not None:
                kw["bias"] = bias
            if accum is not None:
                kw["accum_out"] = accum
            return e.activation(out=out_ap, in_=in_ap, func=func, scale=scale, **kw)
        idx = S.op("act", f, reads=reads, writes=writes, cost=0.2 + fsz(in_ap) / 1150.0, grp=_ACT_GRP.get(func))
        if act_chain[1] and _ACT_GRP.get(func) is not None:
            S.order_after(act_chain[0], idx)
            act_chain[0] = idx
        return idx

    def vcost(eng, ap):
        n = fsz(ap)
        return (0.08 + n / 900.0) if eng == "dve" else (0.12 + n / 460.0)

    def tt(eng, out_ap, in0, in1, op, reads, writes):
        return S.op(eng, lambda e: e.tensor_tensor(out=out_ap, in0=in0, in1=in1, op=op), reads=reads, writes=writes,
                    cost=vcost(eng, out_ap))

    def ts(eng, out_ap, in0, s1, s2, op0, op1, reads, writes):
        if s2 is None:
            return S.op(eng, lambda e: e.tensor_scalar(out=out_ap, in0=in0, scalar1=s1, scalar2=None, op0=op0),
                        reads=reads, writes=writes, cost=vcost(eng, out_ap))
        return S.op(eng, lambda e: e.tensor_scalar(out=out_ap, in0=in0, scalar1=s1, scalar2=s2, op0=op0, op1=op1),
                    reads=reads, writes=writes, cost=vcost(eng, out_ap))

    def stt(out_ap, in0, scalar, in1, op0, op1, reads, writes):
        return S.op("dve", lambda e: e.scalar_tensor_tensor(out=out_ap, in0=in0, scalar=scalar, in1=in1, op0=op0, op1=op1),
                    reads=reads, writes=writes, cost=0.2 + fsz(out_ap) / 900.0)

    def cp(eng, out_ap, in_ap, reads, writes):
        if eng == "act":
            return act(out_ap, in_ap, AF.Copy, reads, writes)
        return S.op(eng, lambda e: e.tensor_copy(out=out_ap, in_=in_ap), reads=reads, writes=writes, cost=vcost(eng, out_ap))

    def memset(eng, ap, val, writes):
        return S.op(eng, lambda e: e.memset(ap, val), writes=writes, cost=vcost(eng, ap))

    dma_chain = [None]

    def chain(idx):
        S.order_after(dma_chain[0], idx)
        dma_chain[0] = idx
        return idx

    def wload(dst, src_d, ktiles, ncols, key_fn, sem_fn, c0=0, dst_c0=0, piece=1024):
        v = src_d.rearrange("(k p) n -> p k n", p=128)
        for a in range(0, ncols, piece):
            n = min(piece, ncols - a)
            chain(dma("pool", dst[:, 0:ktiles, dst_c0 + a:dst_c0 + a + n], v[:, 0:ktiles, c0 + a:c0 + a + n],
                      [], [key_fn(a)], sem_fn(a)))

    dma("sp", ident_bf[:], ident_bf_d, [], [ident_bf], "c_idb")
    dma("sp", ident_f[:], ident_f_d, [], [ident_f], "c_idf")
    dma("sp", negmask[:], negmask_d, [], [negmask], "c_neg")
    dma("sp", colpar[:], colpar_d, [], [colpar], "c_col")
    dma("sp", nege[:], nege_d, [], [nege], "c_nege")
    memset("pool", ones_c[:], 1.0, [ones_c])
    memset("pool", mhalf[:], -0.5, [mhalf])
    ts("pool", pose[:], nege[:], -1.0, None, ALU.mult, None, [nege], [pose])
    act(derived[:, 0:8], colpar[:, APAR:APAR + 8], AF.Exp, [colpar], [(derived, "sc")], scale=-1.0)
    act(derived[:, 0:8], derived[:, 0:8], AF.Ln, [(derived, "sc")], [(derived, "sc")], bias=1.0)
    ts("dve", derived[:, 8:16], derived[:, 0:8], 8.0, None, ALU.mult, None, [(derived, "sc")], [(derived, "nsc")])
    ts("dve", derived[:, 0:8], derived[:, 0:8], -8.0, None, ALU.mult, None, [(derived, "sc"), (derived, "nsc")], [(derived, "sc")])
    ts("dve", derived2[:, 0:8], derived[:, 0:8], 0.5, None, ALU.mult, None, [(derived, "sc")], [(derived2, 0)])
    ts("dve", derived2[:, 8:16], derived[:, 8:16], 0.5, None, ALU.mult, None, [(derived, "nsc")], [(derived2, 1)])
    ts("dve", derived2[:, 16:24], colpar[:, GAB:GAB + 8], 0.5, None, ALU.mult, None, [colpar], [(derived2, 2)])
    ts("dve", derived2[:, 24:32], colpar[:, GXB:GXB + 8], 0.5, None, ALU.mult, None, [colpar], [(derived2, 3)])
    act(derived[0:80, 16:17], colpar[0:80, ALOG:ALOG + 1], AF.Exp, [colpar], [(derived, "A")])
    ts("dve", derived[0:80, 16:17], derived[0:80, 16:17], -1.0, None, ALU.mult, None, [(derived, "A")], [(derived, "A")])

    R1 = P0
    Wz = S.sb("Wz", [128, 8, 1024], BF16, R1)
    Wxbc = S.sb("Wxbc", [128, 8, 2048], BF16, R1 + 16 * KB)
    Wdt = S.sb("Wdt", [128, 8, 80], BF16, R1 + 48 * KB)
    YS0 = 86 * KB
    y_ssdT = S.sb("y_ssdT", [128, 8, S_LEN], BF16, YS0)
    o = 54 * KB
    xTs = [S.sb("xTs%d" % i, [128, 8, TB], BF16, o + i * 8 * KB) for i in range(2)]
    o += 16 * KB
    szb = S.sb("sz", [128, 4, 1024], BF16, o)
    o += 8 * KB
    xs_sb = [S.sb("xs_sb%d" % i, [128, 1024], BF16, o + i * 2 * KB) for i in range(2)]
    o += 4 * KB
    xdt = [S.sb("xdt%d" % i, [128, 1024], BF16, o + i * 2 * KB) for i in range(2)]
    o += 4 * KB
    assert o <= 86 * KB
    o = 118 * KB
    xpad = [S.sb("xpad%d" % i, [128, TB + 32], F32, o + i * (TB + 32) * 4) for i in range(2)]
    o += 2 * (TB + 32) * 4
    cacc = [S.sb("cacc%d" % i, [128, TB], F32, o + i * 2 * KB) for i in range(2)]
    o += 4 * KB
    stail = S.sb("stail", [128, 16, 4], F32, o)
    o += 256
    xbcT = S.sb("xbcT", [128, 16, TB], BF16, o)
    o += 16 * KB
    dt_e = S.sb("dt_e", [80, TB], F32, o); o += 2 * KB
    dtT = S.sb("dtT", [80, TB], F32, o); o += 2 * KB
    aT = S.sb("aT", [80, TB], F32, o); o += 2 * KB
    acs = S.sb("acs", [80, TB], F32, o); o += 2 * KB
    smallT = S.sb("smallT", [80, TB], F32, o); o += 2 * KB
    ddtmp = dt_e
    ea0 = S.sb("ea0", [16, TB], F32, o); o += 2 * KB
    Rb = S.sb("Rb", [48, TB], F32, o); o += 2 * KB
    Lt = [S.sb("Lt0", [48, 16, 128], F32, o)]
    o += 8 * KB
    tokm = [S.sb("tokm%d" % i, [128, 80], F32, o + i * 320) for i in range(2)]
    o += 640
    diagcd = S.sb("diagcd", [16, 16], F32, o); o += 64
    cdrow = [S.sb("cdrow%d" % i, [128, 16], F32, o + i * 64) for i in range(2)]
    o += 128
    ssq = S.sb("ssq", [128, 8], F32, o); o += 32
    o = (o + 31) // 32 * 32
    xdtd = [S.sb("xdtd%d" % i, [128, 1024], BF16, o + i * 2 * KB) for i in range(2)]
    o += 4 * KB
    Btok = [S.sb("Btok%d" % i, [128, 512], BF16, o + i * KB) for i in range(2)]
    o += 2 * KB
    E4 = [S.sb("E4_%d" % i, [128, 512], F32, o + i * 2 * KB) for i in range(2)]
    o += 4 * KB
    scoresT = S.sb("scoresT", [128, 4, 512], BF16, o); o += 4 * KB
    Sst = S.sb("Sst", [128, 1024], F32, o); o += 4 * KB
    S_bf = [S.sb("S_bf%d" % i, [128, 1024], BF16, o + i * 2 * KB) for i in range(2)]
    o += 4 * KB
    t1 = S.sb("t1", [128, 1024], F32, o); o += 4 * KB
    t2 = S.sb("t2", [128, 1024], F32, o); o += 4 * KB
    ytok = S.sb("ytok", [128, 1024], BF16, o); o += 2 * KB
    Drow = S.sb("Drow", [128, 1024], F32, o); o += 4 * KB
    assert o <= 207 * KB, o

    xT_v = xT_d.rearrange("(k p) t -> p k t", p=128)

    def load_xTs(b):
        return dma("pool", xTs[b % 2][:], xT_v[:, :, b * TB:(b + 1) * TB], [], [xTs[b % 2]], "xTs%d" % (b % 2))

    chain(load_xTs(0))
    wload(Wdt, w_dt_d, 8, 80, lambda a: Wdt, lambda a: "w_dt")
    for a in range(0, 1024, 512):
        wload(Wz, w_z_d, 8, 512, lambda q, a=a: (Wz, a // 512), lambda q, a=a: "w_z%d" % (a // 512), c0=a, dst_c0=a)
    for a in range(0, 2048, 512):
        wload(Wxbc, w_xbc_d, 8, 512, lambda q, a=a: (Wxbc, a // 512), lambda q, a=a: "w_xbc%d" % (a // 512),
              c0=a, dst_c0=a)
    dma("sp", Drow[:], rowpar_d[1], [], [Drow], "c_drow")
    dma("sp", Lt[0][:].rearrange("p h s -> p (h s)"), lconst_d, [], [Lt[0]], "c_lt0")
    dma("sp", Rb[:], rconst_d, [], [Rb], "c_rb")
    memset("pool", stail[:], 0.0, [stail])
    memset("pool", Sst[:], 0.0, [Sst])
    memset("pool", S_bf[0][:], 0.0, [S_bf[0]])
    memset("pool", smallT[:], 0.0, [smallT])

    def bc3(ap2, n_inner):
        return ap2.unsqueeze(2).broadcast_to([ap2.shape[0], ap2.shape[1], n_inner])

    hd3 = lambda ap: ap.rearrange("p (h d) -> p h d", d=64)

    for b in range(NB):
        xs_b = xTs[b % 2]
        if b + 1 < NB:
            load_xTs(b + 1)
        pb, pc = bank(7)
        for k in range(8):
            mm(pb[0:80, pc:pc + TB], Wdt[:, k, :], xs_b[:, k, :], k == 0, k == 7, [Wdt, xs_b], [(pb, 1)])
        act(dt_e[:], pb[0:80, pc:pc + TB], AF.Exp, [(pb, 1), colpar], [dt_e], bias=colpar[0:80, DTB:DTB + 1])
        act(dtT[:], dt_e[:], AF.Ln, [dt_e], [dtT], bias=1.0)
        ts("dve", aT[:], dtT[:], derived[0:80, 16:17], None, ALU.mult, None, [dtT, (derived, "A")], [aT])
        for c in range(4):
            S.op("dve", lambda e, c=c: e.tensor_tensor_scan(out=acs[:, c * CH:(c + 1) * CH], data0=ones_c[0:80, 0:CH],
                                                            data1=aT[:, c * CH:(c + 1) * CH], initial=0.0,
                                                            op0=ALU.mult, op1=ALU.add),
                 reads=[ones_c, aT], writes=[(acs, c)])
        cp("pool", Rb[32:48, :], acs[32:48, :], [acs], [Rb])
        cp("pool", smallT[0:16, :], dtT[0:16, :], [dtT], [(smallT, 0)])
        for c in range(4):
            act(ddtmp[32:48, c * CH:(c + 1) * CH], acs[32:48, c * CH:(c + 1) * CH], AF.Exp, [acs, dtT], [(ddtmp, c)],
                bias=acs[32:48, c * CH + CH - 1:c * CH + CH], scale=-1.0)
        tt("pool", smallT[32:48, :], ddtmp[32:48, :], dtT[32:48, :], ALU.mult, [ddtmp, dtT], [(smallT, 1)])
        act(smallT[64:80, :], acs[64:80, :], AF.Exp, [acs], [(smallT, 2)])
        act(ea0[:], acs[0:16, :], AF.Exp, [acs], [ea0])
        for c in range(4):
            pz = PS[c % 2]
            for hf in range(2):
                for k in range(8):
                    mm(pz[:, hf * 512:(hf + 1) * 512], xs_b[:, k, c * CH:(c + 1) * CH], Wz[:, k, hf * 512:(hf + 1) * 512],
                       k == 0, k == 7, [xs_b, (Wz, hf)], [(pz, hf)])
            act(szb[:, c, :], pz[:], AF.Silu, [pz], [(szb, c)])
        for e_ in range(16):
            pb, pc = bank(4 + (e_ % 2))
            pkey = (pb, (4 + e_ % 2) % 2)
            for k in range(8):
                mm(pb[:, pc:pc + TB], Wxbc[:, k, e_ * 128:(e_ + 1) * 128], xs_b[:, k, :], k == 0, k == 7,
                   [xs_b, (Wxbc, e_ // 4)], [pkey])
            xp = xpad[e_ % 2]
            ca = cacc[e_ % 2]
            cw = CW_S + e_ * 4
            act(xp[:, 4:4 + TB], pb[:, pc:pc + TB], AF.Copy, [pkey], [(xp, "m")])
            act(ca[:], pb[:, pc:pc + TB], AF.Identity, [pkey, colpar], [ca], bias=col(CB_S + e_), scale=col(cw + 3))
            cp("pool", xp[:, 0:4], stail[:, e_, :], [(stail, e_)], [(xp, "t")])
            for kk in range(3):
                stt(ca[:], xp[:, 1 + kk:1 + kk + TB], col(cw + kk), ca[:], ALU.mult, ALU.add, [xp, ca, colpar], [ca])
            cp("pool", stail[:, e_, :], xp[:, TB:TB + 4], [xp], [(stail, e_)])
            act(xbcT[:, e_, :], ca[:], AF.Silu, [ca], [(xbcT, e_)])

        def stA(c):
            ci = b * 4 + c
            c0 = c * CH
            r = ci % 2
            tk = tokm[r]
            pb7, pc7 = bank(7)
            tr(pb7[:, pc7:pc7 + 80], smallT[0:80, c0:c0 + CH], ident_f[0:80, 0:80], [smallT, ident_f], [(pb7, 1)])
            ts("pool", diagcd[:], pose[:], ea0[:, c0 + CH - 1:c0 + CH], 1.0, ALU.mult, ALU.mult, [pose, ea0], [diagcd])
            mm(pb7[:, pc7 + 128:pc7 + 144], Rb[0:16, 0:128], diagcd[:], True, True, [Rb, diagcd], [(pb7, 1)])
            pB = pb7[:, pc7 + 256:pc7 + 512].bitcast(BF16)
            for g in range(4):
                tr(pB[:, g * 128:(g + 1) * 128], xbcT[:, 8 + g, c0:c0 + CH], ident_bf[:], [(xbcT, 8 + g), ident_bf], [(pb7, 1)])
            cp("act", tk[:], pb7[:, pc7:pc7 + 80], [(pb7, 1)], [tk])
            cp("act", cdrow[r][:], pb7[:, pc7 + 128:pc7 + 144], [(pb7, 1)], [cdrow[r]])
            cp("act", Btok[r][:], pB, [(pb7, 1)], [Btok[r]])
            pb6, pc6 = bank(6)
            pxs = pb6[:, pc6:pc6 + 512].bitcast(BF16)
            for e_ in range(8):
                tr(pxs[:, e_ * 128:(e_ + 1) * 128], xbcT[:, e_, c0:c0 + CH], ident_bf[:], [(xbcT, e_), ident_bf], [(pb6, 0)])
            cp("act", xs_sb[r][:], pxs, [(pb6, 0)], [xs_sb[r]])
            tt("dve", hd3(xdt[r][:]), hd3(xs_sb[r][:]), bc3(tk[:, 0:16], 64), ALU.mult, [xs_sb[r], tk], [xdt[r]])
            tt("dve", hd3(xdtd[r][:]), hd3(xs_sb[r][:]), bc3(tk[:, 32:48], 64), ALU.mult, [xs_sb[r], tk], [xdtd[r]])
            pb4, pc4 = bank(4)
            for g in range(4):
                mm(pb4[:, pc4 + g * 128:pc4 + (g + 1) * 128], xbcT[:, 8 + g, c0:c0 + CH], xbcT[:, 12 + g, c0:c0 + CH], True, True,
                   [(xbcT, 8 + g), (xbcT, 12 + g)], [(pb4, 0)])
            tt("pool", Lt[0][0:16, :, :], acs[0:16, c0:c0 + CH].unsqueeze(1).broadcast_to([16, 16, CH]),
               bc3(nege[:], CH), ALU.mult, [(acs, c), nege], [Lt[0]])

        def stB(c):
            c0 = c * CH
            lt = Lt[0]
            pb4, pc4 = bank(4)
            for g in range(4):
                pbs, pcs = bank(5) if g % 2 == 0 else bank(6)
                skey = (pbs, 1) if g % 2 == 0 else (pbs, 0)
                mm(pbs[:, pcs:pcs + 512], ident_bf[:], negmask[:], True, False, [ident_bf, negmask], [skey])
                for j in range(4):
                    h = 4 * g + j
                    mm(pbs[:, pcs + j * 128:pcs + (j + 1) * 128], lt[0:48, h, :], Rb[0:48, c0:c0 + CH], False, j == 3,
                       [lt, Rb], [skey])
                e4 = E4[g % 2]
                act(e4[:], pbs[:, pcs:pcs + 512], AF.Exp, [skey], [e4])
                tt("dve", scoresT[:, g, :].rearrange("p (j l) -> p j l", j=4), e4[:].rearrange("p (j l) -> p j l", j=4),
                   pb4[:, pc4 + g * 128:pc4 + (g + 1) * 128].unsqueeze(1).broadcast_to([128, 4, 128]), ALU.mult,
                   [e4, (pb4, 0)], [(scoresT, g)])

        def stC1(c):
            ci = b * 4 + c
            c0 = c * CH
            r = ci % 2
            tk = tokm[r]
            py = PS[0]
            pyo = PS[1]
            sprev = S_bf[ci % 2]
            snext = S_bf[(ci + 1) % 2]
            for g in range(4):
                mm(pyo[:, g * 256:(g + 1) * 256], xbcT[:, 12 + g, c0:c0 + CH], sprev[:, g * 256:(g + 1) * 256], True, True,
                   [(xbcT, 12 + g), sprev], [(pyo, g // 2)])
            for g in range(4):
                for j in range(4):
                    h = 4 * g + j
                    mm(py[:, h * 64:(h + 1) * 64], scoresT[:, g, j * 128:(j + 1) * 128], xdt[r][:, h * 64:(h + 1) * 64], True, True,
                       [(scoresT, g), xdt[r]], [(py, h // 8)])
            tt("dve", hd3(t1[:]), hd3(pyo[:]), bc3(tk[:, 64:80], 64), ALU.mult, [pyo, tk], [t1])
            pst = PS[1]
            for g in range(4):
                mm(pst[:, g * 256:(g + 1) * 256], Btok[r][:, g * 128:(g + 1) * 128], xdtd[r][:, g * 256:(g + 1) * 256], True, True,
                   [Btok[r], xdtd[r]], [(pst, g // 2)])
            tt("dve", hd3(Sst[:]), hd3(Sst[:]), bc3(cdrow[r][:], 64), ALU.mult, [Sst, cdrow[r]], [Sst])
            tt("dve", Sst[:], Sst[:], pst[:], ALU.add, [Sst, pst], [Sst])
            cp("act", snext[:], Sst[:], [Sst], [snext])
            tt("dve", t1[:], t1[:], py[:], ALU.add, [t1, py], [t1])
            tt("dve", t2[:], xs_sb[r][:], Drow[:], ALU.mult, [xs_sb[r], Drow], [t2])

        def stC2(c):
            ci = b * 4 + c
            tt("dve", t1[:], t1[:], t2[:], ALU.add, [t1, t2], [t1])
            tt("dve", t1[:], t1[:], szb[:, c, :], ALU.mult, [t1, (szb, c)], [t1])
            sq = ssq[:, (ci % 2) * 4:(ci % 2) * 4 + 4]
            for g in range(4):
                act(t2[:, g * 256:(g + 1) * 256], t1[:, g * 256:(g + 1) * 256], AF.Square, [t1], [t2], accum=sq[:, g:g + 1])
            ts("pool", sq, sq, 1.0 / 256.0, RMS_EPS, ALU.mult, ALU.add, [t2], [t2])
            tt("pool", sq, sq, mhalf[:, 0:4], ALU.pow, [t2, mhalf], [t2])
            for g in range(4):
                act(ytok[:, g * 256:(g + 1) * 256], t1[:, g * 256:(g + 1) * 256], AF.Identity, [t1, t2], [ytok], scale=sq[:, g:g + 1])
            pb6, pc6 = bank(PYT_BANK)
            pyT = pb6[:, pc6:pc6 + 512].bitcast(BF16)
            for e_ in range(8):
                tr(pyT[:, e_ * 128:(e_ + 1) * 128], ytok[:, e_ * 128:(e_ + 1) * 128], ident_bf[:], [ytok, ident_bf], [(pb6, PYT_BANK % 2)])
            cp("act", y_ssdT[:, :, ci * CH:(ci + 1) * CH], pyT.rearrange("p (e l) -> p e l", e=8), [(pb6, PYT_BANK % 2)], [(y_ssdT, ci)])

        stA(0); stB(0); stA(1)
        for c in range(4):
            stC1(c)
            if c + 1 < 4:
                stB(c + 1)
            if c + 2 < 4:
                stA(c + 2)
            stC2(c)

    final_tokens = []
    if "y_ssd" in dbg:
        dt_ = dbg_tensor("y_ssd", [D, S_LEN])
        stage = S.sb("dbgstage", [128, 8, S_LEN], F32, 118 * KB)
        cp("dve", stage[:], y_ssdT[:], [y_ssdT], [stage])
        dma("sp", dt_.rearrange("(e p) t -> p e t", p=128), stage[:], [stage], [], "dbg"); final_tokens.append("dbg")

    PHASE_MARKS.append(("lru", len(S.nodes)))
    if "stop_ssd" not in dbg:
        Wlx = S.sb("w_lru_x", [128, 8, 1024], BF16, 4 * KB)
        Wlg = S.sb("w_lru_g", [128, 8, 1024], BF16, 20 * KB)
        y_lruT = S.sb("y_lruT", [128, 8, S_LEN], BF16, 118 * KB)
        xl_f = S.sb("xl_f", [128, 8, TB], F32, 54 * KB)
        ra = S.sb("ra", [128, 8, TB], F32, 70 * KB)
        iu = S.sb("iu", [128, 8, TB], F32, 36 * KB)
        o = 150 * KB
        Tm = S.sb("Tm", [128, 8, TB], F32, o); o += 16 * KB
        xl_b = S.sb("xl_b", [128, 8, TB], BF16, o); o += 8 * KB
        xTl = [S.sb("xTl%d" % i, [128, 8, TB], BF16, o + i * 8 * KB) for i in range(2)]
        o += 16 * KB
        xpl = [S.sb("xpl%d" % i, [128, TB + 32], BF16, o + i * (TB + 32) * 2) for i in range(2)]
        o += 2 * (TB + 32) * 2
        Wga = S.sb("Wga", [128, 16, 128], BF16, o); o += 4 * KB
        cdiag = S.sb("cdiag", [128, 32, 128], BF16, o); o += 8 * KB
        ltail = S.sb("ltail", [128, 8, 4], BF16, o); o += 64
        hcarry = S.sb("hcarry", [128, 8], F32, o); o += 32
        ltail_f = S.sb("ltail_f", [128, 8, 4], F32, o); o += 128
        xplf = S.sb("xplf", [128, TB + 32], F32, o); o += (TB + 32) * 4
        assert o <= SB_TOP - SB_BASE, o
        assert o <= 207 * KB, o

        def load_xTl(b):
            dma("pool", xTl[b % 2][:], xT_v[:, :, b * TB:(b + 1) * TB], [], [xTl[b % 2]], "xTl%d" % (b % 2))

        load_xTl(0)
        for a in range(2):
            wload(Wlx if a == 0 else Wlg, w_lru_d, 8, 1024, lambda q, a=a: (Wlx if a == 0 else Wlg), lambda q, a=a: "w_lru%d" % a, c0=a * 1024, dst_c0=0)
        dma("pool", Wga[:], gate_bd_d.rearrange("g i j -> i g j"), [], [Wga], "w_ga")
        cdv = cdiag[:].rearrange("p g j -> p (g j)")
        for hh in range(2):
            dma("pool", cdv[:, hh * 2048:(hh + 1) * 2048], lru_cdiag_d[:, hh * 2048:(hh + 1) * 2048], [], [(cdiag, hh)], "w_cd%d" % hh)
        memset("pool", ltail[:], 0.0, [ltail])
        memset("pool", ltail_f[:], 0.0, [ltail_f])
        SC, NSC = 0, 8

        act_chain[1] = ACT_CHAIN_LRU
        for b in range(NB):
            xs_b = xTl[b % 2]
            if b + 1 < NB:
                load_xTl(b + 1)

            def st1a(c):
                bx_ = LRU_PX[c % len(LRU_PX)]
                pb, pc = bank(bx_)
                pk = (pb, bx_ % 2)
                for k in range(8):
                    mm(pb[:, pc:pc + TB], Wlx[:, k, c * 128:(c + 1) * 128], xs_b[:, k, :], k == 0, k == 7, [xs_b, Wlx], [pk])
                if c in PE_CONV_SET:
                    xp = xpl[c % 2]
                    act(xp[:, 4:4 + TB], pb[:, pc:pc + TB], AF.Copy, [pk], [(xp, "m")])
                    cp("pool", xp[:, 0:4], ltail[:, c, :], [(ltail, c)], [(xp, "t")])
                    bcv_ = LRU_PCV[c % len(LRU_PCV)]
                    pcb, pcc = bank(bcv_)
                    pck = (pcb, bcv_ % 2)
                    for kk in range(4):
                        mm(pcb[:, pcc:pcc + TB], cdiag[:, c * 4 + kk, :], xp[:, 1 + kk:1 + kk + TB], kk == 0, kk == 3, [(cdiag, (c * 4 + kk) // 16), xp], [pck])
                    cp("pool", ltail[:, c, :], xp[:, TB:TB + 4], [xp], [(ltail, c)])
                    act(xl_f[:, c, :], pcb[:, pcc:pcc + TB], AF.Identity, [pck, colpar], [(xl_f, c)], bias=col(CB_L + c))
                else:
                    xp = xplf
                    cw = CW_L + c * 4
                    act(xp[:, 4:4 + TB], pb[:, pc:pc + TB], AF.Copy, [pk], [(xp, "m")])
                    ts("dve", xl_f[:, c, :], xp[:, 4:4 + TB], col(cw + 3), col(CB_L + c), ALU.mult, ALU.add, [(xp, "m"), colpar], [(xl_f, c)])
                    cp("pool", xp[:, 0:4], ltail_f[:, c, :], [(ltail_f, c)], [(xp, "t")])
                    for kk in range(3):
                        stt(xl_f[:, c, :], xp[:, 1 + kk:1 + kk + TB], col(cw + kk), xl_f[:, c, :], ALU.mult, ALU.add,
                            [xp, (xl_f, c), colpar], [(xl_f, c)])
                    cp("pool", ltail_f[:, c, :], xp[:, TB:TB + 4], [xp], [(ltail_f, c)])
                cp("dve", xl_b[:, c, :], xl_f[:, c, :], [(xl_f, c)], [(xl_b, c)])

            def st1b(c):
                for gi in range(2):
                    pb, pc = bank(LRU_PRI[c % len(LRU_PRI)] + gi)
                    pk = (pb, gi)
                    mm(pb[:, pc:pc + TB], Wga[:, gi * 8 + c, :], xl_b[:, c, :], True, True, [Wga, (xl_b, c)], [pk])

            def st2(c):
                pb, _ = bank(LRU_PRI[c % len(LRU_PRI)])
                act(ra[:, c, :], pb[:, 0:TB], AF.Tanh, [(pb, 0), (derived2, 2)], [(ra, c)], bias=derived2[:, 16 + c:17 + c], scale=0.5)
                act(iu[:, c, :], pb[:, 512:512 + TB], AF.Tanh, [(pb, 1), (derived2, 3)], [(iu, c)], bias=derived2[:, 24 + c:25 + c], scale=0.5)
                act(Tm[:, c, :], ra[:, c, :], AF.Tanh, [(ra, c), (derived2, 1)], [(Tm, c)], scale=derived2[:, 8 + c:9 + c],
                    bias=derived2[:, 8 + c:9 + c])

            for c in range(10):
                if c < 8:
                    st1a(c)
                if c >= 2:
                    st2(c - 2)
                if 1 <= c < 9:
                    st1b(c - 1)
            for c in range(8):
                act(ra[:, c, :], ra[:, c, :], AF.Exp, [(ra, c), (derived2, 0)], [(ra, c)], scale=derived2[:, c:c + 1], bias=derived2[:, c:c + 1])
                stt(iu[:, c, :], iu[:, c, :], 1.0, xl_f[:, c, :], ALU.add, ALU.mult, [(iu, c), (xl_f, c)], [(iu, c)])
            for c in range(8):
                tt("dve", xl_f[:, c, :], ra[:, c, :], ra[:, c, :], ALU.mult, [(ra, c)], [(xl_f, c)])
                stt(Tm[:, c, :], xl_f[:, c, :], 1.0, Tm[:, c, :], ALU.add, ALU.mult, [(xl_f, c), (Tm, c)], [(Tm, c)])
            for c in range(8):
                act(Tm[:, c, :], Tm[:, c, :], AF.Ln, [(Tm, c)], [(Tm, c)])
                act(Tm[:, c, :], Tm[:, c, :], AF.Exp, [(Tm, c)], [(Tm, c)], scale=0.5)
                if b == 0:
                    memset("pool", Tm[:, c, 0:1], 1.0, [(Tm, c)])
            for c in range(8):
                tt("dve", iu[:, c, :], iu[:, c, :], Tm[:, c, :], ALU.mult, [(iu, c), (Tm, c)], [(iu, c)])
                init = 0.0 if b == 0 else hcarry[:, c:c + 1]
                S.op("dve", lambda e, c=c, init=init: e.tensor_tensor_scan(out=xl_f[:, c, :], data0=ra[:, c, :], data1=iu[:, c, :],
                                                                           initial=init, op0=ALU.mult, op1=ALU.add),
                     reads=[(ra, c), (iu, c), (hcarry, c)], writes=[(xl_f, c)])
                cp("pool", hcarry[:, c:c + 1], xl_f[:, c, TB - 1:TB], [(xl_f, c)], [(hcarry, c)])
            for c in range(8):
                bg_ = LRU_PG[c % len(LRU_PG)]
                pb, pc = bank(bg_)
                pk = (pb, bg_ % 2)
                for k in range(8):
                    mm(pb[:, pc:pc + TB], Wlg[:, k, c * 128:(c + 1) * 128], xs_b[:, k, :], k == 0, k == 7, [xs_b, Wlg], [pk])
                act(ra[:, c, :], pb[:, pc:pc + TB], AF.Gelu_apprx_tanh, [pk], [(ra, c)])
                stt(y_lruT[:, c, b * TB:(b + 1) * TB], ra[:, c, :], 0.5, xl_f[:, c, :], ALU.mult, ALU.mult, [(ra, c), (xl_f, c)], [(y_lruT, b)])

        act_chain[1] = False
        if "cdiag" in dbg:
            dt_ = dbg_tensor("cdiag", [128, 4096])
            stg = S.sb("dbgstage3", [128, 4096], F32, 54 * KB)
            cp("act", stg[:], cdiag[:].rearrange("p g j -> p (g j)"), [cdiag], [stg])
            dma("sp", dt_, stg[:], [stg], [], "dbg"); final_tokens.append("dbg")
        if "y_lru" in dbg:
            dt_ = dbg_tensor("y_lru", [D, S_LEN])
            stage = S.sb("dbgstage2", [128, 8, S_LEN], F32, 4 * KB)
            cp("dve", stage[:], y_lruT[:], [y_lruT], [stage])
            dma("sp", dt_.rearrange("(e p) t -> p e t", p=128), stage[:], [stage], [], "dbg"); final_tokens.append("dbg")

    def ln_tile(v_ap, vkey, out_ap, okey, grow, brow, wk):
        stats, mv, xh = wk["stats"], wk["mv"], wk["xh"]
        for j in range(2):
            S.op("dve", lambda e, j=j: e.bn_stats(out=stats[:, j * 6:(j + 1) * 6], in_=v_ap[:, j * 512:(j + 1) * 512]),
                 reads=[vkey], writes=[(stats, j)])
        S.op("dve", lambda e: e.bn_aggr(out=mv[:, 0:2], in_=stats[:, 0:12]), reads=[stats], writes=[(mv, 0)])
        ts("pool", mv[:, 2:3], mv[:, 1:2], LN_EPS, 1.0, ALU.add, ALU.mult, [(mv, 0)], [(mv, 1)])
        tt("pool", mv[:, 2:3], mv[:, 2:3], mhalf[:, 0:1], ALU.pow, [(mv, 1), mhalf], [(mv, 1)])
        ts("dve", mv[:, 3:4], mv[:, 0:1], mv[:, 2:3], -1.0, ALU.mult, ALU.mult, [(mv, 0), (mv, 1)], [(mv, 2)])
        act(xh[:], v_ap, AF.Identity, [vkey, (mv, 1), (mv, 2)], [xh], bias=mv[:, 3:4], scale=mv[:, 2:3])
        tt("dve", xh[:], xh[:], grow[:], ALU.mult, [xh, grow], [xh])
        tt("dve", out_ap, xh[:], brow[:], ALU.add, [xh, brow], [okey])

    def ln_work(o):
        wk = {}
        wk["stats"] = S.sb("ln_stats", [128, 12], F32, o); o += 64
        wk["mv"] = S.sb("ln_mv", [128, 8], F32, o); o += 32
        wk["xh"] = S.sb("ln_xh", [128, D], F32, o); o += 4 * KB
        return wk, o

    PHASE_MARKS.append(("o", len(S.nodes)))
    if "stop_ssd" not in dbg and "stop_lru" not in dbg:
        Wo_a = S.sb("w_out_a", [128, 8, D], BF16, 4 * KB)
        Wo_b = S.sb("w_out_b", [128, 8, D], BF16, 166 * KB)

        def Wo_k(k):
            return (Wo_a, k) if k < 8 else (Wo_b, k - 8)
        x1h = [S.sb("x1a", [128, 8, D], F32, 54 * KB), S.sb("x1b", [128, 8, D], F32, 20 * KB)]
        o = 182 * KB
        xtk = [S.sb("xtk%d" % i, [128, D], F32, o + i * 4 * KB) for i in range(2)]
        o += 8 * KB
        g1 = S.sb("g1", [128, D], F32, o); o += 4 * KB
        b1 = S.sb("b1", [128, D], F32, o); o += 4 * KB
        vtmp = [S.sb("vtmp%d" % i, [128, D], F32, o + i * 4 * KB) for i in range(1)]
        o += 4 * KB
        wkO, o = ln_work(o)
        assert o <= 207 * KB, o
        castO = S.sb("castO", [128, D], BF16, 52 * KB)
        v = w_out_d.rearrange("(k p) n -> p k n", p=128)
        for q in range(4):
            wt_, k0_ = Wo_k(q * 4)
            dma("pool", wt_[:, k0_:k0_ + 4, :], v[:, q * 4:(q + 1) * 4, :], [], [(wt_, q % 2)], "w_out%d" % q)
        dma("sp", g1[:], rowpar_d[2], [], [g1], "c_g1")
        dma("sp", b1[:], rowpar_d[3], [], [b1], "c_b1")
        for j in range(8):
            ts("dve", Wo_b[:, j, :], Wo_b[:, j, :], col(146 + j), None, ALU.mult, None, [(Wo_b, j // 4), colpar], [(Wo_b, j // 4)])

        def x1_tile(tt_):
            return x1h[tt_ // 8][:, tt_ % 8, :], (x1h[tt_ // 8], tt_ % 8)

        def load_xtk(tt_):
            dma("sp", xtk[tt_ % 2][:], xtok_d[tt_ * 128:(tt_ + 1) * 128, :], [], [xtk[tt_ % 2]], "xtk%d" % (tt_ % 2))

        load_xtk(0)
        for tt_ in range(16):
            if tt_ + 1 < 16:
                load_xtk(tt_ + 1)
            pm = PS[tt_ % 3]
            for hf in range(2):
                for k in range(16):
                    src = y_lruT if k < 8 else y_ssdT
                    skey = (y_lruT, tt_ // 4) if k < 8 else (y_ssdT, tt_)
                    wt_, kk_ = Wo_k(k)
                    mm(pm[:, hf * 512:(hf + 1) * 512], src[:, k % 8, tt_ * 128:(tt_ + 1) * 128], wt_[:, kk_, hf * 512:(hf + 1) * 512],
                       k == 0, k == 15, [skey, (wt_, kk_ // 4)], [(pm, hf)])
            vt = vtmp[0]
            stt(vt[:], xtk[tt_ % 2][:], ALPHA, pm[:], ALU.mult, ALU.add, [xtk[tt_ % 2], pm], [vt])
            oap, okey = x1_tile(tt_)
            ln_tile(vt[:], vt, oap, okey, g1, b1, wkO)
            cp("act", castO[:], oap, [okey], [castO])
            pb, pc = bank(6 + tt_ % 2)
            pk = (pb, tt_ % 2)
            pT_ = pb[:, pc:pc + 512].bitcast(BF16)
            for e_ in range(8):
                tr(pT_[:, e_ * 128:(e_ + 1) * 128], castO[:, e_ * 128:(e_ + 1) * 128], ident_bf[:], [castO, ident_bf], [pk])
            cp("act", y_ssdT[:, :, tt_ * 128:(tt_ + 1) * 128], pT_.rearrange("p (e l) -> p e l", e=8), [pk], [(y_ssdT, tt_)])

        if "x1" in dbg:
            dt_ = dbg_tensor("x1", [S_LEN, D])
            for hh in range(2):
                dma("sp", dt_[hh * 1024:(hh + 1) * 1024, :].rearrange("(t p) d -> p t d", p=128), x1h[hh][:], [x1h[hh]], [], "dbg")
                final_tokens.append("dbg")

    def to_feature_major(xh2, dstT, castbuf, tiles, nbanks=4):
        for tt_ in tiles:
            cb = castbuf[tt_ % 2]
            cp("act" if tt_ % 2 == 0 else "dve", cb[:], xh2[tt_ // 8][:, tt_ % 8, :], [(xh2[tt_ // 8], tt_ % 8)], [cb])
            pb, pc = bank(tt_ % nbanks)
            pk = (pb, (tt_ % nbanks) % 2)
            pT_ = pb[:, pc:pc + 512].bitcast(BF16)
            for e_ in range(8):
                tr(pT_[:, e_ * 128:(e_ + 1) * 128], cb[:, e_ * 128:(e_ + 1) * 128], ident_bf[:], [cb, ident_bf], [pk])
            cp("dve" if tt_ % 2 == 0 else "act", dstT[:, :, tt_ * 128:(tt_ + 1) * 128], pT_.rearrange("p (e l) -> p e l", e=8), [pk], [(dstT, tt_)])

    PHASE_MARKS.append(("f", len(S.nodes)))
    if not ({"stop_ssd", "stop_lru", "stop_o"} & dbg):
        x1T = y_ssdT
        hidT = [S.sb("hidT%d" % i, [128, 4, S_LEN], BF16, 118 * KB + i * 16 * KB) for i in range(2)]
        Wc = []
        for i, base_ in ((2, 150 * KB), (0, 4 * KB), (1, 166 * KB)):
            Wc.append((S.sb("W1c%d" % i, [128, 8, 512], BF16, base_),
                       S.sb("W2c%d" % i, [128, 4, D], BF16, base_ + 8 * KB)))
        o = 182 * KB
        rtmp = [S.sb("rtmp%d" % i, [128, 512], BF16, o + i * KB) for i in range(2)]
        o += 2 * KB
        assert o <= 187 * KB, o
        NFC = 8
        w1v = w_ff1_d.rearrange("(k p) n -> p k n", p=128)
        w2v = w_ff2_d.rearrange("(k p) n -> p k n", p=128)

        def load_wc(fc):
            w1, w2 = Wc[fc % 3]
            dma("pool", w1[:], w1v[:, :, fc * 512:(fc + 1) * 512], [], [w1], "w1c%d" % (fc % 3))
            dma("pool", w2[:], w2v[:, fc * 4:(fc + 1) * 4, :], [], [w2], "w2c%d" % (fc % 3))

        load_wc(0)
        load_wc(1)
        load_wc(2)

        def ff1(fc, tq):
            w1, _ = Wc[fc % 3]
            hd = hidT[fc % 2]
            for f in range(4):
                n = tq * 4 + f
                pb, pc = bank(n % 4)
                pk = (pb, (n % 4) % 2)
                for k in range(8):
                    mm(pb[:, pc:pc + 512], w1[:, k, f * 128:(f + 1) * 128], x1T[:, k, tq * 512:(tq + 1) * 512], k == 0, k == 7,
                       [w1] + [(x1T, tq * 4 + q) for q in range(4)], [pk])
                rt = rtmp[n % 2]
                act(rt[:], pb[:, pc:pc + 512], AF.Relu, [pk], [rt])
                tt("dve", hd[:, f, tq * 512:(tq + 1) * 512], rt[:], rt[:], ALU.mult, [rt], [(hd, tq)])

        def ff2(fc, tq):
            _, w2 = Wc[fc % 3]
            hd = hidT[fc % 2]
            for t4 in range(4):
                tt_ = tq * 4 + t4
                pa = PS[2 + (tt_ % 2)]
                for hf in range(2):
                    for f in range(4):
                        mm(pa[:, hf * 512:(hf + 1) * 512], hd[:, f, tt_ * 128:(tt_ + 1) * 128], w2[:, f, hf * 512:(hf + 1) * 512],
                           f == 0, f == 3, [(hd, tq), w2], [(pa, hf)])
                xa, xk = x1h[tt_ // 8][:, tt_ % 8, :], (x1h[tt_ // 8], tt_ % 8)
                if fc == 0:
                    stt(xa, xa, ALPHA, pa[:], ALU.mult, ALU.add, [xk, pa], [xk])
                else:
                    tt("dve", xa, xa, pa[:], ALU.add, [xk, pa], [xk])

        seq = []
        for fc in range(NFC):
            for tq in range(4):
                seq.append(("1", fc, tq))
        n_steps = len(seq)
        for i in range(n_steps + 2):
            if i < n_steps:
                _, fc, tq = seq[i]
                ff1(fc, tq)
            if i >= 2:
                _, fc, tq = seq[i - 2]
                ff2(fc, tq)
                if tq == 3 and fc + 3 < NFC:
                    load_wc(fc + 3)

        if "x2" in dbg:
            dt_ = dbg_tensor("x2", [S_LEN, D])
            for hh in range(2):
                dma("sp", dt_[hh * 1024:(hh + 1) * 1024, :].rearrange("(t p) d -> p t d", p=128), x1h[hh][:], [x1h[hh]], [], "dbg")
                final_tokens.append("dbg")

    def ln_group(tiles, grow, brow, gw, affine=True):
        stats, mv, xhs = gw["stats"], gw["mv"], gw["xh"]
        for j, tt_ in enumerate(tiles):
            xa, xk = x1h[tt_ // 8][:, tt_ % 8, :], (x1h[tt_ // 8], tt_ % 8)
            for q in range(2):
                S.op("dve", lambda e, j=j, q=q, xa=xa: e.bn_stats(out=stats[:, j, q * 6:(q + 1) * 6], in_=xa[:, q * 512:(q + 1) * 512]),
                     reads=[xk], writes=[(stats, (j, q))], cost=0.65)
            S.op("dve", lambda e, j=j: e.bn_aggr(out=mv[:, j, 0:2], in_=stats[:, j, :]), reads=[(stats, (j, 0)), (stats, (j, 1))],
                 writes=[(mv, (j, "a"))], cost=0.25)
            ts("pool", mv[:, j, 2:3], mv[:, j, 1:2], LN_EPS, 1.0, ALU.add, ALU.mult, [(mv, (j, "a"))], [(mv, (j, "b"))])
            tt("pool", mv[:, j, 2:3], mv[:, j, 2:3], mhalf[:, 0:1], ALU.pow, [(mv, (j, "b")), mhalf], [(mv, (j, "b"))])
            ts("dve", mv[:, j, 3:4], mv[:, j, 0:1], mv[:, j, 2:3], -1.0, ALU.mult, ALU.mult, [(mv, (j, "a")), (mv, (j, "b"))], [(mv, (j, "c"))])
            act(xhs[j][:], xa, AF.Identity, [xk, (mv, (j, "b")), (mv, (j, "c"))], [xhs[j]], bias=mv[:, j, 3:4], scale=mv[:, j, 2:3])
            if affine:
                tt("dve", xhs[j][:], xhs[j][:], grow[:], ALU.mult, [xhs[j], grow], [xhs[j]])
                tt("dve", xa, xhs[j][:], brow[:], ALU.add, [xhs[j], brow], [xk])

    def ln_group_work(o):
        gw = {}
        gw["stats"] = S.sb("lg_stats", [128, 4, 12], F32, o); o += 192
        gw["mv"] = S.sb("lg_mv", [128, 4, 8], F32, o); o += 128
        gw["xh"] = [S.sb("lg_xh%d" % j, [128, D], F32, o + j * 4 * KB) for j in range(4)]
        o += 16 * KB
        return gw, o

    PHASE_MARKS.append(("g", len(S.nodes)))
    if not ({"stop_ssd", "stop_lru", "stop_o", "stop_f"} & dbg):
        x2T = y_ssdT
        Wg = S.sb("Wg", [128, 8, D], BF16, 187 * KB)
        Wp = S.sb("Wp", [128, 2, D], BF16, 203 * KB)
        pTs = S.sb("pTs", [128, 2, S_LEN], BF16, 118 * KB)
        o = 126 * KB
        castg = [S.sb("castg%d" % i, [128, D], BF16, o + i * 2 * KB) for i in range(4)]
        o += 8 * KB
        gsb = [S.sb("gsb%d" % i, [128, D], F32, o + i * 4 * KB) for i in range(4)]
        o += 16 * KB
        assert o <= 150 * KB
        otb = [S.sb("otb%d" % i, [128, 512], F32, 12 * KB + i * 2 * KB) for i in range(4)]
        o = 4 * KB
        g2 = S.sb("g2", [128, D], F32, o); o += 4 * KB
        b2 = S.sb("b2", [128, D], F32, o); o += 4 * KB
        g3 = b3 = None
        gw2, o = ln_group_work(150 * KB)
        o = (o + 31) // 32 * 32
        gw3, o = ln_group_work(o)
        assert o <= 187 * KB, o
        dma("pool", Wg[:], w_gate_d.rearrange("(k p) n -> p k n", p=128), [], [Wg], "w_g")
        dma("pool", Wp[:], w_ple_d.rearrange("(k p) n -> p k n", p=128), [], [Wp], "w_p")
        for hh in range(2):
            dma("pool", pTs[:, :, hh * 1024:(hh + 1) * 1024], pT_d.rearrange("(k p) t -> p k t", p=128)[:, :, hh * 1024:(hh + 1) * 1024],
                [], [(pTs, hh)], "pTs%d" % hh)
        for r_, t_, n_ in ((4, g2, "c_g2"), (5, b2, "c_b2")):
            dma("sp", t_[:], rowpar_d[r_], [], [t_], n_)

        def stP(grp):
            tiles = list(range(4 * grp, 4 * grp + 4))
            ln_group(tiles, g2, b2, gw2)
            for j, tt_ in enumerate(tiles):
                cp("act", castg[j][:], x1h[tt_ // 8][:, tt_ % 8, :], [(x1h[tt_ // 8], tt_ % 8)], [castg[j]])
            for j, tt_ in enumerate(tiles):
                pb, pc = bank(tt_ % 2)
                pk = (pb, tt_ % 2)
                pT_ = pb[:, pc:pc + 512].bitcast(BF16)
                for e_ in range(8):
                    tr(pT_[:, e_ * 128:(e_ + 1) * 128], castg[j][:, e_ * 128:(e_ + 1) * 128], ident_bf[:], [castg[j], ident_bf], [pk])
                cp("act", x2T[:, :, tt_ * 128:(tt_ + 1) * 128], pT_.rearrange("p (e l) -> p e l", e=8), [pk], [(x2T, tt_)])

        def stQ(grp):
            tiles = list(range(4 * grp, 4 * grp + 4))
            for j, tt_ in enumerate(tiles):
                pg = PS[1 + tt_ % 2]
                pp = PS[3]
                for hf in range(2):
                    for k in range(8):
                        mm(pg[:, hf * 512:(hf + 1) * 512], x2T[:, k, tt_ * 128:(tt_ + 1) * 128], Wg[:, k, hf * 512:(hf + 1) * 512],
                           k == 0, k == 7, [(x2T, tt_), Wg], [(pg, hf)])
                for hf in range(2):
                    for k in range(2):
                        mm(pp[:, hf * 512:(hf + 1) * 512], pTs[:, k, tt_ * 128:(tt_ + 1) * 128], Wp[:, k, hf * 512:(hf + 1) * 512],
                           k == 0, k == 1, [(pTs, tt_ // 8), Wp], [(pp, hf)])
                gs = gsb[j]
                act(gs[:], pg[:], AF.Sigmoid, [pg], [gs])
                tt("dve", gs[:], gs[:], pp[:], ALU.mult, [gs, pp], [gs])
            for j, tt_ in enumerate(tiles):
                xa, xk = x1h[tt_ // 8][:, tt_ % 8, :], (x1h[tt_ // 8], tt_ % 8)
                stt(xa, xa, ALPHA, gsb[j][:], ALU.mult, ALU.add, [xk, gsb[j]], [xk])
            ln_group(tiles, g3, b3, gw3, affine=False)
            for e_ in range(8):
                pb, pc = bank(e_ % 2)
                pk = (pb, e_ % 2)
                for j in range(4):
                    tr(pb[:, pc + j * 128:pc + (j + 1) * 128], gw3["xh"][j][:, e_ * 128:(e_ + 1) * 128], ident_f[:], [gw3["xh"][j], ident_f], [pk])
                ot = otb[(grp * 8 + e_) % 4]
                act(ot[:], pb[:, pc:pc + 512], AF.Identity, [pk, colpar], [ot], bias=col(162 + e_), scale=col(154 + e_))
                osn = "out%d" % ((grp * 8 + e_) % 4)
                dma("sp", out_d[e_ * 128:(e_ + 1) * 128, grp * 512:(grp + 1) * 512], ot[:], [ot], [], osn)
                final_tokens.append(osn)

        for i in range(5):
            if i < 4:
                stP(i)
            if i >= 1:
                stQ(i - 1)

    S.finalize(reorder=REORDER)
    finals = sorted(set(final_tokens))
    with nc.Block() as block:
        @block.sync
        def _(e):
            S.emit("sp", e, finals)

        @block.tensor
        def _(e):
            S.emit("pe", e)

        @block.scalar
        def _(e):
            S.emit("act", e)

        @block.vector
        def _(e):
            S.emit("dve", e)

        @block.gpsimd
        def _(e):
            S.emit("pool", e)
    return nc, dbg_out


def _host_inputs(inp):
    f = lambda k: np.asarray(inp[k], dtype=np.float32)
    w_in = f("w_in")[0]
    sh = {}
    sh["w_lru"] = np.ascontiguousarray(w_in[:, 0:2048])
    sh["w_z"] = np.ascontiguousarray(w_in[:, 2048:3072])
    sh["w_xbc"] = np.ascontiguousarray(w_in[:, 3072:5120])
    wdt = np.zeros((D, 80), np.float32)
    for r0 in (0, 32, 64):
        wdt[:, r0:r0 + 16] = w_in[:, 5120:5136]
    sh["w_dt"] = wdt
    gbd = np.zeros((16, 128, 128), np.float32)
    for gi, key in enumerate(("lru_gate_a_w", "lru_gate_x_w")):
        w = f(key)[0]
        for c in range(8):
            for j in range(2):
                gbd[gi * 8 + c, j * 64:(j + 1) * 64, j * 64:(j + 1) * 64] = w[2 * c + j]
    sh["gate_bd"] = gbd
    lcw_ = f("lru_conv_w")[0]
    cdg = np.zeros((128, 32, 128), np.float32)
    ii_ = np.arange(128)
    for c in range(8):
        for k in range(4):
            cdg[ii_, c * 4 + k, ii_] = lcw_[k, c * 128:(c + 1) * 128]
    sh["lru_cdiag"] = cdg.reshape(128, 32 * 128)
    sh["w_out"] = f("w_out")[0]
    sh["w_ff1"] = f("w_ff1")[0]
    sh["w_ff2"] = f("w_ff2")[0]
    sh["w_gate"] = f("w_ple_gate")[0]
    sh["w_ple"] = f("w_ple")[0]
    cpar = np.zeros((128, 176), np.float32)
    tile_cols = lambda v, n: v.reshape(n, 128).T
    lcw = f("lru_conv_w")[0]
    for k in range(4):
        cpar[:, 0 + np.arange(8) * 4 + k] = tile_cols(lcw[k], 8)
    cpar[:, 32:40] = tile_cols(f("lru_conv_b")[0], 8)
    cpar[:, 40:48] = tile_cols(f("lru_gate_a_b")[0].reshape(-1), 8)
    cpar[:, 48:56] = tile_cols(f("lru_gate_x_b")[0].reshape(-1), 8)
    cpar[:, 56:64] = tile_cols(f("lru_a_param")[0], 8)
    scw = f("ssd_conv_w")[0]
    for k in range(4):
        cpar[:, 64 + np.arange(16) * 4 + k] = tile_cols(scw[k], 16)
    cpar[:, 128:144] = tile_cols(f("ssd_conv_b")[0], 16)
    for r0 in (0, 32, 64):
        cpar[r0:r0 + 16, 144] = f("ssd_dt_bias")[0]
        cpar[r0:r0 + 16, 145] = f("ssd_a_log")[0]
    cpar[:, 146:154] = tile_cols(f("ssd_norm_w")[0], 8)
    cpar[:, 154:162] = tile_cols(f("ln3_g")[0], 8)
    cpar[:, 162:170] = tile_cols(f("ln3_b")[0], 8)
    sh["colpar"] = cpar
    rows = np.stack([f("ssd_norm_w")[0], np.repeat(f("ssd_d")[0], 64), f("ln1_g")[0], f("ln1_b")[0],
                     f("ln2_g")[0], f("ln2_b")[0], f("ln3_g")[0], f("ln3_b")[0]], 0)
    sh["rowpar"] = np.ascontiguousarray(np.broadcast_to(rows[:, None, :], (8, 128, D)))
    sh["ident_bf"] = np.eye(128, dtype=np.float32).astype(ml_dtypes.bfloat16)
    sh["ident_f"] = np.eye(128, dtype=np.float32)
    s_i = np.arange(128)[:, None]
    l_i = np.arange(128)[None, :]
    neg = np.where(l_i >= s_i, 0.0, NEG).astype(np.float32)
    sh["negmask"] = np.tile(neg, (1, 4)).astype(ml_dtypes.bfloat16)
    lconst = np.zeros((48, 16, 128), np.float32)
    for h in range(16):
        lconst[32 + h, h, :] = 1.0
    sh["lconst"] = lconst.reshape(48, 16 * 128)
    rconst = np.zeros((48, 512), np.float32)
    rconst[0:16, :] = 1.0
    sh["rconst"] = rconst
    sh["nege"] = -np.eye(16, dtype=np.float32)
    x = f("x")
    p = f("p")[0]
    per_core = []
    for b in range(8):
        m = dict(sh)
        m["xT"] = np.ascontiguousarray(x[b].T)
        m["xtok"] = np.ascontiguousarray(x[b])
        m["pT"] = np.ascontiguousarray(p[b].T)
        per_core.append(m)
    return per_core


def kernel(**inputs):
    nc, _ = build_program()
    in_maps = _host_inputs(inputs)
    res = run_bass_kernel_spmd(nc, in_maps, core_ids=list(range(8)))
    out = np.stack([np.ascontiguousarray(np.asarray(r["out"], dtype=np.float32).T) for r in res.results], 0)
    return out
```
